# Optimizing a Trainium2 kernel written in Bass

```python
import functools
import jax, jax.numpy as jnp
from jax import lax
import numpy as np

D_MODEL = 1024
BATCH = 4
SEQ = 8192
DEPTH = 1
DEC_BATCH = 128
DEC_SEQ = 1
PAST_LEN = 16384
PAGE_SIZE = 128

N_HEADS = 8
D_NOPE = 64
D_ROPE = 32
D_V = 64
Q_RANK = 384
KV_RANK = 256
ROPE_THETA = 10000.0
SM_SCALE = (D_NOPE + D_ROPE) ** -0.5
Q_BLOCK = 128
D_POOL = 512
POOL_WINDOWS = (2, 4, 8, 16)
N_POOL_GROUPS = len(POOL_WINDOWS)
POOL_GROUP = D_POOL // N_POOL_GROUPS
POOL_OUT_GROUP = D_MODEL // N_POOL_GROUPS
POOL_STATE = max(POOL_WINDOWS) - 1
D_FF = ((8 * D_MODEL // 3 + 255) // 256) * 256
IN_COLS = Q_RANK + KV_RANK + D_ROPE + D_POOL + 2 * D_MODEL
EPS = 1e-6

kernel_name = "mla_pool_parallel_gated_adaln_step"


def _rmsnorm(x, g):
    x32 = x.astype(jnp.float32)
    y = x32 * lax.rsqrt(jnp.mean(x32 * x32, axis=-1, keepdims=True) + EPS)
    return y.astype(x.dtype) * g


def _rope_cos_sin(pos):
    inv = 1.0 / (ROPE_THETA ** (jnp.arange(0, D_ROPE, 2, dtype=jnp.float32) / D_ROPE))
    ang = pos.astype(jnp.float32)[:, None] * inv[None, :]
    return jnp.cos(ang), jnp.sin(ang)


def _rope(x, cos, sin):
    x32 = x.astype(jnp.float32)
    x1, x2 = jnp.split(x32, 2, axis=-1)
    return jnp.concatenate([x1 * cos - x2 * sin, x2 * cos + x1 * sin], axis=-1).astype(x.dtype)


def _prompt_mla(q_nope, q_pe, c_kv, k_pe, w_uk, g_k_nope, w_uv):
    b, s = q_nope.shape[:2]
    nb = s // Q_BLOCK
    k_nope = _rmsnorm(jnp.einsum("bsr,rhd->bshd", c_kv, w_uk), g_k_nope)
    v = jnp.einsum("bsr,rhd->bshd", c_kv, w_uv)
    k_pos = jnp.arange(s)

    def block(args):
        qn, qp, start = args
        sc = (jnp.einsum("bqhd,bkhd->bhqk", qn, k_nope, preferred_element_type=jnp.float32)
              + jnp.einsum("bqhd,bkd->bhqk", qp, k_pe, preferred_element_type=jnp.float32)) * SM_SCALE
        q_pos = start + jnp.arange(Q_BLOCK)
        sc = jnp.where(k_pos[None, :] <= q_pos[:, None], sc, -jnp.inf)
        p = jax.nn.softmax(sc, axis=-1).astype(v.dtype)
        return jnp.einsum("bhqk,bkhd->bqhd", p, v)

    def to_blocks(t):
        return jnp.moveaxis(t.reshape(b, nb, Q_BLOCK, *t.shape[2:]), 1, 0)

    o = lax.map(block, (to_blocks(q_nope), to_blocks(q_pe), jnp.arange(nb) * Q_BLOCK))
    return jnp.moveaxis(o, 0, 1).reshape(b, s, N_HEADS, D_V)


def _sample_mla(q_nope, q_pe, c_kv, k_pe, cache_kv_latent, cache_k_rope, page_table, layer, w_uk, g_k_nope, w_uv):
    t = q_nope.shape[1]

    def one(args):
        pt, qn, qp, cn, kn = args
        lat = jnp.concatenate([cache_kv_latent[layer, pt].reshape(-1, KV_RANK), cn], axis=0)
        kpe = jnp.concatenate([cache_k_rope[layer, pt].reshape(-1, D_ROPE), kn], axis=0)
        past = lat.shape[0] - t
        k_nope = _rmsnorm(jnp.einsum("lr,rhd->lhd", lat, w_uk), g_k_nope)
        sc = (jnp.einsum("thd,lhd->htl", qn, k_nope, preferred_element_type=jnp.float32)
              + jnp.einsum("thd,ld->htl", qp, kpe, preferred_element_type=jnp.float32)) * SM_SCALE
        mask = jnp.arange(past + t)[None, :] <= past + jnp.arange(t)[:, None]
        p = jax.nn.softmax(jnp.where(mask, sc, -jnp.inf), axis=-1).astype(lat.dtype)
        o_lat = jnp.einsum("htl,lr->thr", p, lat)
        return jnp.einsum("thr,rhd->thd", o_lat, w_uv)

    return lax.map(one, (page_table, q_nope, q_pe, c_kv, k_pe))


def _multi_scale_pool(u_ext, pos0, n_out):
    b, l, _ = u_ext.shape
    u32 = u_ext.astype(jnp.float32)
    csz = jnp.concatenate([jnp.zeros((b, 1, D_POOL), jnp.float32), jnp.cumsum(u32, axis=1)], axis=1)
    i = jnp.arange(l - n_out, l)
    abs_pos = pos0 + i
    outs = []
    for g, w in enumerate(POOL_WINDOWS):
        cs = csz[:, :, g * POOL_GROUP:(g + 1) * POOL_GROUP]
        lo = jnp.maximum(i + 1 - w, 0)
        cnt = jnp.minimum(abs_pos + 1, w).astype(jnp.float32)
        outs.append((cs[:, i + 1] - cs[:, lo]) / cnt[None, :, None])
    mean = jnp.concatenate(outs, axis=-1)
    return (mean - u32[:, l - n_out:]).astype(u_ext.dtype)


def _layer(x, c, pos, attend, pool_prev, pool_pos0, w_ada, b_ada, g_norm1, w_in, g_q_lat, w_uq, g_kv_lat,
           g_q_nope, g_q_rope, g_k_rope, w_attn_o, w_pool, s_pool, w_out, g_norm2, w_gu, w_down):
    b, s, _ = x.shape
    mod = jax.nn.silu(c) @ w_ada + b_ada
    sh1, sc1, gt1, sh2, sc2, gt2 = jnp.split(mod[:, None, :], 6, axis=-1)
    h = _rmsnorm(x, g_norm1) * (1 + sc1) + sh1
    proj = h @ w_in
    idx = np.cumsum([Q_RANK, KV_RANK, D_ROPE, D_POOL, D_MODEL]).tolist()
    q_lat, c_kv, k_pe, u, gate_a, gate_b = jnp.split(proj, idx, axis=-1)
    q = jnp.einsum("bsr,rhd->bshd", _rmsnorm(q_lat, g_q_lat), w_uq)
    cos, sin = _rope_cos_sin(pos)
    q_nope = _rmsnorm(q[..., :D_NOPE], g_q_nope)
    q_pe = _rope(_rmsnorm(q[..., D_NOPE:], g_q_rope), cos[:, None, :], sin[:, None, :])
    c_kv = _rmsnorm(c_kv, g_kv_lat)
    k_pe = _rope(_rmsnorm(k_pe, g_k_rope), cos, sin)
    o = attend(q_nope, q_pe, c_kv, k_pe)
    a = jnp.einsum("bshd,hde->bse", o, w_attn_o)
    u_ext = jnp.concatenate([pool_prev, u], axis=1)
    d = _multi_scale_pool(u_ext, pool_pos0, s)
    bb = jnp.einsum("bsgc,gce->bsge", d.reshape(b, s, N_POOL_GROUPS, POOL_GROUP), w_pool).reshape(b, s, D_MODEL) * s_pool
    m = jax.nn.sigmoid(gate_a) * a + jax.nn.sigmoid(gate_b) * bb
    x = x + gt1 * (m @ w_out)
    h2 = _rmsnorm(x, g_norm2) * (1 + sc2) + sh2
    gg, up = jnp.split(h2 @ w_gu, 2, axis=-1)
    x = x + gt2 * ((jax.nn.silu(gg) * up) @ w_down)
    return x, c_kv, k_pe, u_ext[:, -POOL_STATE:]


def setup_inputs(seed: int = 0) -> dict:
    key = jax.random.key(seed)
    ks = iter(jax.random.split(key, 40))
    f32 = jnp.float32

    def nrm(shape, scale):
        return jax.random.normal(next(ks), shape, f32) * scale

    def gain(n):
        return 1.0 + 0.05 * jax.random.normal(next(ks), (DEPTH, n), f32)

    n_pages = PAST_LEN // PAGE_SIZE
    n_phys = (5 * DEC_BATCH * n_pages) // 4
    perm = jax.random.permutation(next(ks), n_phys)[: DEC_BATCH * n_pages]
    page_table = perm.reshape(DEC_BATCH, n_pages).astype(jnp.int32)
    return {
        "x_prompt": nrm((BATCH, SEQ, D_MODEL), 1.0),
        "x_sample": nrm((DEC_BATCH, DEC_SEQ, D_MODEL), 1.0),
        "cache_kv_latent": nrm((DEPTH, n_phys, PAGE_SIZE, KV_RANK), 1.0),
        "cache_k_rope": nrm((DEPTH, n_phys, PAGE_SIZE, D_ROPE), 1.0),
        "state_pool": nrm((DEPTH, DEC_BATCH, POOL_STATE, D_POOL), 1.0),
        "page_table": page_table,
        "c_prompt": nrm((BATCH, D_MODEL), 1.0),
        "c_sample": nrm((DEC_BATCH, D_MODEL), 1.0),
        "w_ada": nrm((DEPTH, D_MODEL, 6 * D_MODEL), D_MODEL ** -0.5),
        "b_ada": nrm((DEPTH, 6 * D_MODEL), 0.02),
        "g_norm1": gain(D_MODEL),
        "w_in": nrm((DEPTH, D_MODEL, IN_COLS), D_MODEL ** -0.5),
        "g_q_lat": gain(Q_RANK),
        "w_uq": nrm((DEPTH, Q_RANK, N_HEADS, D_NOPE + D_ROPE), Q_RANK ** -0.5),
        "g_kv_lat": gain(KV_RANK),
        "g_q_nope": gain(D_NOPE),
        "g_q_rope": gain(D_ROPE),
        "g_k_nope": gain(D_NOPE),
        "g_k_rope": gain(D_ROPE),
        "w_uk": nrm((DEPTH, KV_RANK, N_HEADS, D_NOPE), KV_RANK ** -0.5),
        "w_uv": nrm((DEPTH, KV_RANK, N_HEADS, D_V), KV_RANK ** -0.5),
        "w_attn_o": nrm((DEPTH, N_HEADS, D_V, D_MODEL), (N_HEADS * D_V) ** -0.5),
        "w_pool": nrm((DEPTH, N_POOL_GROUPS, POOL_GROUP, POOL_OUT_GROUP), POOL_GROUP ** -0.5),
        "s_pool": gain(D_MODEL),
        "w_out": nrm((DEPTH, D_MODEL, D_MODEL), D_MODEL ** -0.5),
        "g_norm2": gain(D_MODEL),
        "w_gu": nrm((DEPTH, D_MODEL, 2 * D_FF), D_MODEL ** -0.5),
        "w_down": nrm((DEPTH, D_FF, D_MODEL), D_FF ** -0.5),
    }


def reference(x_prompt, x_sample, cache_kv_latent, cache_k_rope, state_pool, page_table, c_prompt, c_sample,
              w_ada, b_ada, g_norm1, w_in, g_q_lat, w_uq, g_kv_lat, g_q_nope, g_q_rope, g_k_nope, g_k_rope,
              w_uk, w_uv, w_attn_o, w_pool, s_pool, w_out, g_norm2, w_gu, w_down):
    pos_p = jnp.arange(x_prompt.shape[1])
    pos_s = PAST_LEN + jnp.arange(x_sample.shape[1])
    yp, ys = x_prompt, x_sample
    lat_p, kr_p, pool_p, lat_s, kr_s, pool_s = [], [], [], [], [], []
    for l in range(DEPTH):
        shared = (w_ada[l], b_ada[l], g_norm1[l], w_in[l], g_q_lat[l], w_uq[l], g_kv_lat[l], g_q_nope[l],
                  g_q_rope[l], g_k_rope[l], w_attn_o[l], w_pool[l], s_pool[l], w_out[l], g_norm2[l],
                  w_gu[l], w_down[l])
        attend_p = functools.partial(_prompt_mla, w_uk=w_uk[l], g_k_nope=g_k_nope[l], w_uv=w_uv[l])
        attend_s = functools.partial(_sample_mla, cache_kv_latent=cache_kv_latent, cache_k_rope=cache_k_rope,
                                     page_table=page_table, layer=l, w_uk=w_uk[l], g_k_nope=g_k_nope[l],
                                     w_uv=w_uv[l])
        empty = jnp.zeros((yp.shape[0], 0, D_POOL), yp.dtype)
        yp, lp, kp, pp = _layer(yp, c_prompt, pos_p, attend_p, empty, 0, *shared)
        ys, lsm, ksm, psm = _layer(ys, c_sample, pos_s, attend_s, state_pool[l], PAST_LEN - POOL_STATE, *shared)
        lat_p.append(lp); kr_p.append(kp); pool_p.append(pp)
        lat_s.append(lsm); kr_s.append(ksm); pool_s.append(psm)
    return (yp, ys, jnp.stack(lat_p), jnp.stack(kr_p), jnp.stack(pool_p),
            jnp.stack(lat_s), jnp.stack(kr_s), jnp.stack(pool_s))
```

```python
import contextlib
import numpy as np
import concourse.bass as bass
import concourse.mybir as mybir
from concourse.bass_utils import run_bass_kernel_spmd

F32 = mybir.dt.float32
BF16 = mybir.dt.bfloat16
I32 = mybir.dt.int32
ALU = mybir.AluOpType
AF = mybir.ActivationFunctionType
AX = mybir.AxisListType

D = 1024
T = 512
H = 8
QR = 384
KVR = 256
DR = 32
DP = 512
DFF = 2816
NCH_FF = DFF // 128
PAST = 16384
PAGE = 128
NPAGES = 128
EPS = 1e-6
SM = 96.0 ** -0.5
NEG = -30000.0


class Buf:
    __slots__ = ("name", "w", "rs", "dsem", "dcnt")

    def __init__(self, name):
        self.name = name
        self.w = None
        self.rs = []
        self.dsem = None
        self.dcnt = 0


class Sched:
    def __init__(self, nc, same_engine_sync=True):
        self.nc = nc
        self.engs = {"pe": nc.tensor, "act": nc.scalar, "dve": nc.vector, "pool": nc.gpsimd, "sp": nc.sync}
        self.sems = {}
        self.cnt = {}
        self.known = {k: {} for k in self.engs}
        self.same = same_engine_sync
        self._ctx = []
        for k in ("pe", "act", "dve", "pool"):
            cm = nc.semaphore("sem_" + k)
            self.sems[k] = cm.__enter__()
            self._ctx.append(cm)
            self.cnt[k] = 0
        self.dsems = {}
        self.nsem = 4
        self.nops = 0

    def _dsem(self, buf):
        if buf.dsem is None:
            cm = self.nc.semaphore("dsem_%d" % self.nsem)
            buf.dsem = cm.__enter__()
            self._ctx.append(cm)
            self.nsem += 1
            self.dsems[buf.name] = buf
        return buf.dsem

    def _wait(self, ek, key, sem, val):
        kn = self.known[ek]
        if kn.get(key, 0) >= val:
            return
        kn[key] = val
        self.engs[ek].wait_ge(sem, val)

    def _deps(self, ek, reads, writes):
        deps = {}
        own_raw = 0

        def add(tok, raw):
            nonlocal own_raw
            if tok is None:
                return
            key, sem, val = tok
            if key == ek:
                if raw and val > own_raw:
                    own_raw = val
                return
            if key not in deps or deps[key][1] < val:
                deps[key] = (sem, val)

        for b in reads:
            add(b.w, True)
        for b in writes:
            add(b.w, False)
            for r in b.rs:
                add(r, False)
        for key, (sem, val) in deps.items():
            self._wait(ek, key, sem, val)
        if own_raw and self.same and ek != "pe" and ek in self.sems:
            self._wait(ek, ek, self.sems[ek], own_raw)

    def op(self, ek, fn, reads=(), writes=(), signal=True):
        self._deps(ek, reads, writes)
        ins = fn(self.engs[ek])
        self.nops += 1
        if signal:
            ins.then_inc(self.sems[ek], 1)
            self.cnt[ek] += 1
            val = self.cnt[ek]
        else:
            val = self.cnt[ek] + 1
        tok = (ek, self.sems[ek], val)
        for b in reads:
            b.rs.append(tok)
            if len(b.rs) > 64:
                b.rs = _compact(b.rs)
        for b in writes:
            b.w = tok
            b.rs = []
        return ins

    def dma(self, qk, fn, reads=(), writes=(), tag=None):
        self._deps(qk, reads, writes)
        tb = tag if tag is not None else (writes[0] if writes else reads[0])
        sem = self._dsem(tb)
        ins = fn(self.engs[qk])
        self.nops += 1
        ins.then_inc(sem, 16)
        tb.dcnt += 16
        tok = ("d:" + tb.name, sem, tb.dcnt)
        for b in reads:
            b.rs.append(tok)
            if len(b.rs) > 64:
                b.rs = _compact(b.rs)
        for b in writes:
            b.w = tok
            b.rs = []
        return ins

    def inherit(self, dst, src):
        toks = []
        for b in src:
            if b.w is not None:
                toks.append(b.w)
            toks.extend(b.rs)
        toks = _compact(toks)
        for d in dst:
            d.rs = _compact(d.rs + toks)

    def barrier(self):
        for ek in ("pe", "act", "dve", "pool", "sp"):
            for k in ("pe", "act", "dve", "pool"):
                if k != ek and self.cnt[k] > 0:
                    self._wait(ek, k, self.sems[k], self.cnt[k])
            for name, b in self.dsems.items():
                if b.dcnt > 0:
                    self._wait(ek, "d:" + name, b.dsem, b.dcnt)

    def close(self):
        for cm in reversed(self._ctx):
            cm.__exit__(None, None, None)


def _compact(rs):
    best = {}
    for key, sem, val in rs:
        if key not in best or best[key][2] < val:
            best[key] = (key, sem, val)
    return list(best.values())


def build(NT=8, NS=16, NPHYS=20480, do_sample=True, same_sync=True, smp_stop=99, sub=9):
    nc = bass.Bass("TRN2", target_bir_lowering=False)
    S = Sched(nc, same_engine_sync=same_sync)
    es = contextlib.ExitStack()

    def din(name, shape, dtp=F32):
        return nc.dram_tensor(name, list(shape), dtp, kind="ExternalInput").ap()

    def dout(name, shape, dtp=F32):
        return nc.dram_tensor(name, list(shape), dtp, kind="ExternalOutput").ap()

    sb_sizes = {}

    def sb(name, shape, dtp, stack=None):
        sb_sizes[name] = int(np.prod(shape[1:])) * (4 if dtp in (F32, I32) else 2)
        try:
            return (stack or es).enter_context(nc.sbuf_tensor("s_" + name, list(shape), dtp))
        except AssertionError:
            tot = 0
            for k, v in sb_sizes.items():
                tot += v
                print("SBUF %-10s %7d  cum %7d" % (k, v, tot))
            raise

    x_own = din("x_own", [NT * T, D])
    x_oth = din("x_oth", [NT * T, D])
    cT_d = din("cT", [128, 8, 17])
    w_ada = din("w_ada", [D, 6 * D])
    b_gt_d = din("b_gt", [16, 2, D])
    vecs_d = din("vecs", [128, 80])
    rowv_d = din("rowv", [128, 288])
    cbf_d = din("cbf", [128, 5, 128])
    identf_d = din("identf", [128, 128])
    w_in_kv_d = din("w_in_kv", [D, 288])
    w_in_q_d = din("w_in_q", [D, QR])
    w_in_u_d = din("w_in_u", [D, DP])
    w_in_g_d = din("w_in_g", [D, 2 * D])
    w_uq_d = din("w_uq2", [QR, H, 192])
    w_uk_d = din("w_uk", [KVR, 512])
    w_uv_d = din("w_uv", [KVR, 512])
    w_ao_d = din("w_ao", [512, D])
    w_pool_d = din("w_pool", [4, 128, 256])
    w_out_d = din("w_out", [D, D])
    w_gu_d = din("w_gu", [D, 2 * DFF])
    w_down_d = din("w_down", [DFF, D])
    cs_own_d = din("cs_own", [NT, 128, 4, 64])
    cs_oth_d = din("cs_oth", [NT, 128, 4, 64])
    csq_own_d = din("csq_own", [NT, 128, 2, T])
    pcorr_d = din("pcorr", [128, 4, 16])

    y_own = dout("y_own", [NT * T, D])
    lat_own = dout("lat_own", [NT * T, KVR])
    kr_own = dout("kr_own", [NT * T, DR])
    pool_own = dout("pool_own", [16, DP])

    kt_s = nc.dram_tensor("kt_s", [2 * NT, 96, H, T], BF16, kind="Internal").ap()
    v_s = nc.dram_tensor("v_s", [2 * NT, 128, 4, H, 128], BF16, kind="Internal").ap()
    kt_buf = [Buf("kts%d" % i) for i in range(2 * NT)]
    v_buf = [Buf("vs%d" % i) for i in range(2 * NT)]

    if do_sample:
        x_smp = din("x_smp", [NS, D])
        cache_lat = din("cache_lat", [NPHYS * 16, 2048])
        cache_kr = din("cache_kr", [NPHYS * 2, 2048])
        ptT_d = din("ptT", [128, NS], I32)
        state_d = din("state_pool", [NS, 15, DP])
        w_ukT_d = din("w_ukT", [64, H, KVR])
        cs_smp_d = din("cs_smp", [128, 1, 64])
        csq_smp_d = din("csq_smp", [128, 2, NS])
        ciota_d = din("ciota", [128, 16], I32)
        y_smp = dout("y_smp", [NS, D])
        lat_smp = dout("lat_smp", [NS, KVR])
        kr_smp = dout("kr_smp", [NS, DR])
        pool_smp = dout("pool_smp", [NS, 15, DP])

    cbf = sb("cbf", [128, 5, 128], BF16)
    identb = cbf[:, 0, :]
    blk96 = cbf[:, 1, :]
    blk64 = cbf[:, 2, :]
    tri = cbf[:, 3, :]
    onesb = cbf[:, 4, :]
    identf = sb("identf", [128, 128], F32)
    vecs = sb("vecs", [128, 80], F32)
    rowv = sb("rowv", [128, 288], F32)
    modF = sb("modF", [128, 32, 17], F32)
    gt_bc = sb("gt_bc", [128, 2, D], F32)
    scT = sb("scT", [128, 8, 17], BF16)
    w_kv = sb("w_kv", [128, 8, 288], BF16)
    w_u = sb("w_u", [128, 8, DP], BF16)
    w_uq = sb("w_uq", [128, 3, H, 192], BF16)
    w_uk = sb("w_uk", [128, 2, 512], BF16)
    w_uv = sb("w_uv", [128, 2, 512], BF16)
    w_pool = sb("w_pool", [128, 4, 256], BF16)
    w_ao_sb = sb("w_ao_sb", [128, 4, D], BF16)
    b_const = Buf("const")

    V_B = 0
    V_G1 = 32
    V_G2 = 40
    V_SP = 48
    V_GQL = 56
    V_GQ96 = 59
    V_GQ96P = 60
    V_INV96 = 61
    V_GKN2 = 62
    V_FLAGNEG = 63
    V_FLAG = 64
    V_NFLAG = 65
    V_GQK = 66

    NWB = 3
    WELEMS = 4096
    wbuf = [sb("wbuf%d" % i, [128, WELEMS], BF16) for i in range(NWB)]
    wbufB = [Buf("wbuf%d" % i) for i in range(NWB)]

    NXT = 4
    xn = [sb("xn%d" % i, [128, D], BF16) for i in range(2)]
    xnB = [Buf("xn%d" % i) for i in range(2)]
    junk = sb("junk", [128, D], BF16)
    junkB = Buf("junk")
    st = sb("st", [128, 16], F32)
    stB = Buf("st")
    ckv_f = [sb("ckv_f%d" % i, [128, 288], F32) for i in range(2)]
    ckv_fB = [Buf("ckv_f%d" % i) for i in range(2)]
    ckv_b = [sb("ckv_b%d" % i, [128, 288], BF16) for i in range(2)]
    ckv_bB = [Buf("ckv_b%d" % i) for i in range(2)]
    rtmp = sb("rtmp", [128, 64], F32)
    rtmpB = Buf("rtmp")
    epsc = sb("epsc", [128, 1], F32)
    hTB, actTB, ckvnTB, kpeTB = Buf("hT"), Buf("actT"), Buf("ckvnT"), Buf("kpeT")
    sqbB = [Buf("sqb%d" % i) for i in range(3)]
    tfB = [Buf("tf%d" % i) for i in range(4)]
    lnvB = [tfB[0], tfB[1]]
    qlnTB, QTB, uTB, dTB, oTB, mTB = Buf("qlnT"), Buf("QT"), Buf("uT"), Buf("dT"), Buf("oT"), Buf("mT")
    KTstB = QTB
    rt1B, rt2B, pt1B, pt2B = tfB[2], tfB[3], tfB[2], tfB[3]
    VstB = mTB
    sigB = [tfB[0], tfB[1]]
    gtmpB = [tfB[2], tfB[3]]

    esP = contextlib.ExitStack()

    def alloc_tile_bufs(TT, stack):
        nonlocal hT, actT, ckvnT, kpeT, sqb, tf, lnv, qlnT, QT, rt1, rt2, uT, pt1, pt2, dT, oT, mT, sig, gtmp
        sfx = "_%d" % TT
        hT = sb("hT" + sfx, [128, 8, TT], BF16, stack)
        actT = sb("actT" + sfx, [128, NCH_FF, TT], BF16, stack)
        ckvnT = sb("ckvnT" + sfx, [128, 2, TT], BF16, stack)
        kpeT = sb("kpeT" + sfx, [32, TT], BF16, stack)
        sqb = [sb("sqb%d" % i + sfx, [128, T], BF16, stack) for i in range(3)]
        tf = [sb("tf%d" % i + sfx, [128, 16 + T], F32, stack) for i in range(4)]
        lnv = [tf[0], tf[1]]
        qlnT = sb("qlnT" + sfx, [128, 3, TT], BF16, stack)
        QT = sb("QT" + sfx, [96, H, TT], BF16, stack)
        rt1, rt2, pt1, pt2 = tf[2], tf[3], tf[2], tf[3]
        uT = sb("uT" + sfx, [128, 4, 16 + TT], F32, stack)
        dT = sb("dT" + sfx, [128, 4, TT], BF16, stack)
        oT = sb("oT" + sfx, [128, 4, TT], BF16, stack)
        mT = sb("mT" + sfx, [128, 8, TT], BF16, stack)
        sig = [tf[0], tf[1]]
        gtmp = [tf[2], tf[3]]

    hT = actT = ckvnT = kpeT = sqb = tf = lnv = qlnT = QT = rt1 = rt2 = uT = pt1 = pt2 = dT = oT = mT = sig = gtmp = None
    alloc_tile_bufs(T, esP)
    hTo = actT[:, 0:8, :]
    KTst = QT
    Vst = mT[:].rearrange("p a b -> p (a b)").rearrange("p (k h d) -> p k h d", k=4, h=H)
    KTb = [actT[0:96, 8 + 4 * i:12 + 4 * i, :] for i in range(2)]
    KTbB = [Buf("KTb%d" % i) for i in range(2)]
    NPT = 8
    PT = [actT[:, 16 + i, :] for i in range(6)] + [actT[:, 6, :], actT[:, 7, :]]
    PTB = [Buf("PT%d" % i) for i in range(NPT)]
    rden = actT[:, 4:6, :].rearrange("p a b -> p (a b)").bitcast(F32)
    rdenB = Buf("rden")
    rden2 = [rden, actT[:, 2:4, :].rearrange("p a b -> p (a b)").bitcast(F32)]
    rden2B = [rdenB, Buf("rden_b")]
    attn_group = KTbB + PTB + rden2B

    NPS = 8
    ps = [es.enter_context(nc.psum_tensor("ps%d" % i, [128, 512], F32)) for i in range(NPS)]
    psB = [Buf("ps%d" % i) for i in range(NPS)]
    ps_free = list(range(NPS))

    def ps_alloc():
        return ps_free.pop(0)

    def ps_release(i):
        ps_free.append(i)

    def mm(pi, out_ap, pairs, reads, last_signal=True, start=True):
        n = len(pairs)
        for i, (l, r) in enumerate(pairs):
            S.op("pe", lambda e: e.matmul(out_ap, lhsT=l, rhs=r, start=(start and i == 0), stop=(i == n - 1)),
                 reads=reads, writes=[psB[pi]], signal=(last_signal and i == n - 1))

    pieces = []
    wstate = {"issued": 0, "next": 0}

    def add_piece(parts):
        pieces.append(parts)

    def _issue(n):
        b = n % NWB
        for dst_fn, src in pieces[n]:
            S.dma("pool", lambda e: e.dma_start(out=dst_fn(wbuf[b]), in_=src), writes=[wbufB[b]])

    def wget():
        n = wstate["next"]
        wstate["next"] += 1
        lim = min(len(pieces), n + NWB)
        while wstate["issued"] < lim:
            _issue(wstate["issued"])
            wstate["issued"] += 1
        return wbuf[n % NWB], wbufB[n % NWB]

    def kview(k, n, off=0):
        return lambda wb: wb[:, off:off + k * n].rearrange("p (k n) -> p k n", k=k)

    def kmajor(src, c0, c1):
        return src[:, c0:c1].rearrange("(k p) n -> p k n", p=128)

    tiles = ["own%d" % i for i in range(NT)] + (["smp"] if do_sample else [])
    for j in range(12):
        add_piece([(kview(8, 512), kmajor(w_ada, j * 512, (j + 1) * 512))])
    for tname in tiles:
        if tname == "smp":
            for j in (4, 5, 10, 11):
                add_piece([(kview(8, 512), kmajor(w_ada, j * 512, (j + 1) * 512))])
        add_piece([(kview(8, QR), kmajor(w_in_q_d, 0, QR))])
        for j in range(4):
            add_piece([(kview(8, 512), kmajor(w_in_g_d, j * 512, (j + 1) * 512))])
        for j in range(2):
            add_piece([(kview(8, 512), kmajor(w_out_d, j * 512, (j + 1) * 512))])
        for j in range(11):
            add_piece([(kview(8, 256), kmajor(w_gu_d, j * 256, (j + 1) * 256)),
                       (kview(8, 256, 2048), kmajor(w_gu_d, DFF + j * 256, DFF + (j + 1) * 256))])
        for j in range(4):
            for kh in range(2):
                add_piece([(kview(11, 256), w_down_d[kh * 1408:(kh + 1) * 1408, j * 256:(j + 1) * 256].rearrange("(k p) n -> p k n", p=128))])

    def cload(q, dst, src):
        S.dma(q, lambda e: e.dma_start(out=dst, in_=src), writes=[b_const], tag=b_const)

    cload("pool", cbf[:], cbf_d)
    cload("sp", identf[:], identf_d)
    cload("sp", vecs[:], vecs_d)
    cload("sp", rowv[:], rowv_d)
    cload("pool", w_kv[:], kmajor(w_in_kv_d, 0, 288))
    cload("pool", w_u[:], kmajor(w_in_u_d, 0, DP))
    cload("pool", w_uq[:], w_uq_d.rearrange("(c p) h n -> p c h n", p=128))
    cload("pool", w_uk[:], kmajor(w_uk_d, 0, 512))
    cload("pool", w_uv[:], kmajor(w_uv_d, 0, 512))
    cload("pool", w_pool[:], w_pool_d.rearrange("g p n -> p g n"))
    cload("pool", w_ao_sb[:], w_ao_d.rearrange("(pr q) e -> q pr e", q=128))
    S.op("dve", lambda e: e.memset(uT[:], 0.0), writes=[uTB])

    def vcol(c, rows=128, r0=0):
        return vecs[r0:r0 + rows, c:c + 1]

    with contextlib.ExitStack() as es2:
        cT = es2.enter_context(nc.sbuf_tensor("s_cT", [128, 8, 17], F32))
        b_gt = actT[0:16, 0:8, :].rearrange("p a b -> p (a b)").bitcast(F32).rearrange("p (w d) -> p w d", w=2)
        gtP = actT[0:1, 8:16, :].rearrange("p a b -> p (a b)").bitcast(F32).rearrange("p (w d) -> p w d", w=2)
        onesr = es2.enter_context(nc.sbuf_tensor("s_onesr", [1, 128], F32))
        bset = Buf("setup")
        S.dma("sp", lambda e: e.dma_start(out=cT[:], in_=cT_d), writes=[bset])
        S.dma("sp", lambda e: e.dma_start(out=b_gt, in_=b_gt_d), writes=[bset])
        S.op("dve", lambda e: e.memset(onesr[:], 1.0), writes=[bset])
        S.op("act", lambda e: e.activation(out=scT[:], in_=cT[:], func=AF.Silu), reads=[bset], writes=[bset, b_const])
        FMAP = {0: 0, 1: 8, 3: 16, 4: 24}
        for j in range(12):
            wb, wbB = wget()
            wv = kview(8, 512)(wb)
            grp, hf = j // 2, j % 2
            if grp in FMAP:
                pi = ps_alloc()
                for c in range(4):
                    mm(pi, ps[pi][:, c * 32:c * 32 + 17], [(wv[:, k, c * 128:(c + 1) * 128], scT[:, k, :]) for k in range(8)],
                       [wbB, bset], last_signal=(c == 3))
                for c in range(4):
                    ch = FMAP[grp] + hf * 4 + c
                    S.op("dve", lambda e: e.tensor_scalar(out=modF[:, ch, :], in0=ps[pi][:, c * 32:c * 32 + 17],
                                                          scalar1=vcol(V_B + ch), scalar2=None, op0=ALU.add),
                         reads=[psB[pi], b_const], writes=[b_const])
                ps_release(pi)
            else:
                which = 0 if grp == 2 else 1
                pi = ps_alloc()
                pj = ps_alloc()
                mm(pi, ps[pi][0:1, :], [(scT[:, k, 0:1], wv[:, k, :]) for k in range(8)], [wbB, bset])
                cs_ = slice(hf * 512, (hf + 1) * 512)
                S.op("dve", lambda e: e.tensor_tensor(out=gtP[0:1, which, cs_], in0=ps[pi][0:1, :], in1=b_gt[0:1, which, cs_], op=ALU.add),
                     reads=[psB[pi], bset], writes=[bset])

                mm(pi, ps[pi][:, :], [(onesr[0:1, :], gtP[0:1, which, cs_])], [bset])
                S.op("act", lambda e: e.copy(out=gt_bc[:, which, cs_], in_=ps[pi][:, :]), reads=[psB[pi]], writes=[b_const])
                ps_release(pi)
                ps_release(pj)
        for k in range(8):
            S.op("dve", lambda e: e.tensor_scalar(out=modF[:, 8 + k, :], in0=modF[:, 8 + k, :], scalar1=1.0, scalar2=vcol(V_G1 + k),
                                                  op0=ALU.add, op1=ALU.mult), reads=[b_const], writes=[b_const])
            S.op("dve", lambda e: e.tensor_scalar(out=modF[:, 24 + k, :], in0=modF[:, 24 + k, :], scalar1=1.0, scalar2=vcol(V_G2 + k),
                                                  op0=ALU.add, op1=ALU.mult), reads=[b_const], writes=[b_const])
        S.barrier()

    def rstd_small(ss_ap, out_ap, inv_n):
        S.op("act", lambda e: e.activation(out=out_ap, in_=ss_ap, func=AF.Ln, bias=vcol_eps(out_ap), scale=inv_n),
             reads=[stB, b_const], writes=[stB])
        S.op("act", lambda e: e.activation(out=out_ap, in_=out_ap, func=AF.Exp, scale=-0.5), reads=[stB], writes=[stB])

    S.op("dve", lambda e: e.memset(epsc[:], EPS), writes=[b_const])

    def vcol_eps(ap):
        n = ap.shape[0]
        return epsc[0:n, :]

    junkB2 = [Buf("junk_a"), Buf("junk_b")]
    rtmp2 = [rtmp, sb("rtmp_b", [128, 64], F32, esP)]
    rtmp2B = [rtmpB, Buf("rtmp_b")]

    def front2(blocks, P, is_smp, hdst, hdstB, csB):
        nb = len(blocks)
        stv = st[0:P, :].rearrange("p (j c) -> p j c", j=2)
        for j, (xt, xtB, c0, cs_ap, o_lat, o_kr) in enumerate(blocks):
            S.op("act", lambda e: e.activation(out=xn[j][0:P, :], in_=xt[0:P, :], func=AF.Square, accum_out=st[0:P, 8 * j:8 * j + 1]),
                 reads=[xtB], writes=[xnB[j], stB])
        S.op("act", lambda e: e.activation(out=stv[:, 0:nb, 1], in_=stv[:, 0:nb, 0], func=AF.Ln, bias=epsc[0:P, :], scale=1.0 / D),
             reads=[stB, b_const], writes=[stB])
        S.op("act", lambda e: e.activation(out=stv[:, 0:nb, 1], in_=stv[:, 0:nb, 1], func=AF.Exp, scale=-0.5), reads=[stB], writes=[stB])
        for j, (xt, xtB, c0, cs_ap, o_lat, o_kr) in enumerate(blocks):
            S.op("act", lambda e: e.activation(out=xn[j][0:P, :], in_=xt[0:P, :], func=AF.Copy, scale=st[0:P, 8 * j + 1:8 * j + 2]),
                 reads=[xtB, stB], writes=[xnB[j]])
        for j, (xt, xtB, c0, cs_ap, o_lat, o_kr) in enumerate(blocks):
            pi = ps_alloc()
            pv = ps[pi][:].bitcast(BF16)
            for k in range(8):
                S.op("pe", lambda e: e.transpose(out=pv[:, k * 128:k * 128 + P], in_=xn[j][0:P, k * 128:(k + 1) * 128], identity=identb[0:P, 0:P]),
                     reads=[xnB[j], b_const], writes=[psB[pi]], signal=(k == 7))
            pv3 = pv.rearrange("p (k n) -> p k n", k=8)[:, :, 0:P]
            if not is_smp:
                a_ap = modF[:, 8:16, 0:1].to_broadcast([128, 8, P])
                s_ap = modF[:, 0:8, 0:1].to_broadcast([128, 8, P])
            else:
                a_ap = modF[:, 8:16, 1:1 + P]
                s_ap = modF[:, 0:8, 1:1 + P]
            hv = hdst[:, :, c0:c0 + P]
            S.op("dve", lambda e: e.tensor_tensor(out=hv, in0=pv3, in1=a_ap, op=ALU.mult), reads=[psB[pi], b_const], writes=[hdstB])
            S.op("dve", lambda e: e.tensor_tensor(out=hv, in0=hv, in1=s_ap, op=ALU.add), reads=[hdstB, b_const], writes=[hdstB])
            ps_release(pi)
        pcs = []
        for j, (xt, xtB, c0, cs_ap, o_lat, o_kr) in enumerate(blocks):
            pi = ps_alloc()
            pcs.append(pi)
            mm(pi, ps[pi][0:P, 0:288], [(hdst[:, k, c0:c0 + P], w_kv[:, k, :]) for k in range(8)], [hdstB, b_const])
            S.op("act", lambda e: e.activation(out=junk[0:P, j * 288:j * 288 + 256], in_=ps[pi][0:P, 0:256], func=AF.Square, accum_out=st[0:P, 8 * j + 2:8 * j + 3]),
                 reads=[psB[pi]], writes=[junkB2[j], stB])
            S.op("act", lambda e: e.activation(out=junk[0:P, j * 288 + 256:j * 288 + 288], in_=ps[pi][0:P, 256:288], func=AF.Square, accum_out=st[0:P, 8 * j + 3:8 * j + 4]),
                 reads=[psB[pi]], writes=[junkB2[j], stB])
        S.op("act", lambda e: e.activation(out=stv[:, 0:nb, 4], in_=stv[:, 0:nb, 2], func=AF.Ln, bias=epsc[0:P, :], scale=1.0 / KVR),
             reads=[stB, b_const], writes=[stB])
        S.op("act", lambda e: e.activation(out=stv[:, 0:nb, 5], in_=stv[:, 0:nb, 3], func=AF.Ln, bias=epsc[0:P, :], scale=1.0 / DR),
             reads=[stB, b_const], writes=[stB])
        S.op("act", lambda e: e.activation(out=stv[:, 0:nb, 4:6], in_=stv[:, 0:nb, 4:6], func=AF.Exp, scale=-0.5), reads=[stB], writes=[stB])
        for j, (xt, xtB, c0, cs_ap, o_lat, o_kr) in enumerate(blocks):
            pi = pcs[j]
            cf, cfB, cb, cbB = ckv_f[j], ckv_fB[j], ckv_b[j], ckv_bB[j]
            rt, rtB = rtmp2[j], rtmp2B[j]
            smp_ci["i"] = j
            S.op("dve", lambda e: e.scalar_tensor_tensor(out=cf[0:P, 0:256], in0=ps[pi][0:P, 0:256], scalar=st[0:P, 8 * j + 4:8 * j + 5], in1=rowv[0:P, 0:256],
                                                         op0=ALU.mult, op1=ALU.mult), reads=[psB[pi], stB, b_const], writes=[cfB])
            S.op("dve", lambda e: e.scalar_tensor_tensor(out=rt[0:P, 0:32], in0=ps[pi][0:P, 256:288], scalar=st[0:P, 8 * j + 5:8 * j + 6], in1=rowv[0:P, 256:288],
                                                         op0=ALU.mult, op1=ALU.mult), reads=[psB[pi], stB, b_const], writes=[rtB])
            ps_release(pi)
            S.op("pool", lambda e: e.tensor_tensor(out=rt[0:P, 32:64], in0=rt[0:P, 0:32], in1=cs_ap[0:P, 0:32], op=ALU.mult),
                 reads=[rtB, csB], writes=[rtB])
            S.op("pool", lambda e: e.tensor_tensor(out=cf[0:P, 256:272], in0=rt[0:P, 16:32], in1=cs_ap[0:P, 32:48], op=ALU.mult),
                 reads=[rtB, csB], writes=[cfB])
            S.op("pool", lambda e: e.tensor_tensor(out=cf[0:P, 272:288], in0=rt[0:P, 0:16], in1=cs_ap[0:P, 48:64], op=ALU.mult),
                 reads=[rtB, csB], writes=[cfB])
            S.op("pool", lambda e: e.tensor_tensor(out=cf[0:P, 256:288], in0=cf[0:P, 256:288], in1=rt[0:P, 32:64], op=ALU.add),
                 reads=[rtB, cfB], writes=[cfB])
            S.op("pool", lambda e: e.tensor_copy(out=cb[0:P, :], in_=cf[0:P, :]), reads=[cfB], writes=[cbB])
            if o_lat is not None:
                S.dma("sp", lambda e: e.dma_start(out=o_lat, in_=cf[0:P, 0:256]), reads=[cfB], tag=cfB)
                S.dma("sp", lambda e: e.dma_start(out=o_kr, in_=cf[0:P, 256:288]), reads=[cfB], tag=cfB)
        for j, (xt, xtB, c0, cs_ap, o_lat, o_kr) in enumerate(blocks):
            cb, cbB = ckv_b[j], ckv_bB[j]
            pi = ps_alloc()
            pv = ps[pi][:].bitcast(BF16)
            for rc in range(2):
                S.op("pe", lambda e: e.transpose(out=pv[:, rc * 128:rc * 128 + P], in_=cb[0:P, rc * 128:(rc + 1) * 128], identity=identb[0:P, 0:P]),
                     reads=[cbB, b_const], writes=[psB[pi]], signal=False)
            S.op("pe", lambda e: e.transpose(out=pv[0:32, 256:256 + P], in_=cb[0:P, 256:288], identity=identb[0:P, 0:P]),
                 reads=[cbB, b_const], writes=[psB[pi]])
            S.op("act", lambda e: e.copy(out=ckvnT[:, :, c0:c0 + P], in_=pv[:, 0:256].rearrange("p (k n) -> p k n", k=2)[:, :, 0:P]),
                 reads=[psB[pi]], writes=[ckvnTB])
            S.op("act", lambda e: e.copy(out=kpeT[0:32, c0:c0 + P], in_=pv[0:32, 256:256 + P]), reads=[psB[pi]], writes=[kpeTB])
            ps_release(pi)

    class _N:
        n = 0
    front = _N
    front.n = 0

    def rstd_big(pi, rows, scale, li):
        S.op("act", lambda e: e.activation(out=lnv[li][0:rows, :], in_=ps[pi][0:rows, :], func=AF.Ln, bias=epsc[0:rows, :], scale=scale),
             reads=[psB[pi], b_const], writes=[lnvB[li]])
        S.op("act", lambda e: e.activation(out=lnv[li][0:rows, :], in_=lnv[li][0:rows, :], func=AF.Exp, scale=-0.5),
             reads=[lnvB[li]], writes=[lnvB[li]])

    def kv_build(slot, n):
        for pr in range(4):
            pk = ps_alloc()
            mm(pk, ps[pk][:, 0:n], [(w_uk[:, rc, pr * 128:(pr + 1) * 128], ckvnT[:, rc, 0:n]) for rc in range(2)], [ckvnTB, b_const])
            si = pr % 3
            S.op("act", lambda e: e.activation(out=sqb[si][:, 0:n], in_=ps[pk][:, 0:n], func=AF.Square), reads=[psB[pk]], writes=[sqbB[si]])
            pss = ps_alloc()
            mm(pss, ps[pss][:, 0:n], [(blk64, sqb[si][:, 0:n])], [sqbB[si], b_const])
            li = pr % 2
            rstd_big_n(pss, 128, 1.0 / 64, li, n)
            ps_release(pss)
            for hh in range(2):
                r0 = hh * 64
                S.op("dve", lambda e: e.scalar_tensor_tensor(out=KTst[0:64, 2 * pr + hh, 0:n], in0=ps[pk][r0:r0 + 64, 0:n], scalar=vcol(V_GKN2, 64, r0),
                                                             in1=lnv[li][r0:r0 + 64, 0:n], op0=ALU.mult, op1=ALU.mult),
                     reads=[psB[pk], lnvB[li], b_const], writes=[KTstB])
            ps_release(pk)
        S.op("pool", lambda e: e.tensor_copy(out=KTst[64:96, :, 0:n], in_=kpeT[0:32, 0:n].unsqueeze(1).to_broadcast([32, H, n])),
             reads=[kpeTB], writes=[KTstB])
        S.dma("sp", lambda e: e.dma_start(out=kt_s[slot], in_=KTst[:]), reads=[KTstB], writes=[kt_buf[slot]], tag=kt_buf[slot])
        S.op("pool", lambda e: e.memset(Vst[:, :, :, 64:128], 1.0), writes=[VstB])
        for ks in range(n // 128):
            pvv = ps_alloc()
            mm(pvv, ps[pvv][:, :], [(ckvnT[:, rc, ks * 128:(ks + 1) * 128], w_uv[:, rc, :]) for rc in range(2)], [ckvnTB, b_const])
            S.op("act", lambda e: e.copy(out=Vst[:, ks, :, 0:64], in_=ps[pvv][:, :].rearrange("p (h d) -> p h d", h=H)),
                 reads=[psB[pvv]], writes=[VstB])
            ps_release(pvv)
        S.dma("sp", lambda e: e.dma_start(out=v_s[slot], in_=Vst), reads=[VstB], writes=[v_buf[slot]], tag=v_buf[slot])

    def q_build(n, csq_ap, csqBuf, hsrc, hsrcB):
        wb, wbB = wget()
        wq = kview(8, QR)(wb)
        pq = [ps_alloc() for _ in range(3)]
        for c in range(3):
            mm(pq[c], ps[pq[c]][:, 0:n], [(wq[:, k, c * 128:(c + 1) * 128], hsrc[:, k, 0:n]) for k in range(8)], [wbB, hsrcB])
            S.op("act", lambda e: e.activation(out=sqb[c][:, 0:n], in_=ps[pq[c]][:, 0:n], func=AF.Square), reads=[psB[pq[c]]], writes=[sqbB[c]])
        pss = ps_alloc()
        mm(pss, ps[pss][:, 0:n], [(onesb, sqb[c][:, 0:n]) for c in range(3)], sqbB + [b_const])
        rstd_big_n(pss, 128, 1.0 / QR, 0, n)
        ps_release(pss)
        for c in range(3):
            S.op("dve", lambda e: e.scalar_tensor_tensor(out=qlnT[:, c, 0:n], in0=ps[pq[c]][:, 0:n], scalar=vcol(V_GQL + c), in1=lnv[0][:, 0:n],
                                                         op0=ALU.mult, op1=ALU.mult), reads=[psB[pq[c]], lnvB[0], b_const], writes=[qlnTB])
            ps_release(pq[c])
        for h in range(H):
            pa = ps_alloc()
            pb = ps_alloc()
            mm(pa, ps[pa][0:96, 0:n], [(w_uq[:, c, h, 0:96], qlnT[:, c, 0:n]) for c in range(3)], [qlnTB, b_const])
            mm(pb, ps[pb][0:96, 0:n], [(w_uq[:, c, h, 96:192], qlnT[:, c, 0:n]) for c in range(3)], [qlnTB, b_const])
            si = h % 3
            S.op("act", lambda e: e.activation(out=sqb[si][0:96, 0:n], in_=ps[pa][0:96, 0:n], func=AF.Square), reads=[psB[pa]], writes=[sqbB[si]])
            pss = ps_alloc()
            mm(pss, ps[pss][0:96, 0:n], [(blk96[0:96, 0:96], sqb[si][0:96, 0:n])], [sqbB[si], b_const])
            li = h % 2
            rstd_big_n(pss, 96, vcol(V_INV96, 96), li, n)
            ps_release(pss)
            S.op("dve", lambda e: e.scalar_tensor_tensor(out=QT[0:64, h, 0:n], in0=ps[pa][0:64, 0:n], scalar=vcol(V_GQ96, 64), in1=lnv[li][0:64, 0:n],
                                                         op0=ALU.mult, op1=ALU.mult), reads=[psB[pa], lnvB[li], b_const], writes=[QTB])
            S.op("dve", lambda e: e.scalar_tensor_tensor(out=rt1[64:96, 0:n], in0=ps[pa][64:96, 0:n], scalar=vcol(V_GQ96, 32, 64), in1=csq_ap[64:96, 0, 0:n],
                                                         op0=ALU.mult, op1=ALU.mult), reads=[psB[pa], csqBuf, b_const], writes=[rt1B])
            S.op("dve", lambda e: e.scalar_tensor_tensor(out=rt2[64:96, 0:n], in0=ps[pb][64:96, 0:n], scalar=vcol(V_GQ96P, 32, 64), in1=csq_ap[64:96, 1, 0:n],
                                                         op0=ALU.mult, op1=ALU.mult), reads=[psB[pb], csqBuf, b_const], writes=[rt2B])
            S.op("pool", lambda e: e.tensor_tensor(out=rt1[64:96, 0:n], in0=rt1[64:96, 0:n], in1=rt2[64:96, 0:n], op=ALU.add),
                 reads=[rt1B, rt2B], writes=[rt1B])
            S.op("pool", lambda e: e.tensor_tensor(out=QT[64:96, h, 0:n], in0=rt1[64:96, 0:n], in1=lnv[li][64:96, 0:n], op=ALU.mult),
                 reads=[rt1B, lnvB[li]], writes=[QTB])
            ps_release(pa)
            ps_release(pb)

    def rstd_big_n(pi, rows, scale, li, n):
        S.op("act", lambda e: e.activation(out=lnv[li][0:rows, 0:n], in_=ps[pi][0:rows, 0:n], func=AF.Ln, bias=epsc[0:rows, :], scale=scale),
             reads=[psB[pi], b_const], writes=[lnvB[li]])
        S.op("act", lambda e: e.activation(out=lnv[li][0:rows, 0:n], in_=lnv[li][0:rows, 0:n], func=AF.Exp, scale=-0.5),
             reads=[lnvB[li]], writes=[lnvB[li]])

    def attention(i):
        nslots = 2 * i + 2
        S.inherit(attn_group, [actTB])
        for hg in range(2):
            acc = [ps_alloc() for _ in range(4)]
            bufof = {}

            def load(s):
                bi = attention.n % 2
                attention.n += 1
                S.dma("sp", lambda e: e.dma_start(out=KTb[bi], in_=kt_s[s][:, hg * 4:(hg + 1) * 4, :]), reads=[kt_buf[s]], writes=[KTbB[bi]])
                S.dma("sp", lambda e: e.dma_start(out=Vb[bi][:], in_=v_s[s][:, :, hg * 4:(hg + 1) * 4, :]), reads=[v_buf[s]], writes=[VbB[bi]])
                bufof[s] = bi

            steps = [(s, ks) for s in range(nslots) for ks in range(4)]
            pts = {}

            def qk_exp(n, hh):
                s, ks = steps[n]
                bi = bufof[s]
                own = (s == 2 * i)
                oth = (s == 2 * i + 1)
                q0 = ks * 128 if own else 0
                h = hg * 4 + hh
                pi = ps_alloc()
                mm(pi, ps[pi][:, q0:T], [(KTb[bi][0:96, hh, ks * 128:(ks + 1) * 128], QT[0:96, h, q0:T])], [KTbB[bi], QTB])
                pj = attention.p % NPT
                attention.p += 1
                if oth:
                    S.op("act", lambda e: e.activation(out=PT[pj][:, q0:T], in_=ps[pi][:, q0:T], func=AF.Exp, scale=SM, bias=vcol(V_FLAGNEG)),
                         reads=[psB[pi], b_const], writes=[PTB[pj]])
                else:
                    S.op("act", lambda e: e.activation(out=PT[pj][:, q0:T], in_=ps[pi][:, q0:T], func=AF.Exp, scale=SM),
                         reads=[psB[pi]], writes=[PTB[pj]])
                ps_release(pi)
                if own:
                    S.op("pool", lambda e: e.tensor_tensor(out=PT[pj][:, q0:q0 + 128], in0=PT[pj][:, q0:q0 + 128], in1=tri, op=ALU.mult),
                         reads=[PTB[pj], b_const], writes=[PTB[pj]])
                pts[(n, hh)] = (pj, q0)

            load(0)
            for hh in range(4):
                qk_exp(0, hh)
            for n, (s, ks) in enumerate(steps):
                if ks == 0 and s + 1 < nslots:
                    load(s + 1)
                bi = bufof[s]
                last = (n == len(steps) - 1)
                for hh in range(4):
                    if n + 1 < len(steps):
                        qk_exp(n + 1, hh)
                    pj, q0 = pts.pop((n, hh))
                    S.op("pe", lambda e: e.matmul(ps[acc[hh]][:, q0:T], lhsT=Vb[bi][:, ks, hh, :], rhs=PT[pj][:, q0:T], start=(n == 0), stop=last),
                         reads=[VbB[bi], PTB[pj]], writes=[psB[acc[hh]]], signal=True)
            for hh in range(4):
                h = hg * 4 + hh
                a = acc[hh]
                rd, rdB = rden2[hh % 2], rden2B[hh % 2]
                S.op("act", lambda e: e.activation(out=rd[64:128, :], in_=ps[a][64:128, :], func=AF.Ln), reads=[psB[a]], writes=[rdB])
                S.op("act", lambda e: e.activation(out=rd[64:128, :], in_=rd[64:128, :], func=AF.Exp, scale=-1.0), reads=[rdB], writes=[rdB])
                r0 = (h % 2) * 64
                S.op("dve", lambda e: e.tensor_tensor(out=oT[r0:r0 + 64, h // 2, :], in0=ps[a][0:64, :], in1=rd[64:128, :], op=ALU.mult),
                     reads=[psB[a], rdB], writes=[oTB])
                ps_release(a)

    attention.n = 0
    attention.p = 0

    def pool_branch(n, first_tile):
        for g in range(4):
            w = 2 << g
            cur, curB = uT[:, g, :], uTB
            lo = 16 - (w - 1)
            width = 1
            bufs = [(pt1, pt1B), (pt2, pt2B)]
            bi = 0
            while width < w:
                lo2 = lo + width
                dst, dstB = bufs[bi]
                S.op("pool", lambda e: e.tensor_tensor(out=dst[:, lo2:16 + n], in0=cur[:, lo2:16 + n], in1=cur[:, lo2 - width:16 + n - width], op=ALU.add),
                     reads=[curB], writes=[dstB])
                cur, curB = dst[:, :], dstB
                lo = lo2
                width *= 2
                bi ^= 1
            if first_tile:
                S.op("pool", lambda e: e.tensor_tensor(out=cur[:, 16:32], in0=cur[:, 16:32], in1=pcorr[:, g, :], op=ALU.mult),
                     reads=[curB, b_const], writes=[curB])
            S.op("dve", lambda e: e.scalar_tensor_tensor(out=dT[:, g, 0:n], in0=cur[:, 16:16 + n], scalar=1.0 / w, in1=uT[:, g, 16:16 + n],
                                                         op0=ALU.mult, op1=ALU.subtract), reads=[curB, uTB], writes=[dTB])

    def merge_out_ffn(n, nblk, P, xts, xtBs, gt_ap_fn, a2_fn, hsrc, hsrcB, y_rows_fn):
        wao, wb_aoB = w_ao_sb, b_const
        for g in range(4):
            pi = ps_alloc()
            mm(pi, ps[pi][:, 0:n], [(w_u[:, k, g * 128:(g + 1) * 128], hsrc[:, k, 0:n]) for k in range(8)], [hsrcB, b_const])
            S.op("act", lambda e: e.copy(out=uT[:, g, 16:16 + n], in_=ps[pi][:, 0:n]), reads=[psB[pi]], writes=[uTB])
            ps_release(pi)
        merge_out_ffn.pool_hook()
        for e8 in range(8):
            if e8 % 4 == 0:
                wga, wgaB = wget()
            wv = kview(8, 512)(wga)
            c = (e8 % 4) * 128
            pa = ps_alloc()
            mm(pa, ps[pa][:, 0:n], [(wao[:, pr, e8 * 128:(e8 + 1) * 128], oT[:, pr, 0:n]) for pr in range(4)], [wb_aoB, oTB])
            pg = ps_alloc()
            mm(pg, ps[pg][:, 0:n], [(wv[:, k, c:c + 128], hsrc[:, k, 0:n]) for k in range(8)], [wgaB, hsrcB])
            si = e8 % 2
            S.op("act", lambda e: e.activation(out=sig[si][:, 0:n], in_=ps[pg][:, 0:n], func=AF.Sigmoid), reads=[psB[pg]], writes=[sigB[si]])
            ps_release(pg)
            S.op("dve", lambda e: e.tensor_tensor(out=mT[:, e8, 0:n], in0=ps[pa][:, 0:n], in1=sig[si][:, 0:n], op=ALU.mult),
                 reads=[psB[pa], sigB[si]], writes=[mTB])
            ps_release(pa)
        for e8 in range(8):
            if e8 % 4 == 0:
                wgb, wgbB = wget()
            wv = kview(8, 512)(wgb)
            c = (e8 % 4) * 128
            g, hf = e8 // 2, e8 % 2
            pb = ps_alloc()
            mm(pb, ps[pb][:, 0:n], [(w_pool[:, g, hf * 128:(hf + 1) * 128], dT[:, g, 0:n])], [dTB, b_const])
            pg = ps_alloc()
            mm(pg, ps[pg][:, 0:n], [(wv[:, k, c:c + 128], hsrc[:, k, 0:n]) for k in range(8)], [wgbB, hsrcB])
            si = e8 % 2
            S.op("act", lambda e: e.activation(out=sig[si][:, 0:n], in_=ps[pg][:, 0:n], func=AF.Sigmoid), reads=[psB[pg]], writes=[sigB[si]])
            ps_release(pg)
            S.op("dve", lambda e: e.scalar_tensor_tensor(out=gtmp[si][:, 0:n], in0=ps[pb][:, 0:n], scalar=vcol(V_SP + e8), in1=sig[si][:, 0:n],
                                                         op0=ALU.mult, op1=ALU.mult), reads=[psB[pb], sigB[si], b_const], writes=[gtmpB[si]])
            ps_release(pb)
            S.op("pool", lambda e: e.tensor_tensor(out=mT[:, e8, 0:n], in0=mT[:, e8, 0:n], in1=gtmp[si][:, 0:n], op=ALU.add),
                 reads=[gtmpB[si]], writes=[mTB])
        for hf in range(2):
            wo, woB = wget()
            wv = kview(8, 512)(wo)
            for tb in range(nblk):
                pi = ps_alloc()
                mm(pi, ps[pi][0:P, :], [(mT[:, k, tb * P:(tb + 1) * P], wv[:, k, :]) for k in range(8)], [woB, mTB])
                gi = (hf * nblk + tb) % 2
                cs_ = slice(hf * 512, (hf + 1) * 512)
                S.op("dve", lambda e: e.tensor_tensor(out=gtmp[gi][0:P, 0:512], in0=ps[pi][0:P, :], in1=gt_ap_fn(0, P)[:, cs_], op=ALU.mult),
                     reads=[psB[pi], b_const], writes=[gtmpB[gi]])
                ps_release(pi)
                S.op("pool", lambda e: e.tensor_tensor(out=xts[tb][0:P, cs_], in0=xts[tb][0:P, cs_], in1=gtmp[gi][0:P, 0:512], op=ALU.add),
                     reads=[gtmpB[gi], xtBs[tb]], writes=[xtBs[tb]])
        for tb in range(nblk):
            xt, xtB = xts[tb], xtBs[tb]
            S.op("act", lambda e: e.activation(out=junk[0:P, :], in_=xt[0:P, :], func=AF.Square, accum_out=st[0:P, 6:7]),
                 reads=[xtB], writes=[junkB, stB])
            rstd_small(st[0:P, 6:7], st[0:P, 7:8], 1.0 / D)
            xi = front.n % 2
            front.n += 1
            S.op("act", lambda e: e.activation(out=xn[xi][0:P, :], in_=xt[0:P, :], func=AF.Copy, scale=st[0:P, 7:8]),
                 reads=[xtB, stB], writes=[xnB[xi]])
            pi = ps_alloc()
            pv = ps[pi][:].bitcast(BF16)
            for k in range(8):
                S.op("pe", lambda e: e.transpose(out=pv[:, k * 128:k * 128 + P], in_=xn[xi][0:P, k * 128:(k + 1) * 128], identity=identb[0:P, 0:P]),
                     reads=[xnB[xi], b_const], writes=[psB[pi]], signal=(k == 7))
            pv3 = pv.rearrange("p (k n) -> p k n", k=8)[:, :, 0:P]
            a_ap, s_ap = a2_fn(P)
            hv = hsrc[:, :, tb * P:(tb + 1) * P]
            S.op("dve", lambda e: e.tensor_tensor(out=hv, in0=pv3, in1=a_ap, op=ALU.mult), reads=[psB[pi], b_const], writes=[hsrcB])
            S.op("dve", lambda e: e.tensor_tensor(out=hv, in0=hv, in1=s_ap, op=ALU.add), reads=[hsrcB, b_const], writes=[hsrcB])
            ps_release(pi)
        S.inherit([actTB], attn_group)
        for j in range(11):
            wg, wgB = wget()
            wgg = kview(8, 256)(wg)
            wup = kview(8, 256, 2048)(wg)
            for jj in range(2):
                ch = 2 * j + jj
                pg = ps_alloc()
                pu = ps_alloc()
                mm(pg, ps[pg][:, 0:n], [(wgg[:, k, jj * 128:(jj + 1) * 128], hsrc[:, k, 0:n]) for k in range(8)], [wgB, hsrcB])
                mm(pu, ps[pu][:, 0:n], [(wup[:, k, jj * 128:(jj + 1) * 128], hsrc[:, k, 0:n]) for k in range(8)], [wgB, hsrcB])
                si = ch % 2
                S.op("act", lambda e: e.activation(out=sig[si][:, 0:n], in_=ps[pg][:, 0:n], func=AF.Silu), reads=[psB[pg]], writes=[sigB[si]])
                ps_release(pg)
                S.op("dve", lambda e: e.tensor_tensor(out=actT[:, ch, 0:n], in0=ps[pu][:, 0:n], in1=sig[si][:, 0:n], op=ALU.mult),
                     reads=[psB[pu], sigB[si]], writes=[actTB])
                ps_release(pu)
        for qd in range(4):
            cs_ = slice(qd * 256, (qd + 1) * 256)
            pds = [ps_alloc() for _ in range(nblk)]
            for kh in range(2):
                wd, wdB = wget()
                wv = kview(11, 256)(wd)
                for tb in range(nblk):
                    for k in range(11):
                        S.op("pe", lambda e: e.matmul(ps[pds[tb]][0:P, 0:256], lhsT=actT[:, kh * 11 + k, tb * P:(tb + 1) * P], rhs=wv[:, k, :],
                                                      start=(kh == 0 and k == 0), stop=(kh == 1 and k == 10)),
                             reads=[wdB, actTB], writes=[psB[pds[tb]]], signal=(k == 10))
            for tb in range(nblk):
                pi = pds[tb]
                gi = (qd * nblk + tb) % 2
                S.op("dve", lambda e: e.tensor_tensor(out=gtmp[gi][0:P, 0:256], in0=ps[pi][0:P, 0:256], in1=gt_ap_fn(1, P)[:, cs_], op=ALU.mult),
                     reads=[psB[pi], b_const], writes=[gtmpB[gi]])
                ps_release(pi)
                S.op("pool", lambda e: e.tensor_tensor(out=xts[tb][0:P, cs_], in0=xts[tb][0:P, cs_], in1=gtmp[gi][0:P, 0:256], op=ALU.add),
                     reads=[gtmpB[gi], xtBs[tb]], writes=[xtBs[tb]])
        for tb in range(nblk):
            S.dma("sp", lambda e: e.dma_start(out=y_rows_fn(tb), in_=xts[tb][0:P, :]), reads=[xtBs[tb]], tag=xtBs[tb])


    def sample_phase():
        esS = contextlib.ExitStack()
        alloc_tile_bufs(NS, esS)
        G = 16
        NB = PAGE // G
        gtS = sb("gtS", [16, 2, D], F32, esS)
        x_s = sb("x_s", [16, D], F32, esS)
        x_sB = Buf("x_s")
        cs_s = sb("cs_s", [128, 1, 64], F32, esS)
        csq_s = sb("csq_s", [128, 2, NS], F32, esS)
        ptT = sb("ptT", [128, NS], I32, esS)
        ciota = sb("ciota", [128, 16], I32, esS)
        idx_l = sb("idx_l", [128, NS, 16], I32, esS)
        idx_k = sb("idx_k", [128, NS, 2], I32, esS)
        w_ukT = sb("w_ukT", [64, H, KVR], BF16, esS)
        qg = sb("qg", [64, H, NS], BF16, esS)
        qabs = sb("qabs", [128, 2, NS, H], BF16, esS)
        qpe = sb("qpe", [32, NS, H], BF16, esS)
        glat = [sb("glat%d" % i, [128, G, KVR], BF16, esS) for i in range(3)]
        glatB = [Buf("glat%d" % i) for i in range(3)]
        gkr = [sb("gkr%d" % i, [128, PAGE, DR], BF16, esS) for i in range(2)]
        gkrB = [Buf("gkr%d" % i) for i in range(2)]
        latT = [sb("latT%d" % i, [128, 2, 128], BF16, esS) for i in range(2)]
        latTB = [Buf("latT%d" % i) for i in range(2)]
        kT = [sb("kT%d" % i, [32, 128], BF16, esS) for i in range(2)]
        kTB = [Buf("kT%d" % i) for i in range(2)]
        ssb = [sb("ssb%d" % i, [128, G, H], F32, esS) for i in range(2)]
        ssbB = [Buf("ssb%d" % i) for i in range(2)]
        scb = [sb("scb%d" % i, [128, G, H], F32, esS) for i in range(2)]
        scbB = [Buf("scb%d" % i) for i in range(2)]
        pb = [sb("pb%d" % i, [128, G, H], BF16, esS) for i in range(2)]
        pbB = [Buf("pb%d" % i) for i in range(2)]
        nw = sb("nw", [16, 4, H], F32, esS)
        nwB = Buf("nw")
        pnew = sb("pnew", [16, H], BF16, esS)
        pnewB = Buf("pnew")
        dn = sb("dn", [8, 4, H], F32, esS)
        dnB = Buf("dn")
        oln = sb("oln", [8, KVR], BF16, esS)
        olnB = Buf("oln")
        OLT = sb("OLT", [128, 2, NS, H], BF16, esS)
        OLTB = Buf("OLT")
        prevT = sb("prevT", [128, 4, NS, 15], F32, esS)
        prevTB = Buf("prevT")
        stt = [sb("stt%d" % i, [120, DP], F32, esS) for i in range(2)]
        sttB = [Buf("stt%d" % i) for i in range(2)]
        wsum = sb("wsum", [128, 4, NS], F32, esS)
        wsumB = Buf("wsum")
        bS = Buf("smp_const")

        def sload(q, dst, src):
            S.dma(q, lambda e: e.dma_start(out=dst, in_=src), writes=[bS], tag=bS)

        S.dma("sp", lambda e: e.dma_start(out=x_s[:], in_=x_smp), writes=[x_sB])
        sload("sp", cs_s[:], cs_smp_d)
        sload("sp", csq_s[:], csq_smp_d)
        sload("sp", ptT[:], ptT_d)
        sload("sp", ciota[:], ciota_d)
        sload("pool", w_ukT[:], w_ukT_d)
        S.dma("sp", lambda e: e.dma_start(out=pool_smp[:, 0:14, :], in_=state_d[:, 1:15, :]), writes=[bS], tag=bS)
        for blk in range(2):
            S.dma("sp", lambda e: e.dma_start(out=stt[blk][:], in_=state_d[blk * 8:(blk + 1) * 8].rearrange("s r d -> (s r) d")), writes=[sttB[blk]])
        S.op("pool", lambda e: e.tensor_scalar(out=idx_l[:], in0=ptT[:].unsqueeze(2).to_broadcast([128, NS, 16]), scalar1=16, scalar2=None, op0=ALU.mult),
             reads=[bS], writes=[bS])
        S.op("pool", lambda e: e.tensor_tensor(out=idx_l[:], in0=idx_l[:], in1=ciota[:].unsqueeze(1).to_broadcast([128, NS, 16]), op=ALU.add),
             reads=[bS], writes=[bS])
        S.op("pool", lambda e: e.tensor_scalar(out=idx_k[:], in0=ptT[:].unsqueeze(2).to_broadcast([128, NS, 2]), scalar1=2, scalar2=None, op0=ALU.mult),
             reads=[bS], writes=[bS])
        S.op("pool", lambda e: e.tensor_tensor(out=idx_k[:], in0=idx_k[:], in1=ciota[:, 0:2].unsqueeze(1).to_broadcast([128, NS, 2]), op=ALU.add),
             reads=[bS], writes=[bS])
        b_gt = sb("b_gt_s", [16, 2, D], F32, esS)
        b_gtB = Buf("b_gt_s")
        S.dma("sp", lambda e: e.dma_start(out=b_gt[:], in_=b_gt_d), writes=[b_gtB])
        for which in range(2):
            for hf in range(2):
                wb, wbB = wget()
                wv = kview(8, 512)(wb)
                pj = ps_alloc()
                mm(pj, ps[pj][0:16, :], [(scT[:, k, 1:17], wv[:, k, :]) for k in range(8)], [wbB, b_const])
                cs_ = slice(hf * 512, (hf + 1) * 512)
                S.op("dve", lambda e: e.tensor_tensor(out=gtS[:, which, cs_], in0=ps[pj][0:16, :], in1=b_gt[:, which, cs_], op=ALU.add),
                     reads=[psB[pj], b_gtB], writes=[b_const])
                ps_release(pj)
        for blk in range(2):
            for g in range(4):
                pi = ps_alloc()
                S.op("pe", lambda e: e.transpose(out=ps[pi][:, 0:120], in_=stt[blk][:, g * 128:(g + 1) * 128], identity=identf[0:120, 0:120]),
                     reads=[sttB[blk], b_const], writes=[psB[pi]])
                S.op("act", lambda e: e.copy(out=prevT[:, g, blk * 8:(blk + 1) * 8, :], in_=ps[pi][:, 0:120].rearrange("p (s r) -> p s r", s=8)),
                     reads=[psB[pi]], writes=[prevTB])
                ps_release(pi)
        if smp_stop <= 1:
            S.barrier(); esS.close(); return
        front2([(x_s, x_sB, 0, cs_s[:, 0, :], lat_smp, kr_smp)], NS, True, hT, hTB, bS)
        q_build(NS, csq_s, bS, hT, hTB)
        S.op("dve", lambda e: e.tensor_scalar(out=qg[:], in0=QT[0:64, :, 0:NS], scalar1=vcol(V_GQK, 64), scalar2=None, op0=ALU.mult),
             reads=[QTB, b_const], writes=[bS])
        pi = ps_alloc()
        for rc in range(2):
            for h in range(H):
                c0 = (rc * H + h) * NS
                mm(pi, ps[pi][:, c0:c0 + NS], [(w_ukT[0:64, h, rc * 128:(rc + 1) * 128], qg[0:64, h, :])], [bS], last_signal=(rc == 1 and h == H - 1))
        S.op("act", lambda e: e.copy(out=qabs[:].rearrange("p c s h -> p c h s"), in_=ps[pi][:, 0:2 * H * NS].rearrange("p (c h s) -> p c h s", c=2, h=H)),
             reads=[psB[pi]], writes=[bS])
        ps_release(pi)
        S.op("pool", lambda e: e.tensor_copy(out=qpe[:].rearrange("p s h -> p h s"), in_=QT[64:96, :, 0:NS]), reads=[QTB], writes=[bS])
        pk = ps_alloc()
        mm(pk, ps[pk][0:NS, :], [(ckvnT[:, rc, 0:NS], w_uk[:, rc, :]) for rc in range(2)], [ckvnTB, b_const])
        S.op("act", lambda e: e.activation(out=sqb[0][0:NS, :], in_=ps[pk][0:NS, :], func=AF.Square), reads=[psB[pk]], writes=[sqbB[0]])
        ps_release(pk)
        S.op("dve", lambda e: e.tensor_reduce(out=nw[:, 0, :], in_=sqb[0][0:NS, :].rearrange("p (h d) -> p h d", h=H), axis=AX.X, op=ALU.add),
             reads=[sqbB[0]], writes=[nwB])
        S.op("act", lambda e: e.activation(out=nw[:, 1, :], in_=nw[:, 0, :], func=AF.Ln, bias=epsc[0:NS, :], scale=1.0 / 64), reads=[nwB, b_const], writes=[nwB])
        S.op("act", lambda e: e.activation(out=nw[:, 1, :], in_=nw[:, 1, :], func=AF.Exp, scale=-0.5), reads=[nwB], writes=[nwB])

        if smp_stop <= 2:
            S.barrier(); esS.close(); return
        cache_l2 = cache_lat
        cache_k2 = cache_kr
        st_ = {"gl": 0, "t": 0, "b": 0}

        def gather_lat(s, b):
            gi = st_["gl"] % 3
            st_["gl"] += 1
            for cc in range(2):
                c = 2 * b + cc
                S.dma("pool", lambda e: e.indirect_dma_start(out=glat[gi][:, cc * 8:(cc + 1) * 8, :].rearrange("p r d -> p (r d)"), out_offset=None,
                                                             in_=cache_l2, in_offset=bass.IndirectOffsetOnAxis(ap=idx_l[:, s, c:c + 1], axis=0)),
                      reads=[bS], writes=[glatB[gi]])
            return gi

        def gather_kr(s):
            gi = s % 2
            for c in range(2):
                S.dma("pool", lambda e: e.indirect_dma_start(out=gkr[gi][:, c * 64:(c + 1) * 64, :].rearrange("p r d -> p (r d)"), out_offset=None,
                                                             in_=cache_k2, in_offset=bass.IndirectOffsetOnAxis(ap=idx_k[:, s, c:c + 1], axis=0)),
                      reads=[bS], writes=[gkrB[gi]])
            return gi

        seq = [(s, b) for s in range(NS) for b in range(NB)]
        if smp_stop < 90:
            seq = seq[:max(1, smp_stop - 3)]
        ones1k = sb("ones1k", [128, 1024], BF16, esS)
        S.op("pool", lambda e: e.memset(ones1k[:], 1.0), writes=[bS])
        latT4 = [sb("latT4_%d" % i, [128, 4, 2, 128], BF16, esS) for i in range(2)]
        latT4B = [Buf("latT4_%d" % i) for i in range(2)]
        kT4 = [sb("kT4_%d" % i, [32, 4, 128], BF16, esS) for i in range(2)]
        kT4B = [Buf("kT4_%d" % i) for i in range(2)]
        OD = ps_alloc()
        OD_bf = ps[OD][:].bitcast(BF16)
        gl_of, kr_of, pdr_of = {}, {}, {}
        gl_of[0] = gather_lat(*seq[0])
        kr_of[0] = gather_kr(0)

        def batch_res(n_):
            if n_ >= len(seq) or n_ in gl_of:
                return
            s_, b_ = seq[n_]
            gl_of[n_] = gather_lat(s_, b_)
            if b_ == 0 and s_ not in kr_of:
                kr_of[s_] = gather_kr(s_)

        quads = [(n_, q) for n_ in range(len(seq)) for q in range(4)]

        def emit_T(qi):
            n_, q = quads[qi]
            batch_res(n_)
            if n_ not in pdr_of:
                pdr_of[n_] = ps_alloc()
            s_, b_ = seq[n_]
            gl, kr_i, pdr = gl_of[n_], kr_of[s_], pdr_of[n_]
            A = ps_alloc()
            Av = ps[A][:].bitcast(BF16)
            pdv = ps[pdr][:].bitcast(BF16)
            li = qi % 2
            for t in range(4):
                g = q * 4 + t
                for rc in range(2):
                    S.op("pe", lambda e: e.transpose(out=Av[:, (t * 2 + rc) * 128:(t * 2 + rc + 1) * 128], in_=glat[gl][:, g, rc * 128:(rc + 1) * 128], identity=identb),
                         reads=[glatB[gl], b_const], writes=[psB[A]], signal=(t == 3 and rc == 1))
            for t in range(4):
                row = b_ * G + q * 4 + t
                S.op("pe", lambda e: e.transpose(out=pdv[0:32, 512 + t * 128:512 + (t + 1) * 128], in_=gkr[kr_i][:, row, :], identity=identb),
                     reads=[gkrB[kr_i], b_const], writes=[psB[pdr]], signal=(t == 3))
            S.op("dve", lambda e: e.tensor_tensor(out=latT4[li][:].rearrange("p t c n -> p (t c n)"), in0=Av[:, :], in1=ones1k[:, :], op=ALU.mult),
                 reads=[psB[A], bS], writes=[latT4B[li]])
            ps_release(A)
            S.op("act", lambda e: e.copy(out=kT4[li][:].rearrange("p t n -> p (t n)"), in_=pdv[0:32, 512:1024]), reads=[psB[pdr]], writes=[kT4B[li]])

        pending = []

        def batch_pe(s, b, bi, gl):
            for g in range(G):
                S.op("pe", lambda e: e.matmul(ps[OD][0:8, 0:KVR], lhsT=pb[bi][:, g, :], rhs=glat[gl][:, g, :], start=(b == 0 and g == 0), stop=False),
                     reads=[pbB[bi], glatB[gl]], writes=[psB[OD]], signal=(g == G - 1))
            S.op("pe", lambda e: e.matmul(ps[OD][0:8, 256:256 + G * H], lhsT=onesb[:, 0:8], rhs=pb[bi][:].rearrange("p g h -> p (g h)"), start=(b == 0), stop=(b == NB - 1)),
                 reads=[pbB[bi], b_const], writes=[psB[OD]])
            if b == NB - 1:
                pdn = ps_alloc()
                mm(pdn, ps[pdn][0:NS, 0:8], [(ckvnT[:, rc, 0:NS], qabs[:, rc, s, :]) for rc in range(2)], [ckvnTB, bS], last_signal=False)
                mm(pdn, ps[pdn][0:NS, 8:16], [(kpeT[0:32, 0:NS], qpe[:, s, :])], [kpeTB, bS])
                S.op("dve", lambda e: e.tensor_tensor(out=nw[:, 2, :], in0=ps[pdn][0:NS, 0:8], in1=nw[:, 1, :], op=ALU.mult), reads=[psB[pdn], nwB], writes=[nwB])
                S.op("dve", lambda e: e.tensor_tensor(out=nw[:, 2, :], in0=ps[pdn][0:NS, 8:16], in1=nw[:, 2, :], op=ALU.add), reads=[psB[pdn], nwB], writes=[nwB])
                S.op("act", lambda e: e.activation(out=nw[:, 3, :], in_=nw[:, 2, :], func=AF.Exp, scale=SM), reads=[nwB], writes=[nwB])
                S.op("dve", lambda e: e.tensor_scalar(out=pnew[:], in0=nw[:, 3, :], scalar1=identf[0:NS, s:s + 1], scalar2=None, op0=ALU.mult),
                     reads=[nwB, b_const], writes=[pnewB])
                S.op("pe", lambda e: e.matmul(ps[OD][0:8, 0:KVR], lhsT=pnew[:, :], rhs=ckv_b[smp_ci["i"]][0:NS, 0:KVR], start=False, stop=True),
                     reads=[pnewB, ckv_bB[smp_ci["i"]]], writes=[psB[OD]])
                S.op("dve", lambda e: e.tensor_reduce(out=dn[:, 0, :], in_=ps[OD][0:8, 256:256 + G * H].rearrange("p (g h) -> p h g", g=G), axis=AX.X, op=ALU.add),
                     reads=[psB[OD]], writes=[dnB])
                mm(pdn, ps[pdn][0:8, 16:16 + H], [(onesb[0:NS, 0:8], pnew[:, :])], [pnewB, b_const])
                S.op("dve", lambda e: e.tensor_tensor(out=dn[:, 0, :], in0=ps[pdn][0:8, 16:16 + H], in1=dn[:, 0, :], op=ALU.add), reads=[psB[pdn], dnB], writes=[dnB])
                ps_release(pdn)
                S.op("dve", lambda e: e.tensor_tensor(out=dn[:, 1, :], in0=dn[:, 0, :], in1=identf[0:8, 0:8], op=ALU.mult), reads=[dnB, b_const], writes=[dnB])
                S.op("dve", lambda e: e.tensor_reduce(out=dn[:, 2, 0:1], in_=dn[:, 1, :], axis=AX.X, op=ALU.add), reads=[dnB], writes=[dnB])
                S.op("dve", lambda e: e.reciprocal(out=dn[:, 2, 1:2], in_=dn[:, 2, 0:1]), reads=[dnB], writes=[dnB])
                S.op("dve", lambda e: e.tensor_scalar(out=oln[:], in0=ps[OD][0:8, 0:KVR], scalar1=dn[:, 2, 1:2], scalar2=None, op0=ALU.mult),
                     reads=[psB[OD], dnB], writes=[olnB])
                for rc in range(2):
                    c0 = 768 + (rc * NS + s) * H
                    S.op("pe", lambda e: e.transpose(out=OD_bf[:, c0:c0 + H], in_=oln[0:8, rc * 128:(rc + 1) * 128], identity=identb[0:8, 0:8]),
                         reads=[olnB, b_const], writes=[psB[OD]], signal=(rc == 1))

        emit_T(0)
        for qi, (n_, q) in enumerate(quads):
            s, b = seq[n_]
            if q == 1:
                batch_res(n_ + 1)
                batch_res(n_ + 2)
            if qi + 1 < len(quads):
                emit_T(qi + 1)
            gl, pdr = gl_of[n_], pdr_of[n_]
            bi = n_ % 2
            li = qi % 2
            for t in range(4):
                g = q * 4 + t
                pk = ps_alloc()
                mm(pk, ps[pk][:, :], [(latT4[li][:, t, rc, :], w_uk[:, rc, :]) for rc in range(2)], [latT4B[li], b_const])
                si = st_["t"] % 3
                st_["t"] += 1
                S.op("act", lambda e: e.activation(out=sqb[si][:, :], in_=ps[pk][:, :], func=AF.Square), reads=[psB[pk]], writes=[sqbB[si]])
                ps_release(pk)
                red = "dve"
                S.op(red, lambda e: e.tensor_reduce(out=ssb[bi][:, g, :], in_=sqb[si][:, :].rearrange("p (h d) -> p h d", h=H), axis=AX.X, op=ALU.add),
                     reads=[sqbB[si]], writes=[ssbB[bi]])
                mm(pdr, ps[pdr][:, g * 16:g * 16 + 8], [(latT4[li][:, t, rc, :], qabs[:, rc, s, :]) for rc in range(2)], [latT4B[li], bS], last_signal=False)
                mm(pdr, ps[pdr][:, g * 16 + 8:g * 16 + 16], [(kT4[li][:, t, :], qpe[:, s, :])], [kT4B[li], bS], last_signal=True)
            if q == 0 and pending:
                batch_pe(*pending.pop(0))
            if q < 3:
                continue
            S.op("act", lambda e: e.activation(out=ssb[bi][:], in_=ssb[bi][:], func=AF.Ln, bias=epsc[:], scale=1.0 / 64), reads=[ssbB[bi], b_const], writes=[ssbB[bi]])
            S.op("act", lambda e: e.activation(out=ssb[bi][:], in_=ssb[bi][:], func=AF.Exp, scale=-0.5), reads=[ssbB[bi]], writes=[ssbB[bi]])
            drv = ps[pdr][:, 0:G * 16].rearrange("p (g x) -> p g x", g=G)
            S.op("dve", lambda e: e.tensor_tensor(out=scb[bi][:], in0=drv[:, :, 0:8], in1=ssb[bi][:], op=ALU.mult), reads=[psB[pdr], ssbB[bi]], writes=[scbB[bi]])
            S.op("dve", lambda e: e.tensor_tensor(out=scb[bi][:], in0=drv[:, :, 8:16], in1=scb[bi][:], op=ALU.add), reads=[psB[pdr], scbB[bi]], writes=[scbB[bi]])
            ps_release(pdr)
            S.op("act", lambda e: e.activation(out=pb[bi][:], in_=scb[bi][:], func=AF.Exp, scale=SM), reads=[scbB[bi]], writes=[pbB[bi]])
            pending.append((s, b, bi, gl))
        while pending:
            batch_pe(*pending.pop(0))
        pT_all = OD
        pT_v = OD_bf[:, 768:1024]
        if smp_stop < 95:
            S.barrier(); esS.close(); return
        S.op("act", lambda e: e.copy(out=OLT[:].rearrange("p c s h -> p (c s h)"), in_=pT_v[:, 0:2 * NS * H]), reads=[psB[pT_all]], writes=[OLTB])
        ps_release(pT_all)
        for h in range(H):
            pi = ps_alloc()
            mm(pi, ps[pi][0:64, 0:NS], [(w_uv[:, rc, h * 64:(h + 1) * 64], OLT[:, rc, :, h]) for rc in range(2)], [OLTB, b_const])
            r0 = (h % 2) * 64
            S.op("act", lambda e: e.copy(out=oT[r0:r0 + 64, h // 2, 0:NS], in_=ps[pi][0:64, 0:NS]), reads=[psB[pi]], writes=[oTB])
            ps_release(pi)

        def pool_hook_s():
            for g in range(4):
                w = 2 << g
                S.op("dve", lambda e: e.tensor_reduce(out=wsum[:, g, :], in_=prevT[:, g, :, 15 - (w - 1):15], axis=AX.X, op=ALU.add),
                     reads=[prevTB], writes=[wsumB])
                S.op("dve", lambda e: e.tensor_tensor(out=wsum[:, g, :], in0=wsum[:, g, :], in1=uT[:, g, 16:16 + NS], op=ALU.add),
                     reads=[wsumB, uTB], writes=[wsumB])
                S.op("dve", lambda e: e.scalar_tensor_tensor(out=dT[:, g, 0:NS], in0=wsum[:, g, :], scalar=1.0 / w, in1=uT[:, g, 16:16 + NS],
                                                             op0=ALU.mult, op1=ALU.subtract), reads=[wsumB, uTB], writes=[dTB])
            pi = ps_alloc()
            for g in range(4):
                S.op("pe", lambda e: e.transpose(out=ps[pi][0:NS, g * 128:(g + 1) * 128], in_=uT[:, g, 16:16 + NS], identity=identf[:]),
                     reads=[uTB, b_const], writes=[psB[pi]], signal=(g == 3))
            S.op("act", lambda e: e.copy(out=gtmp[0][0:NS, 0:512], in_=ps[pi][0:NS, :]), reads=[psB[pi]], writes=[gtmpB[0]])
            ps_release(pi)
            S.dma("sp", lambda e: e.dma_start(out=pool_smp[:, 14, :], in_=gtmp[0][0:NS, 0:512]), reads=[gtmpB[0]], tag=gtmpB[0])

        merge_out_ffn.pool_hook = pool_hook_s
        merge_out_ffn(NS, 1, NS, [x_s], [x_sB],
                      lambda which, P: gtS[0:P, which, :],
                      lambda P: (modF[:, 24:32, 1:1 + P], modF[:, 16:24, 1:1 + P]),
                      hT, hTB,
                      lambda tb: y_smp)
        S.barrier()
        esS.close()

    smp_ci = {"i": 0}

    x_tok = [sb("x_tok%d" % i, [128, D], F32, esP) for i in range(NXT)]
    x_tokB = [Buf("x_tok%d" % i) for i in range(NXT)]
    xo_tok = [sb("xo_tok%d" % i, [128, D], F32, esP) for i in range(2)]
    xo_tokB = [Buf("xo_tok%d" % i) for i in range(2)]
    cs_tok = [sb("cs_tok%d" % i, [128, 4, 64], F32, esP) for i in range(2)]
    cs_tokB = [Buf("cs_tok%d" % i) for i in range(2)]
    csq = sb("csq", [128, 2, T], F32, esP)
    csqB = Buf("csq")
    utail = [sb("utail%d" % i, [128, 4, 16], F32, esP) for i in range(2)]
    utailB = [Buf("utail%d" % i) for i in range(2)]
    pcorr = sb("pcorr", [128, 4, 16], F32, esP)
    Vb = [sb("Vb%d" % i, [128, 4, 4, 128], BF16, esP) for i in range(2)]
    VbB = [Buf("Vb%d" % i) for i in range(2)]
    cload("sp", pcorr[:], pcorr_d)
    S.op("dve", lambda e: e.memset(utail[0][:], 0.0), writes=[utailB[0]])
    S.op("dve", lambda e: e.memset(utail[1][:], 0.0), writes=[utailB[1]])
    for i in range(2):
        S.op("pool", lambda e: e.memset(Vb[i][:], 1.0), writes=[VbB[i]])
    xq = {"n": 0}

    def x_block():
        i = xq["n"] % NXT
        xq["n"] += 1
        return x_tok[i], x_tokB[i]

    for i in range(NT):
        ci = i % 2
        S.dma("sp", lambda e: e.dma_start(out=cs_tok[0][:], in_=cs_own_d[i]), writes=[cs_tokB[0]])
        S.dma("sp", lambda e: e.dma_start(out=cs_tok[1][:], in_=cs_oth_d[i]), writes=[cs_tokB[1]])
        S.dma("sp", lambda e: e.dma_start(out=csq[:], in_=csq_own_d[i]), writes=[csqB])
        for half in range(2):
            blks = []
            for jj in range(2):
                sbk = 2 * half + jj
                r0 = i * T + sbk * 128
                S.dma("sp", lambda e: e.dma_start(out=xo_tok[jj][:], in_=x_oth[r0:r0 + 128, :]), writes=[xo_tokB[jj]])
                blks.append((xo_tok[jj], xo_tokB[jj], sbk * 128, cs_tok[1][:, sbk, :], None, None))
            front2(blks, 128, False, hTo, actTB, cs_tokB[1])
        kv_build(2 * i + 1, T)
        ut = utail[i % 2]
        utB_ = utailB[i % 2]
        for g in range(4):
            pi = ps_alloc()
            mm(pi, ps[pi][:, 0:16], [(w_u[:, k, g * 128:(g + 1) * 128], hTo[:, k, T - 16:T]) for k in range(8)], [actTB, b_const])
            S.op("act", lambda e: e.copy(out=ut[:, g, :], in_=ps[pi][:, 0:16]), reads=[psB[pi]], writes=[utB_])
            ps_release(pi)
        xts, xtBs = [], []
        for half in range(2):
            blks = []
            for jj in range(2):
                sbk = 2 * half + jj
                xt, xtB = x_block()
                xts.append(xt)
                xtBs.append(xtB)
                r0 = i * T + sbk * 128
                S.dma("sp", lambda e: e.dma_start(out=xt[:], in_=x_own[r0:r0 + 128, :]), writes=[xtB])
                blks.append((xt, xtB, sbk * 128, cs_tok[0][:, sbk, :], lat_own[r0:r0 + 128, :], kr_own[r0:r0 + 128, :]))
            front2(blks, 128, False, hT, hTB, cs_tokB[0])
        kv_build(2 * i, T)
        q_build(T, csq, csqB, hT, hTB)
        attention(i)

        def pool_hook(i=i):
            prev = utail[(i + 1) % 2]
            prevB = utailB[(i + 1) % 2]
            cur = utail[i % 2]
            curB = utailB[i % 2]
            S.op("pool", lambda e: e.tensor_scalar(out=pt1[:, 0:64].rearrange("p (g n) -> p g n", g=4), in0=prev[:], scalar1=vcol(V_NFLAG), scalar2=None, op0=ALU.mult),
                 reads=[prevB, b_const], writes=[pt1B])
            S.op("dve", lambda e: e.scalar_tensor_tensor(out=uT[:, :, 1:16], in0=cur[:, :, 1:16], scalar=vcol(V_FLAG),
                                                          in1=pt1[:, 0:64].rearrange("p (g n) -> p g n", g=4)[:, :, 1:16], op0=ALU.mult, op1=ALU.add),
                 reads=[curB, pt1B, b_const], writes=[uTB])
            pool_branch(T, first_tile=(i == 0))
            if i == NT - 1:
                pi = ps_alloc()
                for g in range(4):
                    S.op("pe", lambda e: e.transpose(out=ps[pi][0:16, g * 128:(g + 1) * 128], in_=uT[:, g, T:T + 16], identity=identf[:]),
                         reads=[uTB, b_const], writes=[psB[pi]], signal=(g == 3))
                S.op("act", lambda e: e.copy(out=gtmp[0][0:16, 0:512], in_=ps[pi][0:16, :]), reads=[psB[pi]], writes=[gtmpB[0]])
                ps_release(pi)
                S.dma("sp", lambda e: e.dma_start(out=pool_own, in_=gtmp[0][0:16, 0:512]), reads=[gtmpB[0]], tag=gtmpB[0])

        merge_out_ffn.pool_hook = pool_hook
        merge_out_ffn(T, 4, 128, xts, xtBs,
                      lambda which, P: gt_bc[0:P, which, :],
                      lambda P: (modF[:, 24:32, 0:1].to_broadcast([128, 8, P]), modF[:, 16:24, 0:1].to_broadcast([128, 8, P])),
                      hT, hTB,
                      lambda tb, i=i: y_own[i * T + tb * 128:i * T + (tb + 1) * 128, :])

    S.barrier()
    esP.close()
    if do_sample:
        sample_phase()
        S.barrier()
    es.close()
    S.close()
    return nc


def _rope_tables(pos):
    inv = 1.0 / (10000.0 ** (np.arange(0, DR, 2, dtype=np.float32) / DR))
    ang = pos.astype(np.float32)[:, None] * inv[None, :].astype(np.float32)
    ang = ang.astype(np.float32)
    return np.cos(ang).astype(np.float32), np.sin(ang).astype(np.float32)


def _consts():
    cb = np.zeros((128, 5, 128), np.float32)
    cb[:, 0, :] = np.eye(128)
    cb[0:64, 1, 0:64] = 1.0
    cb[64:96, 1, 64:96] = 1.0
    cb[0:64, 2, 0:64] = 1.0
    cb[64:128, 2, 64:128] = 1.0
    cb[:, 3, :] = (np.arange(128)[:, None] <= np.arange(128)[None, :]).astype(np.float32)
    cb[:, 4, :] = 1.0
    return cb


def _fm(v, ncol):
    return np.ascontiguousarray(v.reshape(ncol, 128).T)


PERM = np.concatenate([np.arange(16, 32), np.arange(0, 16)])


def make_in_maps(inp, NT=8, NS=16, do_sample=True, cores=range(8)):
    f32 = np.float32
    w_in = inp["w_in"][0]
    w_uq = inp["w_uq"][0]
    b_ada = inp["b_ada"][0]
    shared = {}
    shared["w_ada"] = np.ascontiguousarray(inp["w_ada"][0])
    shared["b_gt"] = np.ascontiguousarray(np.broadcast_to(np.stack([b_ada[2048:3072], b_ada[5120:6144]])[None], (16, 2, D)))
    shared["rowv"] = np.ascontiguousarray(np.broadcast_to(np.concatenate([inp["g_kv_lat"][0], inp["g_k_rope"][0]])[None], (128, 288)))
    shared["cbf"] = _consts()
    shared["identf"] = np.eye(128, dtype=f32)
    shared["w_in_q"] = np.ascontiguousarray(w_in[:, 0:384])
    shared["w_in_kv"] = np.ascontiguousarray(w_in[:, 384:672])
    shared["w_in_u"] = np.ascontiguousarray(w_in[:, 672:1184])
    shared["w_in_g"] = np.ascontiguousarray(w_in[:, 1184:3232])
    wq2 = np.zeros((QR, H, 192), f32)
    wq2[:, :, 0:96] = w_uq
    wq2[:, :, 96 + 64:192] = w_uq[:, :, 64 + PERM]
    shared["w_uq2"] = wq2
    shared["w_uk"] = np.ascontiguousarray(inp["w_uk"][0].reshape(KVR, 512))
    shared["w_uv"] = np.ascontiguousarray(inp["w_uv"][0].reshape(KVR, 512))
    shared["w_ao"] = np.ascontiguousarray(inp["w_attn_o"][0].reshape(512, D))
    shared["w_pool"] = np.ascontiguousarray(inp["w_pool"][0])
    shared["w_out"] = np.ascontiguousarray(inp["w_out"][0])
    shared["w_gu"] = np.ascontiguousarray(inp["w_gu"][0])
    shared["w_down"] = np.ascontiguousarray(inp["w_down"][0])
    if do_sample:
        shared["w_ukT"] = np.ascontiguousarray(inp["w_uk"][0].transpose(2, 1, 0))
        shared["cache_lat"] = inp["cache_kv_latent"][0].reshape(-1, 2048)
        shared["cache_kr"] = inp["cache_k_rope"][0].reshape(-1, 2048)
    gq = inp["g_q_rope"][0]
    in_maps = []
    for c in cores:
        b, half = c // 2, c % 2
        m = dict(shared)
        xs = inp["x_prompt"][b].reshape(16, T, D)
        own_t = [2 * i + half for i in range(NT)]
        oth_t = [2 * i + 1 - half for i in range(NT)]
        m["x_own"] = np.ascontiguousarray(xs[own_t].reshape(NT * T, D))
        m["x_oth"] = np.ascontiguousarray(xs[oth_t].reshape(NT * T, D))
        cT = np.zeros((128, 8, 17), f32)
        cT[:, :, 0] = _fm(inp["c_prompt"][b], 8)
        for s in range(NS):
            cT[:, :, 1 + s] = _fm(inp["c_sample"][16 * c + s], 8)
        m["cT"] = cT
        vecs = np.zeros((128, 80), f32)
        for gi, base in enumerate([0, 1024, 3072, 4096]):
            vecs[:, gi * 8:(gi + 1) * 8] = _fm(b_ada[base:base + 1024], 8)
        vecs[:, 32:40] = _fm(inp["g_norm1"][0], 8)
        vecs[:, 40:48] = _fm(inp["g_norm2"][0], 8)
        vecs[:, 48:56] = _fm(inp["s_pool"][0], 8)
        vecs[:, 56:59] = _fm(inp["g_q_lat"][0], 3)
        vecs[0:64, 59] = inp["g_q_nope"][0]
        vecs[64:96, 59] = gq
        vecs[64:96, 60] = gq[PERM]
        vecs[0:64, 61] = 1.0 / 64
        vecs[64:96, 61] = 1.0 / 32
        vecs[0:64, 62] = inp["g_k_nope"][0]
        vecs[64:128, 62] = inp["g_k_nope"][0]
        vecs[:, 63] = 0.0 if half == 1 else NEG
        vecs[:, 64] = float(half)
        vecs[:, 65] = 1.0 - float(half)
        vecs[0:64, 66] = inp["g_k_nope"][0]
        m["vecs"] = vecs
        cs_o = np.zeros((NT, 128, 4, 64), f32)
        cs_x = np.zeros((NT, 128, 4, 64), f32)
        csq_o = np.zeros((NT, 128, 2, T), f32)
        for i in range(NT):
            for tl, dst in ((own_t[i], cs_o), (oth_t[i], cs_x)):
                cos, sin = _rope_tables(np.arange(tl * T, (tl + 1) * T))
                tab = np.concatenate([cos, cos, -sin, sin], axis=1).reshape(4, 128, 64).transpose(1, 0, 2)
                dst[i] = tab
            cos, sin = _rope_tables(np.arange(own_t[i] * T, (own_t[i] + 1) * T))
            csq_o[i, 64:96, 0, :] = np.concatenate([cos, cos], axis=1).T
            csq_o[i, 64:96, 1, :] = np.concatenate([-sin, sin], axis=1).T
        m["cs_own"], m["cs_oth"], m["csq_own"] = cs_o, cs_x, csq_o
        pc = np.ones((128, 4, 16), f32)
        if half == 0:
            for g, w in enumerate((2, 4, 8, 16)):
                pc[:, g, :] = w / np.minimum(np.arange(16) + 1, w).astype(f32)
        m["pcorr"] = pc
        if do_sample:
            sl = slice(16 * c, 16 * c + NS)
            m["x_smp"] = np.ascontiguousarray(inp["x_sample"][sl, 0, :])
            m["ptT"] = np.ascontiguousarray(inp["page_table"][sl].T.astype(np.int32))
            m["state_pool"] = np.ascontiguousarray(inp["state_pool"][0, sl])
            cos, sin = _rope_tables(np.array([PAST]))
            m["cs_smp"] = np.ascontiguousarray(np.broadcast_to(np.concatenate([cos, cos, -sin, sin], axis=1)[None], (128, 1, 64)))
            cq = np.zeros((128, 2, NS), f32)
            cq[64:96, 0, :] = np.concatenate([cos, cos], axis=1).T
            cq[64:96, 1, :] = np.concatenate([-sin, sin], axis=1).T
            m["csq_smp"] = cq
            m["ciota"] = np.ascontiguousarray(np.broadcast_to(np.arange(16, dtype=np.int32)[None], (128, 16)))
        in_maps.append(m)
    return in_maps


_NC_CACHE = {}


def kernel(**inputs):
    inp = {k: np.asarray(v) for k, v in inputs.items()}
    nphys = inp["cache_kv_latent"].shape[1]
    key = ("full", nphys)
    if key not in _NC_CACHE:
        _NC_CACHE[key] = build(NT=8, NS=16, NPHYS=nphys, do_sample=True)
    nc = _NC_CACHE[key]
    in_maps = make_in_maps(inp, do_sample=True)
    res = run_bass_kernel_spmd(nc, in_maps, core_ids=list(range(8)))
    return assemble(res.results, inp)


def assemble(results, inp, NT=8):
    f32 = np.float32
    yp = np.zeros((4, 16, T, D), f32)
    lat = np.zeros((1, 4, 16, T, KVR), f32)
    kr = np.zeros((1, 4, 16, T, DR), f32)
    pool_p = np.zeros((1, 4, 15, DP), f32)
    ys = np.zeros((128, 1, D), f32)
    lat_s = np.zeros((1, 128, 1, KVR), f32)
    kr_s = np.zeros((1, 128, 1, DR), f32)
    pool_s = np.zeros((1, 128, 15, DP), f32)
    for c, r in enumerate(results):
        b, half = c // 2, c % 2
        for i in range(NT):
            t = 2 * i + half
            yp[b, t] = r["y_own"][i * T:(i + 1) * T]
            lat[0, b, t] = r["lat_own"][i * T:(i + 1) * T]
            kr[0, b, t] = r["kr_own"][i * T:(i + 1) * T]
        if half == 1:
            pool_p[0, b] = r["pool_own"][1:16]
        sl = slice(16 * c, 16 * c + 16)
        ys[sl, 0] = r["y_smp"]
        lat_s[0, sl, 0] = r["lat_smp"]
        kr_s[0, sl, 0] = r["kr_smp"]
        pool_s[0, sl] = r["pool_smp"]
    return (yp.reshape(4, 16 * T, D), ys, lat.reshape(1, 4, 16 * T, KVR), kr.reshape(1, 4, 16 * T, DR), pool_p,
            lat_s, kr_s, pool_s)
```

```python
import contextlib
import numpy as np
import concourse.bass as bass
import concourse.mybir as mybir
from concourse.bass_utils import run_bass_kernel_spmd

F32 = mybir.dt.float32
BF16 = mybir.dt.bfloat16
I32 = mybir.dt.int32
ALU = mybir.AluOpType
AF = mybir.ActivationFunctionType
AX = mybir.AxisListType

D = 1024
T = 512
H = 8
QR = 384
KVR = 256
DR = 32
DP = 512
DFF = 2816
NCH_FF = DFF // 128
PAST = 16384
PAGE = 128
NPAGES = 128
EPS = 1e-6
SM = 96.0 ** -0.5
NEG = -30000.0


class Buf:
    __slots__ = ("name", "w", "rs", "dsem", "dcnt")

    def __init__(self, name):
        self.name = name
        self.w = None
        self.rs = []
        self.dsem = None
        self.dcnt = 0


class Sched:
    def __init__(self, nc, same_engine_sync=True):
        self.nc = nc
        self.engs = {"pe": nc.tensor, "act": nc.scalar, "dve": nc.vector, "pool": nc.gpsimd, "sp": nc.sync}
        self.sems = {}
        self.cnt = {}
        self.known = {k: {} for k in self.engs}
        self.same = same_engine_sync
        self._ctx = []
        for k in ("pe", "act", "dve", "pool"):
            cm = nc.semaphore("sem_" + k)
            self.sems[k] = cm.__enter__()
            self._ctx.append(cm)
            self.cnt[k] = 0
        self.dsems = {}
        self.nsem = 4
        self.nops = 0

    def _dsem(self, buf):
        if buf.dsem is None:
            cm = self.nc.semaphore("dsem_%d" % self.nsem)
            buf.dsem = cm.__enter__()
            self._ctx.append(cm)
            self.nsem += 1
            self.dsems[buf.name] = buf
        return buf.dsem

    def _wait(self, ek, key, sem, val):
        kn = self.known[ek]
        if kn.get(key, 0) >= val:
            return
        kn[key] = val
        self.engs[ek].wait_ge(sem, val)

    def _deps(self, ek, reads, writes):
        deps = {}
        own_raw = 0

        def add(tok, raw):
            nonlocal own_raw
            if tok is None:
                return
            key, sem, val = tok
            if key == ek:
                if raw and val > own_raw:
                    own_raw = val
                return
            if key not in deps or deps[key][1] < val:
                deps[key] = (sem, val)

        for b in reads:
            add(b.w, True)
        for b in writes:
            add(b.w, False)
            for r in b.rs:
                add(r, False)
        for key, (sem, val) in deps.items():
            self._wait(ek, key, sem, val)
        if own_raw and self.same and ek != "pe" and ek in self.sems:
            self._wait(ek, ek, self.sems[ek], own_raw)

    def op(self, ek, fn, reads=(), writes=(), signal=True):
        self._deps(ek, reads, writes)
        ins = fn(self.engs[ek])
        self.nops += 1
        if signal:
            ins.then_inc(self.sems[ek], 1)
            self.cnt[ek] += 1
            val = self.cnt[ek]
        else:
            val = self.cnt[ek] + 1
        tok = (ek, self.sems[ek], val)
        for b in reads:
            b.rs.append(tok)
            if len(b.rs) > 64:
                b.rs = _compact(b.rs)
        for b in writes:
            b.w = tok
            b.rs = []
        return ins

    def dma(self, qk, fn, reads=(), writes=(), tag=None):
        self._deps(qk, reads, writes)
        tb = tag if tag is not None else (writes[0] if writes else reads[0])
        sem = self._dsem(tb)
        ins = fn(self.engs[qk])
        self.nops += 1
        ins.then_inc(sem, 16)
        tb.dcnt += 16
        tok = ("d:" + tb.name, sem, tb.dcnt)
        for b in reads:
            b.rs.append(tok)
            if len(b.rs) > 64:
                b.rs = _compact(b.rs)
        for b in writes:
            b.w = tok
            b.rs = []
        return ins

    def inherit(self, dst, src):
        toks = []
        for b in src:
            if b.w is not None:
                toks.append(b.w)
            toks.extend(b.rs)
        toks = _compact(toks)
        for d in dst:
            d.rs = _compact(d.rs + toks)

    def barrier(self):
        for ek in ("pe", "act", "dve", "pool", "sp"):
            for k in ("pe", "act", "dve", "pool"):
                if k != ek and self.cnt[k] > 0:
                    self._wait(ek, k, self.sems[k], self.cnt[k])
            for name, b in self.dsems.items():
                if b.dcnt > 0:
                    self._wait(ek, "d:" + name, b.dsem, b.dcnt)

    def close(self):
        for cm in reversed(self._ctx):
            cm.__exit__(None, None, None)


def _compact(rs):
    best = {}
    for key, sem, val in rs:
        if key not in best or best[key][2] < val:
            best[key] = (key, sem, val)
    return list(best.values())


def build(NT=8, NS=16, NPHYS=20480, do_sample=True, same_sync=True, smp_stop=99, sub=9):
    nc = bass.Bass("TRN2", target_bir_lowering=False)
    S = Sched(nc, same_engine_sync=same_sync)
    es = contextlib.ExitStack()

    def din(name, shape, dtp=F32):
        return nc.dram_tensor(name, list(shape), dtp, kind="ExternalInput").ap()

    def dout(name, shape, dtp=F32):
        return nc.dram_tensor(name, list(shape), dtp, kind="ExternalOutput").ap()

    sb_sizes = {}

    def sb(name, shape, dtp, stack=None):
        sb_sizes[name] = int(np.prod(shape[1:])) * (4 if dtp in (F32, I32) else 2)
        try:
            return (stack or es).enter_context(nc.sbuf_tensor("s_" + name, list(shape), dtp))
        except AssertionError:
            tot = 0
            for k, v in sb_sizes.items():
                tot += v
                print("SBUF %-10s %7d  cum %7d" % (k, v, tot))
            raise

    x_own = din("x_own", [NT * T, D])
    x_oth = din("x_oth", [NT * T, D])
    cT_d = din("cT", [128, 8, 17])
    w_ada = din("w_ada", [D, 6 * D])
    b_gt_d = din("b_gt", [16, 2, D])
    vecs_d = din("vecs", [128, 80])
    rowv_d = din("rowv", [128, 288])
    cbf_d = din("cbf", [128, 5, 128])
    identf_d = din("identf", [128, 128])
    w_in_kv_d = din("w_in_kv", [D, 288])
    w_in_q_d = din("w_in_q", [D, QR])
    w_in_u_d = din("w_in_u", [D, DP])
    w_in_g_d = din("w_in_g", [D, 2 * D])
    w_uq_d = din("w_uq2", [QR, H, 192])
    w_uk_d = din("w_uk", [KVR, 512])
    w_uv_d = din("w_uv", [KVR, 512])
    w_ao_d = din("w_ao", [512, D])
    w_pool_d = din("w_pool", [4, 128, 256])
    w_out_d = din("w_out", [D, D])
    w_gu_d = din("w_gu", [D, 2 * DFF])
    w_down_d = din("w_down", [DFF, D])
    cs_own_d = din("cs_own", [NT, 128, 4, 64])
    cs_oth_d = din("cs_oth", [NT, 128, 4, 64])
    csq_own_d = din("csq_own", [NT, 128, 2, T])
    pcorr_d = din("pcorr", [128, 4, 16])

    y_own = dout("y_own", [NT * T, D])
    lat_own = dout("lat_own", [NT * T, KVR])
    kr_own = dout("kr_own", [NT * T, DR])
    pool_own = dout("pool_own", [16, DP])

    kt_s = nc.dram_tensor("kt_s", [2 * NT, 96, H, T], BF16, kind="Internal").ap()
    v_s = nc.dram_tensor("v_s", [2 * NT, 128, 4, H, 128], BF16, kind="Internal").ap()
    kt_buf = [Buf("kts%d" % i) for i in range(2 * NT)]
    v_buf = [Buf("vs%d" % i) for i in range(2 * NT)]

    if do_sample:
        x_smp = din("x_smp", [NS, D])
        cache_lat = din("cache_lat", [NPHYS * 16, 2048])
        cache_kr = din("cache_kr", [NPHYS * 2, 2048])
        ptT_d = din("ptT", [128, NS], I32)
        state_d = din("state_pool", [NS, 15, DP])
        w_ukT_d = din("w_ukT", [64, H, KVR])
        cs_smp_d = din("cs_smp", [128, 1, 64])
        csq_smp_d = din("csq_smp", [128, 2, NS])
        ciota_d = din("ciota", [128, 16], I32)
        y_smp = dout("y_smp", [NS, D])
        lat_smp = dout("lat_smp", [NS, KVR])
        kr_smp = dout("kr_smp", [NS, DR])
        pool_smp = dout("pool_smp", [NS, 15, DP])

    cbf = sb("cbf", [128, 5, 128], BF16)
    identb = cbf[:, 0, :]
    blk96 = cbf[:, 1, :]
    blk64 = cbf[:, 2, :]
    tri = cbf[:, 3, :]
    onesb = cbf[:, 4, :]
    identf = sb("identf", [128, 128], F32)
    vecs = sb("vecs", [128, 80], F32)
    rowv = sb("rowv", [128, 288], F32)
    modF = sb("modF", [128, 32, 17], F32)
    gt_bc = sb("gt_bc", [128, 2, D], F32)
    scT = sb("scT", [128, 8, 17], BF16)
    w_kv = sb("w_kv", [128, 8, 288], BF16)
    w_u = sb("w_u", [128, 8, DP], BF16)
    w_uq = sb("w_uq", [128, 3, H, 192], BF16)
    w_uk = sb("w_uk", [128, 2, 512], BF16)
    w_uv = sb("w_uv", [128, 2, 512], BF16)
    w_pool = sb("w_pool", [128, 4, 256], BF16)
    w_ao_sb = sb("w_ao_sb", [128, 4, D], BF16)
    b_const = Buf("const")

    V_B = 0
    V_G1 = 32
    V_G2 = 40
    V_SP = 48
    V_GQL = 56
    V_GQ96 = 59
    V_GQ96P = 60
    V_INV96 = 61
    V_GKN2 = 62
    V_FLAGNEG = 63
    V_FLAG = 64
    V_NFLAG = 65
    V_GQK = 66

    NWB = 3
    WELEMS = 4096
    wbuf = [sb("wbuf%d" % i, [128, WELEMS], BF16) for i in range(NWB)]
    wbufB = [Buf("wbuf%d" % i) for i in range(NWB)]

    NXT = 4
    xn = [sb("xn%d" % i, [128, D], BF16) for i in range(2)]
    xnB = [Buf("xn%d" % i) for i in range(2)]
    junk = sb("junk", [128, D], BF16)
    junkB = Buf("junk")
    st = sb("st", [128, 16], F32)
    stB = Buf("st")
    ckv_f = [sb("ckv_f%d" % i, [128, 288], F32) for i in range(2)]
    ckv_fB = [Buf("ckv_f%d" % i) for i in range(2)]
    ckv_b = [sb("ckv_b%d" % i, [128, 288], BF16) for i in range(2)]
    ckv_bB = [Buf("ckv_b%d" % i) for i in range(2)]
    rtmp = sb("rtmp", [128, 64], F32)
    rtmpB = Buf("rtmp")
    epsc = sb("epsc", [128, 1], F32)
    hTB, actTB, ckvnTB, kpeTB = Buf("hT"), Buf("actT"), Buf("ckvnT"), Buf("kpeT")
    sqbB = [Buf("sqb%d" % i) for i in range(3)]
    tfB = [Buf("tf%d" % i) for i in range(4)]
    lnvB = [tfB[0], tfB[1]]
    qlnTB, QTB, uTB, dTB, oTB, mTB = Buf("qlnT"), Buf("QT"), Buf("uT"), Buf("dT"), Buf("oT"), Buf("mT")
    KTstB = QTB
    rt1B, rt2B, pt1B, pt2B = tfB[2], tfB[3], tfB[2], tfB[3]
    VstB = mTB
    sigB = [tfB[0], tfB[1]]
    gtmpB = [tfB[2], tfB[3]]

    esP = contextlib.ExitStack()

    def alloc_tile_bufs(TT, stack):
        nonlocal hT, actT, ckvnT, kpeT, sqb, tf, lnv, qlnT, QT, rt1, rt2, uT, pt1, pt2, dT, oT, mT, sig, gtmp
        sfx = "_%d" % TT
        hT = sb("hT" + sfx, [128, 8, TT], BF16, stack)
        actT = sb("actT" + sfx, [128, NCH_FF, TT], BF16, stack)
        ckvnT = sb("ckvnT" + sfx, [128, 2, TT], BF16, stack)
        kpeT = sb("kpeT" + sfx, [32, TT], BF16, stack)
        sqb = [sb("sqb%d" % i + sfx, [128, T], BF16, stack) for i in range(3)]
        tf = [sb("tf%d" % i + sfx, [128, 16 + T], F32, stack) for i in range(4)]
        lnv = [tf[0], tf[1]]
        qlnT = sb("qlnT" + sfx, [128, 3, TT], BF16, stack)
        QT = sb("QT" + sfx, [96, H, TT], BF16, stack)
        rt1, rt2, pt1, pt2 = tf[2], tf[3], tf[2], tf[3]
        uT = sb("uT" + sfx, [128, 4, 16 + TT], F32, stack)
        dT = sb("dT" + sfx, [128, 4, TT], BF16, stack)
        oT = sb("oT" + sfx, [128, 4, TT], BF16, stack)
        mT = sb("mT" + sfx, [128, 8, TT], BF16, stack)
        sig = [tf[0], tf[1]]
        gtmp = [tf[2], tf[3]]

    hT = actT = ckvnT = kpeT = sqb = tf = lnv = qlnT = QT = rt1 = rt2 = uT = pt1 = pt2 = dT = oT = mT = sig = gtmp = None
    alloc_tile_bufs(T, esP)
    hTo = actT[:, 0:8, :]
    KTst = QT
    Vst = mT[:].rearrange("p a b -> p (a b)").rearrange("p (k h d) -> p k h d", k=4, h=H)
    KTb = [actT[0:96, 8 + 4 * i:12 + 4 * i, :] for i in range(2)]
    KTbB = [Buf("KTb%d" % i) for i in range(2)]
    NPT = 8
    PT = [actT[:, 16 + i, :] for i in range(6)] + [actT[:, 6, :], actT[:, 7, :]]
    PTB = [Buf("PT%d" % i) for i in range(NPT)]
    rden = actT[:, 4:6, :].rearrange("p a b -> p (a b)").bitcast(F32)
    rdenB = Buf("rden")
    rden2 = [rden, actT[:, 2:4, :].rearrange("p a b -> p (a b)").bitcast(F32)]
    rden2B = [rdenB, Buf("rden_b")]
    attn_group = KTbB + PTB + rden2B

    NPS = 8
    ps = [es.enter_context(nc.psum_tensor("ps%d" % i, [128, 512], F32)) for i in range(NPS)]
    psB = [Buf("ps%d" % i) for i in range(NPS)]
    ps_free = list(range(NPS))

    def ps_alloc():
        return ps_free.pop(0)

    def ps_release(i):
        ps_free.append(i)

    def mm(pi, out_ap, pairs, reads, last_signal=True, start=True):
        n = len(pairs)
        for i, (l, r) in enumerate(pairs):
            S.op("pe", lambda e: e.matmul(out_ap, lhsT=l, rhs=r, start=(start and i == 0), stop=(i == n - 1)),
                 reads=reads, writes=[psB[pi]], signal=(last_signal and i == n - 1))

    pieces = []
    wstate = {"issued": 0, "next": 0}

    def add_piece(parts):
        pieces.append(parts)

    def _issue(n):
        b = n % NWB
        for dst_fn, src in pieces[n]:
            S.dma("pool", lambda e: e.dma_start(out=dst_fn(wbuf[b]), in_=src), writes=[wbufB[b]])

    def wget():
        n = wstate["next"]
        wstate["next"] += 1
        lim = min(len(pieces), n + NWB)
        while wstate["issued"] < lim:
            _issue(wstate["issued"])
            wstate["issued"] += 1
        return wbuf[n % NWB], wbufB[n % NWB]

    def kview(k, n, off=0):
        return lambda wb: wb[:, off:off + k * n].rearrange("p (k n) -> p k n", k=k)

    def kmajor(src, c0, c1):
        return src[:, c0:c1].rearrange("(k p) n -> p k n", p=128)

    tiles = ["own%d" % i for i in range(NT)] + (["smp"] if do_sample else [])
    for j in range(12):
        add_piece([(kview(8, 512), kmajor(w_ada, j * 512, (j + 1) * 512))])
    for tname in tiles:
        if tname == "smp":
            for j in (4, 5, 10, 11):
                add_piece([(kview(8, 512), kmajor(w_ada, j * 512, (j + 1) * 512))])
        add_piece([(kview(8, QR), kmajor(w_in_q_d, 0, QR))])
        for j in range(4):
            add_piece([(kview(8, 512), kmajor(w_in_g_d, j * 512, (j + 1) * 512))])
        for j in range(2):
            add_piece([(kview(8, 512), kmajor(w_out_d, j * 512, (j + 1) * 512))])
        for j in range(11):
            add_piece([(kview(8, 256), kmajor(w_gu_d, j * 256, (j + 1) * 256)),
                       (kview(8, 256, 2048), kmajor(w_gu_d, DFF + j * 256, DFF + (j + 1) * 256))])
        for j in range(4):
            for kh in range(2):
                add_piece([(kview(11, 256), w_down_d[kh * 1408:(kh + 1) * 1408, j * 256:(j + 1) * 256].rearrange("(k p) n -> p k n", p=128))])

    def cload(q, dst, src):
        S.dma(q, lambda e: e.dma_start(out=dst, in_=src), writes=[b_const], tag=b_const)

    cload("pool", cbf[:], cbf_d)
    cload("sp", identf[:], identf_d)
    cload("sp", vecs[:], vecs_d)
    cload("sp", rowv[:], rowv_d)
    cload("pool", w_kv[:], kmajor(w_in_kv_d, 0, 288))
    cload("pool", w_u[:], kmajor(w_in_u_d, 0, DP))
    cload("pool", w_uq[:], w_uq_d.rearrange("(c p) h n -> p c h n", p=128))
    cload("pool", w_uk[:], kmajor(w_uk_d, 0, 512))
    cload("pool", w_uv[:], kmajor(w_uv_d, 0, 512))
    cload("pool", w_pool[:], w_pool_d.rearrange("g p n -> p g n"))
    cload("pool", w_ao_sb[:], w_ao_d.rearrange("(pr q) e -> q pr e", q=128))
    S.op("dve", lambda e: e.memset(uT[:], 0.0), writes=[uTB])

    def vcol(c, rows=128, r0=0):
        return vecs[r0:r0 + rows, c:c + 1]

    with contextlib.ExitStack() as es2:
        cT = es2.enter_context(nc.sbuf_tensor("s_cT", [128, 8, 17], F32))
        b_gt = actT[0:16, 0:8, :].rearrange("p a b -> p (a b)").bitcast(F32).rearrange("p (w d) -> p w d", w=2)
        gtP = actT[0:1, 8:16, :].rearrange("p a b -> p (a b)").bitcast(F32).rearrange("p (w d) -> p w d", w=2)
        onesr = es2.enter_context(nc.sbuf_tensor("s_onesr", [1, 128], F32))
        bset = Buf("setup")
        S.dma("sp", lambda e: e.dma_start(out=cT[:], in_=cT_d), writes=[bset])
        S.dma("sp", lambda e: e.dma_start(out=b_gt, in_=b_gt_d), writes=[bset])
        S.op("dve", lambda e: e.memset(onesr[:], 1.0), writes=[bset])
        S.op("act", lambda e: e.activation(out=scT[:], in_=cT[:], func=AF.Silu), reads=[bset], writes=[bset, b_const])
        FMAP = {0: 0, 1: 8, 3: 16, 4: 24}
        for j in range(12):
            wb, wbB = wget()
            wv = kview(8, 512)(wb)
            grp, hf = j // 2, j % 2
            if grp in FMAP:
                pi = ps_alloc()
                for c in range(4):
                    mm(pi, ps[pi][:, c * 32:c * 32 + 17], [(wv[:, k, c * 128:(c + 1) * 128], scT[:, k, :]) for k in range(8)],
                       [wbB, bset], last_signal=(c == 3))
                for c in range(4):
                    ch = FMAP[grp] + hf * 4 + c
                    S.op("dve", lambda e: e.tensor_scalar(out=modF[:, ch, :], in0=ps[pi][:, c * 32:c * 32 + 17],
                                                          scalar1=vcol(V_B + ch), scalar2=None, op0=ALU.add),
                         reads=[psB[pi], b_const], writes=[b_const])
                ps_release(pi)
            else:
                which = 0 if grp == 2 else 1
                pi = ps_alloc()
                pj = ps_alloc()
                mm(pi, ps[pi][0:1, :], [(scT[:, k, 0:1], wv[:, k, :]) for k in range(8)], [wbB, bset])
                cs_ = slice(hf * 512, (hf + 1) * 512)
                S.op("dve", lambda e: e.tensor_tensor(out=gtP[0:1, which, cs_], in0=ps[pi][0:1, :], in1=b_gt[0:1, which, cs_], op=ALU.add),
                     reads=[psB[pi], bset], writes=[bset])

                mm(pi, ps[pi][:, :], [(onesr[0:1, :], gtP[0:1, which, cs_])], [bset])
                S.op("act", lambda e: e.copy(out=gt_bc[:, which, cs_], in_=ps[pi][:, :]), reads=[psB[pi]], writes=[b_const])
                ps_release(pi)
                ps_release(pj)
        for k in range(8):
            S.op("dve", lambda e: e.tensor_scalar(out=modF[:, 8 + k, :], in0=modF[:, 8 + k, :], scalar1=1.0, scalar2=vcol(V_G1 + k),
                                                  op0=ALU.add, op1=ALU.mult), reads=[b_const], writes=[b_const])
            S.op("dve", lambda e: e.tensor_scalar(out=modF[:, 24 + k, :], in0=modF[:, 24 + k, :], scalar1=1.0, scalar2=vcol(V_G2 + k),
                                                  op0=ALU.add, op1=ALU.mult), reads=[b_const], writes=[b_const])
        S.barrier()

    def rstd_small(ss_ap, out_ap, inv_n):
        S.op("act", lambda e: e.activation(out=out_ap, in_=ss_ap, func=AF.Ln, bias=vcol_eps(out_ap), scale=inv_n),
             reads=[stB, b_const], writes=[stB])
        S.op("act", lambda e: e.activation(out=out_ap, in_=out_ap, func=AF.Exp, scale=-0.5), reads=[stB], writes=[stB])

    S.op("dve", lambda e: e.memset(epsc[:], EPS), writes=[b_const])

    def vcol_eps(ap):
        n = ap.shape[0]
        return epsc[0:n, :]

    junkB2 = [Buf("junk_a"), Buf("junk_b")]
    rtmp2 = [rtmp, sb("rtmp_b", [128, 64], F32, esP)]
    rtmp2B = [rtmpB, Buf("rtmp_b")]

    def front2(blocks, P, is_smp, hdst, hdstB, csB):
        nb = len(blocks)
        stv = st[0:P, :].rearrange("p (j c) -> p j c", j=2)
        for j, (xt, xtB, c0, cs_ap, o_lat, o_kr) in enumerate(blocks):
            S.op("act", lambda e: e.activation(out=xn[j][0:P, :], in_=xt[0:P, :], func=AF.Square, accum_out=st[0:P, 8 * j:8 * j + 1]),
                 reads=[xtB], writes=[xnB[j], stB])
        S.op("act", lambda e: e.activation(out=stv[:, 0:nb, 1], in_=stv[:, 0:nb, 0], func=AF.Ln, bias=epsc[0:P, :], scale=1.0 / D),
             reads=[stB, b_const], writes=[stB])
        S.op("act", lambda e: e.activation(out=stv[:, 0:nb, 1], in_=stv[:, 0:nb, 1], func=AF.Exp, scale=-0.5), reads=[stB], writes=[stB])
        for j, (xt, xtB, c0, cs_ap, o_lat, o_kr) in enumerate(blocks):
            S.op("act", lambda e: e.activation(out=xn[j][0:P, :], in_=xt[0:P, :], func=AF.Copy, scale=st[0:P, 8 * j + 1:8 * j + 2]),
                 reads=[xtB, stB], writes=[xnB[j]])
        for j, (xt, xtB, c0, cs_ap, o_lat, o_kr) in enumerate(blocks):
            pi = ps_alloc()
            pv = ps[pi][:].bitcast(BF16)
            for k in range(8):
                S.op("pe", lambda e: e.transpose(out=pv[:, k * 128:k * 128 + P], in_=xn[j][0:P, k * 128:(k + 1) * 128], identity=identb[0:P, 0:P]),
                     reads=[xnB[j], b_const], writes=[psB[pi]], signal=(k == 7))
            pv3 = pv.rearrange("p (k n) -> p k n", k=8)[:, :, 0:P]
            if not is_smp:
                a_ap = modF[:, 8:16, 0:1].to_broadcast([128, 8, P])
                s_ap = modF[:, 0:8, 0:1].to_broadcast([128, 8, P])
            else:
                a_ap = modF[:, 8:16, 1:1 + P]
                s_ap = modF[:, 0:8, 1:1 + P]
            hv = hdst[:, :, c0:c0 + P]
            S.op("dve", lambda e: e.tensor_tensor(out=hv, in0=pv3, in1=a_ap, op=ALU.mult), reads=[psB[pi], b_const], writes=[hdstB])
            S.op("dve", lambda e: e.tensor_tensor(out=hv, in0=hv, in1=s_ap, op=ALU.add), reads=[hdstB, b_const], writes=[hdstB])
            ps_release(pi)
        pcs = []
        for j, (xt, xtB, c0, cs_ap, o_lat, o_kr) in enumerate(blocks):
            pi = ps_alloc()
            pcs.append(pi)
            mm(pi, ps[pi][0:P, 0:288], [(hdst[:, k, c0:c0 + P], w_kv[:, k, :]) for k in range(8)], [hdstB, b_const])
            S.op("act", lambda e: e.activation(out=junk[0:P, j * 288:j * 288 + 256], in_=ps[pi][0:P, 0:256], func=AF.Square, accum_out=st[0:P, 8 * j + 2:8 * j + 3]),
                 reads=[psB[pi]], writes=[junkB2[j], stB])
            S.op("act", lambda e: e.activation(out=junk[0:P, j * 288 + 256:j * 288 + 288], in_=ps[pi][0:P, 256:288], func=AF.Square, accum_out=st[0:P, 8 * j + 3:8 * j + 4]),
                 reads=[psB[pi]], writes=[junkB2[j], stB])
        S.op("act", lambda e: e.activation(out=stv[:, 0:nb, 4], in_=stv[:, 0:nb, 2], func=AF.Ln, bias=epsc[0:P, :], scale=1.0 / KVR),
             reads=[stB, b_const], writes=[stB])
        S.op("act", lambda e: e.activation(out=stv[:, 0:nb, 5], in_=stv[:, 0:nb, 3], func=AF.Ln, bias=epsc[0:P, :], scale=1.0 / DR),
             reads=[stB, b_const], writes=[stB])
        S.op("act", lambda e: e.activation(out=stv[:, 0:nb, 4:6], in_=stv[:, 0:nb, 4:6], func=AF.Exp, scale=-0.5), reads=[stB], writes=[stB])
        for j, (xt, xtB, c0, cs_ap, o_lat, o_kr) in enumerate(blocks):
            pi = pcs[j]
            cf, cfB, cb, cbB = ckv_f[j], ckv_fB[j], ckv_b[j], ckv_bB[j]
            rt, rtB = rtmp2[j], rtmp2B[j]
            smp_ci["i"] = j
            S.op("dve", lambda e: e.scalar_tensor_tensor(out=cf[0:P, 0:256], in0=ps[pi][0:P, 0:256], scalar=st[0:P, 8 * j + 4:8 * j + 5], in1=rowv[0:P, 0:256],
                                                         op0=ALU.mult, op1=ALU.mult), reads=[psB[pi], stB, b_const], writes=[cfB])
            S.op("dve", lambda e: e.scalar_tensor_tensor(out=rt[0:P, 0:32], in0=ps[pi][0:P, 256:288], scalar=st[0:P, 8 * j + 5:8 * j + 6], in1=rowv[0:P, 256:288],
                                                         op0=ALU.mult, op1=ALU.mult), reads=[psB[pi], stB, b_const], writes=[rtB])
            ps_release(pi)
            S.op("pool", lambda e: e.tensor_tensor(out=rt[0:P, 32:64], in0=rt[0:P, 0:32], in1=cs_ap[0:P, 0:32], op=ALU.mult),
                 reads=[rtB, csB], writes=[rtB])
            S.op("pool", lambda e: e.tensor_tensor(out=cf[0:P, 256:272], in0=rt[0:P, 16:32], in1=cs_ap[0:P, 32:48], op=ALU.mult),
                 reads=[rtB, csB], writes=[cfB])
            S.op("pool", lambda e: e.tensor_tensor(out=cf[0:P, 272:288], in0=rt[0:P, 0:16], in1=cs_ap[0:P, 48:64], op=ALU.mult),
                 reads=[rtB, csB], writes=[cfB])
            S.op("pool", lambda e: e.tensor_tensor(out=cf[0:P, 256:288], in0=cf[0:P, 256:288], in1=rt[0:P, 32:64], op=ALU.add),
                 reads=[rtB, cfB], writes=[cfB])
            S.op("pool", lambda e: e.tensor_copy(out=cb[0:P, :], in_=cf[0:P, :]), reads=[cfB], writes=[cbB])
            if o_lat is not None:
                S.dma("sp", lambda e: e.dma_start(out=o_lat, in_=cf[0:P, 0:256]), reads=[cfB], tag=cfB)
                S.dma("sp", lambda e: e.dma_start(out=o_kr, in_=cf[0:P, 256:288]), reads=[cfB], tag=cfB)
        for j, (xt, xtB, c0, cs_ap, o_lat, o_kr) in enumerate(blocks):
            cb, cbB = ckv_b[j], ckv_bB[j]
            pi = ps_alloc()
            pv = ps[pi][:].bitcast(BF16)
            for rc in range(2):
                S.op("pe", lambda e: e.transpose(out=pv[:, rc * 128:rc * 128 + P], in_=cb[0:P, rc * 128:(rc + 1) * 128], identity=identb[0:P, 0:P]),
                     reads=[cbB, b_const], writes=[psB[pi]], signal=False)
            S.op("pe", lambda e: e.transpose(out=pv[0:32, 256:256 + P], in_=cb[0:P, 256:288], identity=identb[0:P, 0:P]),
                 reads=[cbB, b_const], writes=[psB[pi]])
            S.op("act", lambda e: e.copy(out=ckvnT[:, :, c0:c0 + P], in_=pv[:, 0:256].rearrange("p (k n) -> p k n", k=2)[:, :, 0:P]),
                 reads=[psB[pi]], writes=[ckvnTB])
            S.op("act", lambda e: e.copy(out=kpeT[0:32, c0:c0 + P], in_=pv[0:32, 256:256 + P]), reads=[psB[pi]], writes=[kpeTB])
            ps_release(pi)

    class _N:
        n = 0
    front = _N
    front.n = 0

    def rstd_big(pi, rows, scale, li):
        S.op("act", lambda e: e.activation(out=lnv[li][0:rows, :], in_=ps[pi][0:rows, :], func=AF.Ln, bias=epsc[0:rows, :], scale=scale),
             reads=[psB[pi], b_const], writes=[lnvB[li]])
        S.op("act", lambda e: e.activation(out=lnv[li][0:rows, :], in_=lnv[li][0:rows, :], func=AF.Exp, scale=-0.5),
             reads=[lnvB[li]], writes=[lnvB[li]])

    def kv_build(slot, n):
        for pr in range(4):
            pk = ps_alloc()
            mm(pk, ps[pk][:, 0:n], [(w_uk[:, rc, pr * 128:(pr + 1) * 128], ckvnT[:, rc, 0:n]) for rc in range(2)], [ckvnTB, b_const])
            si = pr % 3
            S.op("act", lambda e: e.activation(out=sqb[si][:, 0:n], in_=ps[pk][:, 0:n], func=AF.Square), reads=[psB[pk]], writes=[sqbB[si]])
            pss = ps_alloc()
            mm(pss, ps[pss][:, 0:n], [(blk64, sqb[si][:, 0:n])], [sqbB[si], b_const])
            li = pr % 2
            rstd_big_n(pss, 128, 1.0 / 64, li, n)
            ps_release(pss)
            for hh in range(2):
                r0 = hh * 64
                S.op("dve", lambda e: e.scalar_tensor_tensor(out=KTst[0:64, 2 * pr + hh, 0:n], in0=ps[pk][r0:r0 + 64, 0:n], scalar=vcol(V_GKN2, 64, r0),
                                                             in1=lnv[li][r0:r0 + 64, 0:n], op0=ALU.mult, op1=ALU.mult),
                     reads=[psB[pk], lnvB[li], b_const], writes=[KTstB])
            ps_release(pk)
        S.op("pool", lambda e: e.tensor_copy(out=KTst[64:96, :, 0:n], in_=kpeT[0:32, 0:n].unsqueeze(1).to_broadcast([32, H, n])),
             reads=[kpeTB], writes=[KTstB])
        S.dma("sp", lambda e: e.dma_start(out=kt_s[slot], in_=KTst[:]), reads=[KTstB], writes=[kt_buf[slot]], tag=kt_buf[slot])
        S.op("pool", lambda e: e.memset(Vst[:, :, :, 64:128], 1.0), writes=[VstB])
        for ks in range(n // 128):
            pvv = ps_alloc()
            mm(pvv, ps[pvv][:, :], [(ckvnT[:, rc, ks * 128:(ks + 1) * 128], w_uv[:, rc, :]) for rc in range(2)], [ckvnTB, b_const])
            S.op("act", lambda e: e.copy(out=Vst[:, ks, :, 0:64], in_=ps[pvv][:, :].rearrange("p (h d) -> p h d", h=H)),
                 reads=[psB[pvv]], writes=[VstB])
            ps_release(pvv)
        S.dma("sp", lambda e: e.dma_start(out=v_s[slot], in_=Vst), reads=[VstB], writes=[v_buf[slot]], tag=v_buf[slot])

    def q_build(n, csq_ap, csqBuf, hsrc, hsrcB):
        wb, wbB = wget()
        wq = kview(8, QR)(wb)
        pq = [ps_alloc() for _ in range(3)]
        for c in range(3):
            mm(pq[c], ps[pq[c]][:, 0:n], [(wq[:, k, c * 128:(c + 1) * 128], hsrc[:, k, 0:n]) for k in range(8)], [wbB, hsrcB])
            S.op("act", lambda e: e.activation(out=sqb[c][:, 0:n], in_=ps[pq[c]][:, 0:n], func=AF.Square), reads=[psB[pq[c]]], writes=[sqbB[c]])
        pss = ps_alloc()
        mm(pss, ps[pss][:, 0:n], [(onesb, sqb[c][:, 0:n]) for c in range(3)], sqbB + [b_const])
        rstd_big_n(pss, 128, 1.0 / QR, 0, n)
        ps_release(pss)
        for c in range(3):
            S.op("dve", lambda e: e.scalar_tensor_tensor(out=qlnT[:, c, 0:n], in0=ps[pq[c]][:, 0:n], scalar=vcol(V_GQL + c), in1=lnv[0][:, 0:n],
                                                         op0=ALU.mult, op1=ALU.mult), reads=[psB[pq[c]], lnvB[0], b_const], writes=[qlnTB])
            ps_release(pq[c])
        for h in range(H):
            pa = ps_alloc()
            pb = ps_alloc()
            mm(pa, ps[pa][0:96, 0:n], [(w_uq[:, c, h, 0:96], qlnT[:, c, 0:n]) for c in range(3)], [qlnTB, b_const])
            mm(pb, ps[pb][0:96, 0:n], [(w_uq[:, c, h, 96:192], qlnT[:, c, 0:n]) for c in range(3)], [qlnTB, b_const])
            si = h % 3
            S.op("act", lambda e: e.activation(out=sqb[si][0:96, 0:n], in_=ps[pa][0:96, 0:n], func=AF.Square), reads=[psB[pa]], writes=[sqbB[si]])
            pss = ps_alloc()
            mm(pss, ps[pss][0:96, 0:n], [(blk96[0:96, 0:96], sqb[si][0:96, 0:n])], [sqbB[si], b_const])
            li = h % 2
            rstd_big_n(pss, 96, vcol(V_INV96, 96), li, n)
            ps_release(pss)
            S.op("dve", lambda e: e.scalar_tensor_tensor(out=QT[0:64, h, 0:n], in0=ps[pa][0:64, 0:n], scalar=vcol(V_GQ96, 64), in1=lnv[li][0:64, 0:n],
                                                         op0=ALU.mult, op1=ALU.mult), reads=[psB[pa], lnvB[li], b_const], writes=[QTB])
            S.op("dve", lambda e: e.scalar_tensor_tensor(out=rt1[64:96, 0:n], in0=ps[pa][64:96, 0:n], scalar=vcol(V_GQ96, 32, 64), in1=csq_ap[64:96, 0, 0:n],
                                                         op0=ALU.mult, op1=ALU.mult), reads=[psB[pa], csqBuf, b_const], writes=[rt1B])
            S.op("dve", lambda e: e.scalar_tensor_tensor(out=rt2[64:96, 0:n], in0=ps[pb][64:96, 0:n], scalar=vcol(V_GQ96P, 32, 64), in1=csq_ap[64:96, 1, 0:n],
                                                         op0=ALU.mult, op1=ALU.mult), reads=[psB[pb], csqBuf, b_const], writes=[rt2B])
            S.op("pool", lambda e: e.tensor_tensor(out=rt1[64:96, 0:n], in0=rt1[64:96, 0:n], in1=rt2[64:96, 0:n], op=ALU.add),
                 reads=[rt1B, rt2B], writes=[rt1B])
            S.op("pool", lambda e: e.tensor_tensor(out=QT[64:96, h, 0:n], in0=rt1[64:96, 0:n], in1=lnv[li][64:96, 0:n], op=ALU.mult),
                 reads=[rt1B, lnvB[li]], writes=[QTB])
            ps_release(pa)
            ps_release(pb)

    def rstd_big_n(pi, rows, scale, li, n):
        S.op("act", lambda e: e.activation(out=lnv[li][0:rows, 0:n], in_=ps[pi][0:rows, 0:n], func=AF.Ln, bias=epsc[0:rows, :], scale=scale),
             reads=[psB[pi], b_const], writes=[lnvB[li]])
        S.op("act", lambda e: e.activation(out=lnv[li][0:rows, 0:n], in_=lnv[li][0:rows, 0:n], func=AF.Exp, scale=-0.5),
             reads=[lnvB[li]], writes=[lnvB[li]])

    def attention(i):
        nslots = 2 * i + 2
        S.inherit(attn_group, [actTB])
        for hg in range(2):
            acc = [ps_alloc() for _ in range(4)]
            bufof = {}

            def load(s):
                bi = attention.n % 2
                attention.n += 1
                S.dma("sp", lambda e: e.dma_start(out=KTb[bi], in_=kt_s[s][:, hg * 4:(hg + 1) * 4, :]), reads=[kt_buf[s]], writes=[KTbB[bi]])
                S.dma("sp", lambda e: e.dma_start(out=Vb[bi][:], in_=v_s[s][:, :, hg * 4:(hg + 1) * 4, :]), reads=[v_buf[s]], writes=[VbB[bi]])
                bufof[s] = bi

            steps = [(s, ks) for s in range(nslots) for ks in range(4)]
            pts = {}

            def qk_exp(n, hh):
                s, ks = steps[n]
                bi = bufof[s]
                own = (s == 2 * i)
                oth = (s == 2 * i + 1)
                q0 = ks * 128 if own else 0
                h = hg * 4 + hh
                pi = ps_alloc()
                mm(pi, ps[pi][:, q0:T], [(KTb[bi][0:96, hh, ks * 128:(ks + 1) * 128], QT[0:96, h, q0:T])], [KTbB[bi], QTB])
                pj = attention.p % NPT
                attention.p += 1
                if oth:
                    S.op("act", lambda e: e.activation(out=PT[pj][:, q0:T], in_=ps[pi][:, q0:T], func=AF.Exp, scale=SM, bias=vcol(V_FLAGNEG)),
                         reads=[psB[pi], b_const], writes=[PTB[pj]])
                else:
                    S.op("act", lambda e: e.activation(out=PT[pj][:, q0:T], in_=ps[pi][:, q0:T], func=AF.Exp, scale=SM),
                         reads=[psB[pi]], writes=[PTB[pj]])
                ps_release(pi)
                if own:
                    S.op("pool", lambda e: e.tensor_tensor(out=PT[pj][:, q0:q0 + 128], in0=PT[pj][:, q0:q0 + 128], in1=tri, op=ALU.mult),
                         reads=[PTB[pj], b_const], writes=[PTB[pj]])
                pts[(n, hh)] = (pj, q0)

            load(0)
            for hh in range(4):
                qk_exp(0, hh)
            for n, (s, ks) in enumerate(steps):
                if ks == 0 and s + 1 < nslots:
                    load(s + 1)
                bi = bufof[s]
                last = (n == len(steps) - 1)
                for hh in range(4):
                    if n + 1 < len(steps):
                        qk_exp(n + 1, hh)
                    pj, q0 = pts.pop((n, hh))
                    S.op("pe", lambda e: e.matmul(ps[acc[hh]][:, q0:T], lhsT=Vb[bi][:, ks, hh, :], rhs=PT[pj][:, q0:T], start=(n == 0), stop=last),
                         reads=[VbB[bi], PTB[pj]], writes=[psB[acc[hh]]], signal=True)
            for hh in range(4):
                h = hg * 4 + hh
                a = acc[hh]
                rd, rdB = rden2[hh % 2], rden2B[hh % 2]
                S.op("act", lambda e: e.activation(out=rd[64:128, :], in_=ps[a][64:128, :], func=AF.Ln), reads=[psB[a]], writes=[rdB])
                S.op("act", lambda e: e.activation(out=rd[64:128, :], in_=rd[64:128, :], func=AF.Exp, scale=-1.0), reads=[rdB], writes=[rdB])
                r0 = (h % 2) * 64
                S.op("dve", lambda e: e.tensor_tensor(out=oT[r0:r0 + 64, h // 2, :], in0=ps[a][0:64, :], in1=rd[64:128, :], op=ALU.mult),
                     reads=[psB[a], rdB], writes=[oTB])
                ps_release(a)

    attention.n = 0
    attention.p = 0

    def pool_branch(n, first_tile):
        for g in range(4):
            w = 2 << g
            cur, curB = uT[:, g, :], uTB
            lo = 16 - (w - 1)
            width = 1
            bufs = [(pt1, pt1B), (pt2, pt2B)]
            bi = 0
            while width < w:
                lo2 = lo + width
                dst, dstB = bufs[bi]
                S.op("pool", lambda e: e.tensor_tensor(out=dst[:, lo2:16 + n], in0=cur[:, lo2:16 + n], in1=cur[:, lo2 - width:16 + n - width], op=ALU.add),
                     reads=[curB], writes=[dstB])
                cur, curB = dst[:, :], dstB
                lo = lo2
                width *= 2
                bi ^= 1
            if first_tile:
                S.op("pool", lambda e: e.tensor_tensor(out=cur[:, 16:32], in0=cur[:, 16:32], in1=pcorr[:, g, :], op=ALU.mult),
                     reads=[curB, b_const], writes=[curB])
            S.op("dve", lambda e: e.scalar_tensor_tensor(out=dT[:, g, 0:n], in0=cur[:, 16:16 + n], scalar=1.0 / w, in1=uT[:, g, 16:16 + n],
                                                         op0=ALU.mult, op1=ALU.subtract), reads=[curB, uTB], writes=[dTB])

    def merge_out_ffn(n, nblk, P, xts, xtBs, gt_ap_fn, a2_fn, hsrc, hsrcB, y_rows_fn):
        wao, wb_aoB = w_ao_sb, b_const
        for g in range(4):
            pi = ps_alloc()
            mm(pi, ps[pi][:, 0:n], [(w_u[:, k, g * 128:(g + 1) * 128], hsrc[:, k, 0:n]) for k in range(8)], [hsrcB, b_const])
            S.op("act", lambda e: e.copy(out=uT[:, g, 16:16 + n], in_=ps[pi][:, 0:n]), reads=[psB[pi]], writes=[uTB])
            ps_release(pi)
        merge_out_ffn.pool_hook()
        for e8 in range(8):
            if e8 % 4 == 0:
                wga, wgaB = wget()
            wv = kview(8, 512)(wga)
            c = (e8 % 4) * 128
            pa = ps_alloc()
            mm(pa, ps[pa][:, 0:n], [(wao[:, pr, e8 * 128:(e8 + 1) * 128], oT[:, pr, 0:n]) for pr in range(4)], [wb_aoB, oTB])
            pg = ps_alloc()
            mm(pg, ps[pg][:, 0:n], [(wv[:, k, c:c + 128], hsrc[:, k, 0:n]) for k in range(8)], [wgaB, hsrcB])
            si = e8 % 2
            S.op("act", lambda e: e.activation(out=sig[si][:, 0:n], in_=ps[pg][:, 0:n], func=AF.Sigmoid), reads=[psB[pg]], writes=[sigB[si]])
            ps_release(pg)
            S.op("dve", lambda e: e.tensor_tensor(out=mT[:, e8, 0:n], in0=ps[pa][:, 0:n], in1=sig[si][:, 0:n], op=ALU.mult),
                 reads=[psB[pa], sigB[si]], writes=[mTB])
            ps_release(pa)
        for e8 in range(8):
            if e8 % 4 == 0:
                wgb, wgbB = wget()
            wv = kview(8, 512)(wgb)
            c = (e8 % 4) * 128
            g, hf = e8 // 2, e8 % 2
            pb = ps_alloc()
            mm(pb, ps[pb][:, 0:n], [(w_pool[:, g, hf * 128:(hf + 1) * 128], dT[:, g, 0:n])], [dTB, b_const])
            pg = ps_alloc()
            mm(pg, ps[pg][:, 0:n], [(wv[:, k, c:c + 128], hsrc[:, k, 0:n]) for k in range(8)], [wgbB, hsrcB])
            si = e8 % 2
            S.op("act", lambda e: e.activation(out=sig[si][:, 0:n], in_=ps[pg][:, 0:n], func=AF.Sigmoid), reads=[psB[pg]], writes=[sigB[si]])
            ps_release(pg)
            S.op("dve", lambda e: e.scalar_tensor_tensor(out=gtmp[si][:, 0:n], in0=ps[pb][:, 0:n], scalar=vcol(V_SP + e8), in1=sig[si][:, 0:n],
                                                         op0=ALU.mult, op1=ALU.mult), reads=[psB[pb], sigB[si], b_const], writes=[gtmpB[si]])
            ps_release(pb)
            S.op("pool", lambda e: e.tensor_tensor(out=mT[:, e8, 0:n], in0=mT[:, e8, 0:n], in1=gtmp[si][:, 0:n], op=ALU.add),
                 reads=[gtmpB[si]], writes=[mTB])
        for hf in range(2):
            wo, woB = wget()
            wv = kview(8, 512)(wo)
            for tb in range(nblk):
                pi = ps_alloc()
                mm(pi, ps[pi][0:P, :], [(mT[:, k, tb * P:(tb + 1) * P], wv[:, k, :]) for k in range(8)], [woB, mTB])
                gi = (hf * nblk + tb) % 2
                cs_ = slice(hf * 512, (hf + 1) * 512)
                S.op("dve", lambda e: e.tensor_tensor(out=gtmp[gi][0:P, 0:512], in0=ps[pi][0:P, :], in1=gt_ap_fn(0, P)[:, cs_], op=ALU.mult),
                     reads=[psB[pi], b_const], writes=[gtmpB[gi]])
                ps_release(pi)
                S.op("pool", lambda e: e.tensor_tensor(out=xts[tb][0:P, cs_], in0=xts[tb][0:P, cs_], in1=gtmp[gi][0:P, 0:512], op=ALU.add),
                     reads=[gtmpB[gi], xtBs[tb]], writes=[xtBs[tb]])
        for tb in range(nblk):
            xt, xtB = xts[tb], xtBs[tb]
            S.op("act", lambda e: e.activation(out=junk[0:P, :], in_=xt[0:P, :], func=AF.Square, accum_out=st[0:P, 6:7]),
                 reads=[xtB], writes=[junkB, stB])
            rstd_small(st[0:P, 6:7], st[0:P, 7:8], 1.0 / D)
            xi = front.n % 2
            front.n += 1
            S.op("act", lambda e: e.activation(out=xn[xi][0:P, :], in_=xt[0:P, :], func=AF.Copy, scale=st[0:P, 7:8]),
                 reads=[xtB, stB], writes=[xnB[xi]])
            pi = ps_alloc()
            pv = ps[pi][:].bitcast(BF16)
            for k in range(8):
                S.op("pe", lambda e: e.transpose(out=pv[:, k * 128:k * 128 + P], in_=xn[xi][0:P, k * 128:(k + 1) * 128], identity=identb[0:P, 0:P]),
                     reads=[xnB[xi], b_const], writes=[psB[pi]], signal=(k == 7))
            pv3 = pv.rearrange("p (k n) -> p k n", k=8)[:, :, 0:P]
            a_ap, s_ap = a2_fn(P)
            hv = hsrc[:, :, tb * P:(tb + 1) * P]
            S.op("dve", lambda e: e.tensor_tensor(out=hv, in0=pv3, in1=a_ap, op=ALU.mult), reads=[psB[pi], b_const], writes=[hsrcB])
            S.op("dve", lambda e: e.tensor_tensor(out=hv, in0=hv, in1=s_ap, op=ALU.add), reads=[hsrcB, b_const], writes=[hsrcB])
            ps_release(pi)
        S.inherit([actTB], attn_group)
        for j in range(11):
            wg, wgB = wget()
            wgg = kview(8, 256)(wg)
            wup = kview(8, 256, 2048)(wg)
            for jj in range(2):
                ch = 2 * j + jj
                pg = ps_alloc()
                pu = ps_alloc()
                mm(pg, ps[pg][:, 0:n], [(wgg[:, k, jj * 128:(jj + 1) * 128], hsrc[:, k, 0:n]) for k in range(8)], [wgB, hsrcB])
                mm(pu, ps[pu][:, 0:n], [(wup[:, k, jj * 128:(jj + 1) * 128], hsrc[:, k, 0:n]) for k in range(8)], [wgB, hsrcB])
                si = ch % 2
                S.op("act", lambda e: e.activation(out=sig[si][:, 0:n], in_=ps[pg][:, 0:n], func=AF.Silu), reads=[psB[pg]], writes=[sigB[si]])
                ps_release(pg)
                S.op("dve", lambda e: e.tensor_tensor(out=actT[:, ch, 0:n], in0=ps[pu][:, 0:n], in1=sig[si][:, 0:n], op=ALU.mult),
                     reads=[psB[pu], sigB[si]], writes=[actTB])
                ps_release(pu)
        for qd in range(4):
            cs_ = slice(qd * 256, (qd + 1) * 256)
            pds = [ps_alloc() for _ in range(nblk)]
            for kh in range(2):
                wd, wdB = wget()
                wv = kview(11, 256)(wd)
                for tb in range(nblk):
                    for k in range(11):
                        S.op("pe", lambda e: e.matmul(ps[pds[tb]][0:P, 0:256], lhsT=actT[:, kh * 11 + k, tb * P:(tb + 1) * P], rhs=wv[:, k, :],
                                                      start=(kh == 0 and k == 0), stop=(kh == 1 and k == 10)),
                             reads=[wdB, actTB], writes=[psB[pds[tb]]], signal=(k == 10))
            for tb in range(nblk):
                pi = pds[tb]
                gi = (qd * nblk + tb) % 2
                S.op("dve", lambda e: e.tensor_tensor(out=gtmp[gi][0:P, 0:256], in0=ps[pi][0:P, 0:256], in1=gt_ap_fn(1, P)[:, cs_], op=ALU.mult),
                     reads=[psB[pi], b_const], writes=[gtmpB[gi]])
                ps_release(pi)
                S.op("pool", lambda e: e.tensor_tensor(out=xts[tb][0:P, cs_], in0=xts[tb][0:P, cs_], in1=gtmp[gi][0:P, 0:256], op=ALU.add),
                     reads=[gtmpB[gi], xtBs[tb]], writes=[xtBs[tb]])
        for tb in range(nblk):
            S.dma("sp", lambda e: e.dma_start(out=y_rows_fn(tb), in_=xts[tb][0:P, :]), reads=[xtBs[tb]], tag=xtBs[tb])


    def sample_phase():
        esS = contextlib.ExitStack()
        alloc_tile_bufs(NS, esS)
        G = 16
        NB = PAGE // G
        gtS = sb("gtS", [16, 2, D], F32, esS)
        x_s = sb("x_s", [16, D], F32, esS)
        x_sB = Buf("x_s")
        cs_s = sb("cs_s", [128, 1, 64], F32, esS)
        csq_s = sb("csq_s", [128, 2, NS], F32, esS)
        ptT = sb("ptT", [128, NS], I32, esS)
        ciota = sb("ciota", [128, 16], I32, esS)
        idx_l = sb("idx_l", [128, NS, 16], I32, esS)
        idx_k = sb("idx_k", [128, NS, 2], I32, esS)
        w_ukT = sb("w_ukT", [64, H, KVR], BF16, esS)
        qg = sb("qg", [64, H, NS], BF16, esS)
        qabs = sb("qabs", [128, 2, NS, H], BF16, esS)
        qpe = sb("qpe", [32, NS, H], BF16, esS)
        glat = [sb("glat%d" % i, [128, G, KVR], BF16, esS) for i in range(3)]
        glatB = [Buf("glat%d" % i) for i in range(3)]
        gkr = [sb("gkr%d" % i, [128, PAGE, DR], BF16, esS) for i in range(2)]
        gkrB = [Buf("gkr%d" % i) for i in range(2)]
        latT = [sb("latT%d" % i, [128, 2, 128], BF16, esS) for i in range(2)]
        latTB = [Buf("latT%d" % i) for i in range(2)]
        kT = [sb("kT%d" % i, [32, 128], BF16, esS) for i in range(2)]
        kTB = [Buf("kT%d" % i) for i in range(2)]
        ssb = [sb("ssb%d" % i, [128, G, H], F32, esS) for i in range(2)]
        ssbB = [Buf("ssb%d" % i) for i in range(2)]
        scb = [sb("scb%d" % i, [128, G, H], F32, esS) for i in range(2)]
        scbB = [Buf("scb%d" % i) for i in range(2)]
        pb = [sb("pb%d" % i, [128, G, H], BF16, esS) for i in range(2)]
        pbB = [Buf("pb%d" % i) for i in range(2)]
        nw = sb("nw", [16, 4, H], F32, esS)
        nwB = Buf("nw")
        pnew = sb("pnew", [16, H], BF16, esS)
        pnewB = Buf("pnew")
        dn = sb("dn", [8, 4, H], F32, esS)
        dnB = Buf("dn")
        oln = sb("oln", [8, KVR], BF16, esS)
        olnB = Buf("oln")
        OLT = sb("OLT", [128, 2, NS, H], BF16, esS)
        OLTB = Buf("OLT")
        prevT = sb("prevT", [128, 4, NS, 15], F32, esS)
        prevTB = Buf("prevT")
        stt = [sb("stt%d" % i, [120, DP], F32, esS) for i in range(2)]
        sttB = [Buf("stt%d" % i) for i in range(2)]
        wsum = sb("wsum", [128, 4, NS], F32, esS)
        wsumB = Buf("wsum")
        bS = Buf("smp_const")

        def sload(q, dst, src):
            S.dma(q, lambda e: e.dma_start(out=dst, in_=src), writes=[bS], tag=bS)

        S.dma("sp", lambda e: e.dma_start(out=x_s[:], in_=x_smp), writes=[x_sB])
        sload("sp", cs_s[:], cs_smp_d)
        sload("sp", csq_s[:], csq_smp_d)
        sload("sp", ptT[:], ptT_d)
        sload("sp", ciota[:], ciota_d)
        sload("pool", w_ukT[:], w_ukT_d)
        S.dma("sp", lambda e: e.dma_start(out=pool_smp[:, 0:14, :], in_=state_d[:, 1:15, :]), writes=[bS], tag=bS)
        for blk in range(2):
            S.dma("sp", lambda e: e.dma_start(out=stt[blk][:], in_=state_d[blk * 8:(blk + 1) * 8].rearrange("s r d -> (s r) d")), writes=[sttB[blk]])
        S.op("pool", lambda e: e.tensor_scalar(out=idx_l[:], in0=ptT[:].unsqueeze(2).to_broadcast([128, NS, 16]), scalar1=16, scalar2=None, op0=ALU.mult),
             reads=[bS], writes=[bS])
        S.op("pool", lambda e: e.tensor_tensor(out=idx_l[:], in0=idx_l[:], in1=ciota[:].unsqueeze(1).to_broadcast([128, NS, 16]), op=ALU.add),
             reads=[bS], writes=[bS])
        S.op("pool", lambda e: e.tensor_scalar(out=idx_k[:], in0=ptT[:].unsqueeze(2).to_broadcast([128, NS, 2]), scalar1=2, scalar2=None, op0=ALU.mult),
             reads=[bS], writes=[bS])
        S.op("pool", lambda e: e.tensor_tensor(out=idx_k[:], in0=idx_k[:], in1=ciota[:, 0:2].unsqueeze(1).to_broadcast([128, NS, 2]), op=ALU.add),
             reads=[bS], writes=[bS])
        b_gt = sb("b_gt_s", [16, 2, D], F32, esS)
        b_gtB = Buf("b_gt_s")
        S.dma("sp", lambda e: e.dma_start(out=b_gt[:], in_=b_gt_d), writes=[b_gtB])
        for which in range(2):
            for hf in range(2):
                wb, wbB = wget()
                wv = kview(8, 512)(wb)
                pj = ps_alloc()
                mm(pj, ps[pj][0:16, :], [(scT[:, k, 1:17], wv[:, k, :]) for k in range(8)], [wbB, b_const])
                cs_ = slice(hf * 512, (hf + 1) * 512)
                S.op("dve", lambda e: e.tensor_tensor(out=gtS[:, which, cs_], in0=ps[pj][0:16, :], in1=b_gt[:, which, cs_], op=ALU.add),
                     reads=[psB[pj], b_gtB], writes=[b_const])
                ps_release(pj)
        for blk in range(2):
            for g in range(4):
                pi = ps_alloc()
                S.op("pe", lambda e: e.transpose(out=ps[pi][:, 0:120], in_=stt[blk][:, g * 128:(g + 1) * 128], identity=identf[0:120, 0:120]),
                     reads=[sttB[blk], b_const], writes=[psB[pi]])
                S.op("act", lambda e: e.copy(out=prevT[:, g, blk * 8:(blk + 1) * 8, :], in_=ps[pi][:, 0:120].rearrange("p (s r) -> p s r", s=8)),
                     reads=[psB[pi]], writes=[prevTB])
                ps_release(pi)
        if smp_stop <= 1:
            S.barrier(); esS.close(); return
        front2([(x_s, x_sB, 0, cs_s[:, 0, :], lat_smp, kr_smp)], NS, True, hT, hTB, bS)
        q_build(NS, csq_s, bS, hT, hTB)
        S.op("dve", lambda e: e.tensor_scalar(out=qg[:], in0=QT[0:64, :, 0:NS], scalar1=vcol(V_GQK, 64), scalar2=None, op0=ALU.mult),
             reads=[QTB, b_const], writes=[bS])
        pi = ps_alloc()
        for rc in range(2):
            for h in range(H):
                c0 = (rc * H + h) * NS
                mm(pi, ps[pi][:, c0:c0 + NS], [(w_ukT[0:64, h, rc * 128:(rc + 1) * 128], qg[0:64, h, :])], [bS], last_signal=(rc == 1 and h == H - 1))
        S.op("act", lambda e: e.copy(out=qabs[:].rearrange("p c s h -> p c h s"), in_=ps[pi][:, 0:2 * H * NS].rearrange("p (c h s) -> p c h s", c=2, h=H)),
             reads=[psB[pi]], writes=[bS])
        ps_release(pi)
        S.op("pool", lambda e: e.tensor_copy(out=qpe[:].rearrange("p s h -> p h s"), in_=QT[64:96, :, 0:NS]), reads=[QTB], writes=[bS])
        pk = ps_alloc()
        mm(pk, ps[pk][0:NS, :], [(ckvnT[:, rc, 0:NS], w_uk[:, rc, :]) for rc in range(2)], [ckvnTB, b_const])
        S.op("act", lambda e: e.activation(out=sqb[0][0:NS, :], in_=ps[pk][0:NS, :], func=AF.Square), reads=[psB[pk]], writes=[sqbB[0]])
        ps_release(pk)
        S.op("dve", lambda e: e.tensor_reduce(out=nw[:, 0, :], in_=sqb[0][0:NS, :].rearrange("p (h d) -> p h d", h=H), axis=AX.X, op=ALU.add),
             reads=[sqbB[0]], writes=[nwB])
        S.op("act", lambda e: e.activation(out=nw[:, 1, :], in_=nw[:, 0, :], func=AF.Ln, bias=epsc[0:NS, :], scale=1.0 / 64), reads=[nwB, b_const], writes=[nwB])
        S.op("act", lambda e: e.activation(out=nw[:, 1, :], in_=nw[:, 1, :], func=AF.Exp, scale=-0.5), reads=[nwB], writes=[nwB])

        if smp_stop <= 2:
            S.barrier(); esS.close(); return
        cache_l2 = cache_lat
        cache_k2 = cache_kr
        st_ = {"gl": 0, "t": 0, "b": 0}

        def gather_lat(s, b):
            gi = st_["gl"] % 3
            st_["gl"] += 1
            for cc in range(2):
                c = 2 * b + cc
                S.dma("pool", lambda e: e.indirect_dma_start(out=glat[gi][:, cc * 8:(cc + 1) * 8, :].rearrange("p r d -> p (r d)"), out_offset=None,
                                                             in_=cache_l2, in_offset=bass.IndirectOffsetOnAxis(ap=idx_l[:, s, c:c + 1], axis=0)),
                      reads=[bS], writes=[glatB[gi]])
            return gi

        def gather_kr(s):
            gi = s % 2
            for c in range(2):
                S.dma("pool", lambda e: e.indirect_dma_start(out=gkr[gi][:, c * 64:(c + 1) * 64, :].rearrange("p r d -> p (r d)"), out_offset=None,
                                                             in_=cache_k2, in_offset=bass.IndirectOffsetOnAxis(ap=idx_k[:, s, c:c + 1], axis=0)),
                      reads=[bS], writes=[gkrB[gi]])
            return gi

        seq = [(s, b) for s in range(NS) for b in range(NB)]
        if smp_stop < 90:
            seq = seq[:max(1, smp_stop - 3)]
        ones1k = sb("ones1k", [128, 1024], BF16, esS)
        S.op("pool", lambda e: e.memset(ones1k[:], 1.0), writes=[bS])
        latT4 = [sb("latT4_%d" % i, [128, 4, 2, 128], BF16, esS) for i in range(3)]
        latT4B = [Buf("latT4_%d" % i) for i in range(3)]
        kT4 = [sb("kT4_%d" % i, [32, 4, 128], BF16, esS) for i in range(3)]
        kT4B = [Buf("kT4_%d" % i) for i in range(3)]
        OD = ps_alloc()
        OD_bf = ps[OD][:].bitcast(BF16)
        gl_of, kr_of, pdr_of = {}, {}, {}
        gl_of[0] = gather_lat(*seq[0])
        kr_of[0] = gather_kr(0)

        def batch_res(n_):
            if n_ >= len(seq) or n_ in gl_of:
                return
            s_, b_ = seq[n_]
            gl_of[n_] = gather_lat(s_, b_)
            if b_ == 0 and s_ not in kr_of:
                kr_of[s_] = gather_kr(s_)

        quads = [(n_, q) for n_ in range(len(seq)) for q in range(4)]

        def emit_T(qi):
            n_, q = quads[qi]
            batch_res(n_)
            if n_ not in pdr_of:
                pdr_of[n_] = ps_alloc()
            s_, b_ = seq[n_]
            gl, kr_i, pdr = gl_of[n_], kr_of[s_], pdr_of[n_]
            A = ps_alloc()
            Av = ps[A][:].bitcast(BF16)
            pdv = ps[pdr][:].bitcast(BF16)
            li = qi % 3
            for t in range(4):
                g = q * 4 + t
                for rc in range(2):
                    S.op("pe", lambda e: e.transpose(out=Av[:, (t * 2 + rc) * 128:(t * 2 + rc + 1) * 128], in_=glat[gl][:, g, rc * 128:(rc + 1) * 128], identity=identb),
                         reads=[glatB[gl], b_const], writes=[psB[A]], signal=(t == 3 and rc == 1))
            for t in range(4):
                row = b_ * G + q * 4 + t
                S.op("pe", lambda e: e.transpose(out=pdv[0:32, 512 + t * 128:512 + (t + 1) * 128], in_=gkr[kr_i][:, row, :], identity=identb),
                     reads=[gkrB[kr_i], b_const], writes=[psB[pdr]], signal=(t == 3))
            S.op("dve", lambda e: e.tensor_tensor(out=latT4[li][:].rearrange("p t c n -> p (t c n)"), in0=Av[:, :], in1=ones1k[:, :], op=ALU.mult),
                 reads=[psB[A], bS], writes=[latT4B[li]])
            ps_release(A)
            S.op("act", lambda e: e.copy(out=kT4[li][:].rearrange("p t n -> p (t n)"), in_=pdv[0:32, 512:1024]), reads=[psB[pdr]], writes=[kT4B[li]])

        pending = []

        def batch_pe(s, b, bi, gl):
            for g in range(G):
                S.op("pe", lambda e: e.matmul(ps[OD][0:8, 0:KVR], lhsT=pb[bi][:, g, :], rhs=glat[gl][:, g, :], start=(b == 0 and g == 0), stop=False),
                     reads=[pbB[bi], glatB[gl]], writes=[psB[OD]], signal=(g == G - 1))
            S.op("pe", lambda e: e.matmul(ps[OD][0:8, 256:256 + G * H], lhsT=onesb[:, 0:8], rhs=pb[bi][:].rearrange("p g h -> p (g h)"), start=(b == 0), stop=(b == NB - 1)),
                 reads=[pbB[bi], b_const], writes=[psB[OD]])
            if b == NB - 1:
                pdn = ps_alloc()
                mm(pdn, ps[pdn][0:NS, 0:8], [(ckvnT[:, rc, 0:NS], qabs[:, rc, s, :]) for rc in range(2)], [ckvnTB, bS], last_signal=False)
                mm(pdn, ps[pdn][0:NS, 8:16], [(kpeT[0:32, 0:NS], qpe[:, s, :])], [kpeTB, bS])
                S.op("dve", lambda e: e.tensor_tensor(out=nw[:, 2, :], in0=ps[pdn][0:NS, 0:8], in1=nw[:, 1, :], op=ALU.mult), reads=[psB[pdn], nwB], writes=[nwB])
                S.op("dve", lambda e: e.tensor_tensor(out=nw[:, 2, :], in0=ps[pdn][0:NS, 8:16], in1=nw[:, 2, :], op=ALU.add), reads=[psB[pdn], nwB], writes=[nwB])
                S.op("act", lambda e: e.activation(out=nw[:, 3, :], in_=nw[:, 2, :], func=AF.Exp, scale=SM), reads=[nwB], writes=[nwB])
                S.op("dve", lambda e: e.tensor_scalar(out=pnew[:], in0=nw[:, 3, :], scalar1=identf[0:NS, s:s + 1], scalar2=None, op0=ALU.mult),
                     reads=[nwB, b_const], writes=[pnewB])
                S.op("pe", lambda e: e.matmul(ps[OD][0:8, 0:KVR], lhsT=pnew[:, :], rhs=ckv_b[smp_ci["i"]][0:NS, 0:KVR], start=False, stop=True),
                     reads=[pnewB, ckv_bB[smp_ci["i"]]], writes=[psB[OD]])
                S.op("dve", lambda e: e.tensor_reduce(out=dn[:, 0, :], in_=ps[OD][0:8, 256:256 + G * H].rearrange("p (g h) -> p h g", g=G), axis=AX.X, op=ALU.add),
                     reads=[psB[OD]], writes=[dnB])
                mm(pdn, ps[pdn][0:8, 16:16 + H], [(onesb[0:NS, 0:8], pnew[:, :])], [pnewB, b_const])
                S.op("dve", lambda e: e.tensor_tensor(out=dn[:, 0, :], in0=ps[pdn][0:8, 16:16 + H], in1=dn[:, 0, :], op=ALU.add), reads=[psB[pdn], dnB], writes=[dnB])
                ps_release(pdn)
                S.op("dve", lambda e: e.tensor_tensor(out=dn[:, 1, :], in0=dn[:, 0, :], in1=identf[0:8, 0:8], op=ALU.mult), reads=[dnB, b_const], writes=[dnB])
                S.op("dve", lambda e: e.tensor_reduce(out=dn[:, 2, 0:1], in_=dn[:, 1, :], axis=AX.X, op=ALU.add), reads=[dnB], writes=[dnB])
                S.op("dve", lambda e: e.reciprocal(out=dn[:, 2, 1:2], in_=dn[:, 2, 0:1]), reads=[dnB], writes=[dnB])
                S.op("dve", lambda e: e.tensor_scalar(out=oln[:], in0=ps[OD][0:8, 0:KVR], scalar1=dn[:, 2, 1:2], scalar2=None, op0=ALU.mult),
                     reads=[psB[OD], dnB], writes=[olnB])
                for rc in range(2):
                    c0 = 768 + (rc * NS + s) * H
                    S.op("pe", lambda e: e.transpose(out=OD_bf[:, c0:c0 + H], in_=oln[0:8, rc * 128:(rc + 1) * 128], identity=identb[0:8, 0:8]),
                         reads=[olnB, b_const], writes=[psB[OD]], signal=(rc == 1))

        emit_T(0)
        if len(quads) > 1:
            emit_T(1)
        for qi, (n_, q) in enumerate(quads):
            s, b = seq[n_]
            if q == 1:
                batch_res(n_ + 1)
                batch_res(n_ + 2)
            if qi + 2 < len(quads):
                emit_T(qi + 2)
            gl, pdr = gl_of[n_], pdr_of[n_]
            bi = n_ % 2
            li = qi % 3
            for t in range(4):
                g = q * 4 + t
                pk = ps_alloc()
                mm(pk, ps[pk][:, :], [(latT4[li][:, t, rc, :], w_uk[:, rc, :]) for rc in range(2)], [latT4B[li], b_const])
                si = st_["t"] % 3
                st_["t"] += 1
                S.op("act", lambda e: e.activation(out=sqb[si][:, :], in_=ps[pk][:, :], func=AF.Square), reads=[psB[pk]], writes=[sqbB[si]])
                ps_release(pk)
                red = "dve"
                S.op(red, lambda e: e.tensor_reduce(out=ssb[bi][:, g, :], in_=sqb[si][:, :].rearrange("p (h d) -> p h d", h=H), axis=AX.X, op=ALU.add),
                     reads=[sqbB[si]], writes=[ssbB[bi]])
                mm(pdr, ps[pdr][:, g * 16:g * 16 + 8], [(latT4[li][:, t, rc, :], qabs[:, rc, s, :]) for rc in range(2)], [latT4B[li], bS], last_signal=False)
                mm(pdr, ps[pdr][:, g * 16 + 8:g * 16 + 16], [(kT4[li][:, t, :], qpe[:, s, :])], [kT4B[li], bS], last_signal=True)
            if q == 0 and pending:
                batch_pe(*pending.pop(0))
            if q < 3:
                continue
            S.op("act", lambda e: e.activation(out=ssb[bi][:], in_=ssb[bi][:], func=AF.Ln, bias=epsc[:], scale=1.0 / 64), reads=[ssbB[bi], b_const], writes=[ssbB[bi]])
            S.op("act", lambda e: e.activation(out=ssb[bi][:], in_=ssb[bi][:], func=AF.Exp, scale=-0.5), reads=[ssbB[bi]], writes=[ssbB[bi]])
            drv = ps[pdr][:, 0:G * 16].rearrange("p (g x) -> p g x", g=G)
            S.op("dve", lambda e: e.tensor_tensor(out=scb[bi][:], in0=drv[:, :, 0:8], in1=ssb[bi][:], op=ALU.mult), reads=[psB[pdr], ssbB[bi]], writes=[scbB[bi]])
            S.op("dve", lambda e: e.tensor_tensor(out=scb[bi][:], in0=drv[:, :, 8:16], in1=scb[bi][:], op=ALU.add), reads=[psB[pdr], scbB[bi]], writes=[scbB[bi]])
            ps_release(pdr)
            S.op("act", lambda e: e.activation(out=pb[bi][:], in_=scb[bi][:], func=AF.Exp, scale=SM), reads=[scbB[bi]], writes=[pbB[bi]])
            pending.append((s, b, bi, gl))
        while pending:
            batch_pe(*pending.pop(0))
        pT_all = OD
        pT_v = OD_bf[:, 768:1024]
        if smp_stop < 95:
            S.barrier(); esS.close(); return
        S.op("act", lambda e: e.copy(out=OLT[:].rearrange("p c s h -> p (c s h)"), in_=pT_v[:, 0:2 * NS * H]), reads=[psB[pT_all]], writes=[OLTB])
        ps_release(pT_all)
        for h in range(H):
            pi = ps_alloc()
            mm(pi, ps[pi][0:64, 0:NS], [(w_uv[:, rc, h * 64:(h + 1) * 64], OLT[:, rc, :, h]) for rc in range(2)], [OLTB, b_const])
            r0 = (h % 2) * 64
            S.op("act", lambda e: e.copy(out=oT[r0:r0 + 64, h // 2, 0:NS], in_=ps[pi][0:64, 0:NS]), reads=[psB[pi]], writes=[oTB])
            ps_release(pi)

        def pool_hook_s():
            for g in range(4):
                w = 2 << g
                S.op("dve", lambda e: e.tensor_reduce(out=wsum[:, g, :], in_=prevT[:, g, :, 15 - (w - 1):15], axis=AX.X, op=ALU.add),
                     reads=[prevTB], writes=[wsumB])
                S.op("dve", lambda e: e.tensor_tensor(out=wsum[:, g, :], in0=wsum[:, g, :], in1=uT[:, g, 16:16 + NS], op=ALU.add),
                     reads=[wsumB, uTB], writes=[wsumB])
                S.op("dve", lambda e: e.scalar_tensor_tensor(out=dT[:, g, 0:NS], in0=wsum[:, g, :], scalar=1.0 / w, in1=uT[:, g, 16:16 + NS],
                                                             op0=ALU.mult, op1=ALU.subtract), reads=[wsumB, uTB], writes=[dTB])
            pi = ps_alloc()
            for g in range(4):
                S.op("pe", lambda e: e.transpose(out=ps[pi][0:NS, g * 128:(g + 1) * 128], in_=uT[:, g, 16:16 + NS], identity=identf[:]),
                     reads=[uTB, b_const], writes=[psB[pi]], signal=(g == 3))
            S.op("act", lambda e: e.copy(out=gtmp[0][0:NS, 0:512], in_=ps[pi][0:NS, :]), reads=[psB[pi]], writes=[gtmpB[0]])
            ps_release(pi)
            S.dma("sp", lambda e: e.dma_start(out=pool_smp[:, 14, :], in_=gtmp[0][0:NS, 0:512]), reads=[gtmpB[0]], tag=gtmpB[0])

        merge_out_ffn.pool_hook = pool_hook_s
        merge_out_ffn(NS, 1, NS, [x_s], [x_sB],
                      lambda which, P: gtS[0:P, which, :],
                      lambda P: (modF[:, 24:32, 1:1 + P], modF[:, 16:24, 1:1 + P]),
                      hT, hTB,
                      lambda tb: y_smp)
        S.barrier()
        esS.close()

    smp_ci = {"i": 0}

    x_tok = [sb("x_tok%d" % i, [128, D], F32, esP) for i in range(NXT)]
    x_tokB = [Buf("x_tok%d" % i) for i in range(NXT)]
    xo_tok = [sb("xo_tok%d" % i, [128, D], F32, esP) for i in range(2)]
    xo_tokB = [Buf("xo_tok%d" % i) for i in range(2)]
    cs_tok = [sb("cs_tok%d" % i, [128, 4, 64], F32, esP) for i in range(2)]
    cs_tokB = [Buf("cs_tok%d" % i) for i in range(2)]
    csq = sb("csq", [128, 2, T], F32, esP)
    csqB = Buf("csq")
    utail = [sb("utail%d" % i, [128, 4, 16], F32, esP) for i in range(2)]
    utailB = [Buf("utail%d" % i) for i in range(2)]
    pcorr = sb("pcorr", [128, 4, 16], F32, esP)
    Vb = [sb("Vb%d" % i, [128, 4, 4, 128], BF16, esP) for i in range(2)]
    VbB = [Buf("Vb%d" % i) for i in range(2)]
    cload("sp", pcorr[:], pcorr_d)
    S.op("dve", lambda e: e.memset(utail[0][:], 0.0), writes=[utailB[0]])
    S.op("dve", lambda e: e.memset(utail[1][:], 0.0), writes=[utailB[1]])
    for i in range(2):
        S.op("pool", lambda e: e.memset(Vb[i][:], 1.0), writes=[VbB[i]])
    xq = {"n": 0}

    def x_block():
        i = xq["n"] % NXT
        xq["n"] += 1
        return x_tok[i], x_tokB[i]

    for i in range(NT):
        ci = i % 2
        S.dma("sp", lambda e: e.dma_start(out=cs_tok[0][:], in_=cs_own_d[i]), writes=[cs_tokB[0]])
        S.dma("sp", lambda e: e.dma_start(out=cs_tok[1][:], in_=cs_oth_d[i]), writes=[cs_tokB[1]])
        S.dma("sp", lambda e: e.dma_start(out=csq[:], in_=csq_own_d[i]), writes=[csqB])
        for half in range(2):
            blks = []
            for jj in range(2):
                sbk = 2 * half + jj
                r0 = i * T + sbk * 128
                S.dma("sp", lambda e: e.dma_start(out=xo_tok[jj][:], in_=x_oth[r0:r0 + 128, :]), writes=[xo_tokB[jj]])
                blks.append((xo_tok[jj], xo_tokB[jj], sbk * 128, cs_tok[1][:, sbk, :], None, None))
            front2(blks, 128, False, hTo, actTB, cs_tokB[1])
        kv_build(2 * i + 1, T)
        ut = utail[i % 2]
        utB_ = utailB[i % 2]
        for g in range(4):
            pi = ps_alloc()
            mm(pi, ps[pi][:, 0:16], [(w_u[:, k, g * 128:(g + 1) * 128], hTo[:, k, T - 16:T]) for k in range(8)], [actTB, b_const])
            S.op("act", lambda e: e.copy(out=ut[:, g, :], in_=ps[pi][:, 0:16]), reads=[psB[pi]], writes=[utB_])
            ps_release(pi)
        xts, xtBs = [], []
        for half in range(2):
            blks = []
            for jj in range(2):
                sbk = 2 * half + jj
                xt, xtB = x_block()
                xts.append(xt)
                xtBs.append(xtB)
                r0 = i * T + sbk * 128
                S.dma("sp", lambda e: e.dma_start(out=xt[:], in_=x_own[r0:r0 + 128, :]), writes=[xtB])
                blks.append((xt, xtB, sbk * 128, cs_tok[0][:, sbk, :], lat_own[r0:r0 + 128, :], kr_own[r0:r0 + 128, :]))
            front2(blks, 128, False, hT, hTB, cs_tokB[0])
        kv_build(2 * i, T)
        q_build(T, csq, csqB, hT, hTB)
        attention(i)

        def pool_hook(i=i):
            prev = utail[(i + 1) % 2]
            prevB = utailB[(i + 1) % 2]
            cur = utail[i % 2]
            curB = utailB[i % 2]
            S.op("pool", lambda e: e.tensor_scalar(out=pt1[:, 0:64].rearrange("p (g n) -> p g n", g=4), in0=prev[:], scalar1=vcol(V_NFLAG), scalar2=None, op0=ALU.mult),
                 reads=[prevB, b_const], writes=[pt1B])
            S.op("dve", lambda e: e.scalar_tensor_tensor(out=uT[:, :, 1:16], in0=cur[:, :, 1:16], scalar=vcol(V_FLAG),
                                                          in1=pt1[:, 0:64].rearrange("p (g n) -> p g n", g=4)[:, :, 1:16], op0=ALU.mult, op1=ALU.add),
                 reads=[curB, pt1B, b_const], writes=[uTB])
            pool_branch(T, first_tile=(i == 0))
            if i == NT - 1:
                pi = ps_alloc()
                for g in range(4):
                    S.op("pe", lambda e: e.transpose(out=ps[pi][0:16, g * 128:(g + 1) * 128], in_=uT[:, g, T:T + 16], identity=identf[:]),
                         reads=[uTB, b_const], writes=[psB[pi]], signal=(g == 3))
                S.op("act", lambda e: e.copy(out=gtmp[0][0:16, 0:512], in_=ps[pi][0:16, :]), reads=[psB[pi]], writes=[gtmpB[0]])
                ps_release(pi)
                S.dma("sp", lambda e: e.dma_start(out=pool_own, in_=gtmp[0][0:16, 0:512]), reads=[gtmpB[0]], tag=gtmpB[0])

        merge_out_ffn.pool_hook = pool_hook
        merge_out_ffn(T, 4, 128, xts, xtBs,
                      lambda which, P: gt_bc[0:P, which, :],
                      lambda P: (modF[:, 24:32, 0:1].to_broadcast([128, 8, P]), modF[:, 16:24, 0:1].to_broadcast([128, 8, P])),
                      hT, hTB,
                      lambda tb, i=i: y_own[i * T + tb * 128:i * T + (tb + 1) * 128, :])

    S.barrier()
    esP.close()
    if do_sample:
        sample_phase()
        S.barrier()
    es.close()
    S.close()
    return nc


def _rope_tables(pos):
    inv = 1.0 / (10000.0 ** (np.arange(0, DR, 2, dtype=np.float32) / DR))
    ang = pos.astype(np.float32)[:, None] * inv[None, :].astype(np.float32)
    ang = ang.astype(np.float32)
    return np.cos(ang).astype(np.float32), np.sin(ang).astype(np.float32)


def _consts():
    cb = np.zeros((128, 5, 128), np.float32)
    cb[:, 0, :] = np.eye(128)
    cb[0:64, 1, 0:64] = 1.0
    cb[64:96, 1, 64:96] = 1.0
    cb[0:64, 2, 0:64] = 1.0
    cb[64:128, 2, 64:128] = 1.0
    cb[:, 3, :] = (np.arange(128)[:, None] <= np.arange(128)[None, :]).astype(np.float32)
    cb[:, 4, :] = 1.0
    return cb


def _fm(v, ncol):
    return np.ascontiguousarray(v.reshape(ncol, 128).T)


PERM = np.concatenate([np.arange(16, 32), np.arange(0, 16)])


def make_in_maps(inp, NT=8, NS=16, do_sample=True, cores=range(8)):
    f32 = np.float32
    w_in = inp["w_in"][0]
    w_uq = inp["w_uq"][0]
    b_ada = inp["b_ada"][0]
    shared = {}
    shared["w_ada"] = np.ascontiguousarray(inp["w_ada"][0])
    shared["b_gt"] = np.ascontiguousarray(np.broadcast_to(np.stack([b_ada[2048:3072], b_ada[5120:6144]])[None], (16, 2, D)))
    shared["rowv"] = np.ascontiguousarray(np.broadcast_to(np.concatenate([inp["g_kv_lat"][0], inp["g_k_rope"][0]])[None], (128, 288)))
    shared["cbf"] = _consts()
    shared["identf"] = np.eye(128, dtype=f32)
    shared["w_in_q"] = np.ascontiguousarray(w_in[:, 0:384])
    shared["w_in_kv"] = np.ascontiguousarray(w_in[:, 384:672])
    shared["w_in_u"] = np.ascontiguousarray(w_in[:, 672:1184])
    shared["w_in_g"] = np.ascontiguousarray(w_in[:, 1184:3232])
    wq2 = np.zeros((QR, H, 192), f32)
    wq2[:, :, 0:96] = w_uq
    wq2[:, :, 96 + 64:192] = w_uq[:, :, 64 + PERM]
    shared["w_uq2"] = wq2
    shared["w_uk"] = np.ascontiguousarray(inp["w_uk"][0].reshape(KVR, 512))
    shared["w_uv"] = np.ascontiguousarray(inp["w_uv"][0].reshape(KVR, 512))
    shared["w_ao"] = np.ascontiguousarray(inp["w_attn_o"][0].reshape(512, D))
    shared["w_pool"] = np.ascontiguousarray(inp["w_pool"][0])
    shared["w_out"] = np.ascontiguousarray(inp["w_out"][0])
    shared["w_gu"] = np.ascontiguousarray(inp["w_gu"][0])
    shared["w_down"] = np.ascontiguousarray(inp["w_down"][0])
    if do_sample:
        shared["w_ukT"] = np.ascontiguousarray(inp["w_uk"][0].transpose(2, 1, 0))
        shared["cache_lat"] = inp["cache_kv_latent"][0].reshape(-1, 2048)
        shared["cache_kr"] = inp["cache_k_rope"][0].reshape(-1, 2048)
    gq = inp["g_q_rope"][0]
    in_maps = []
    for c in cores:
        b, half = c // 2, c % 2
        m = dict(shared)
        xs = inp["x_prompt"][b].reshape(16, T, D)
        own_t = [2 * i + half for i in range(NT)]
        oth_t = [2 * i + 1 - half for i in range(NT)]
        m["x_own"] = np.ascontiguousarray(xs[own_t].reshape(NT * T, D))
        m["x_oth"] = np.ascontiguousarray(xs[oth_t].reshape(NT * T, D))
        cT = np.zeros((128, 8, 17), f32)
        cT[:, :, 0] = _fm(inp["c_prompt"][b], 8)
        for s in range(NS):
            cT[:, :, 1 + s] = _fm(inp["c_sample"][16 * c + s], 8)
        m["cT"] = cT
        vecs = np.zeros((128, 80), f32)
        for gi, base in enumerate([0, 1024, 3072, 4096]):
            vecs[:, gi * 8:(gi + 1) * 8] = _fm(b_ada[base:base + 1024], 8)
        vecs[:, 32:40] = _fm(inp["g_norm1"][0], 8)
        vecs[:, 40:48] = _fm(inp["g_norm2"][0], 8)
        vecs[:, 48:56] = _fm(inp["s_pool"][0], 8)
        vecs[:, 56:59] = _fm(inp["g_q_lat"][0], 3)
        vecs[0:64, 59] = inp["g_q_nope"][0]
        vecs[64:96, 59] = gq
        vecs[64:96, 60] = gq[PERM]
        vecs[0:64, 61] = 1.0 / 64
        vecs[64:96, 61] = 1.0 / 32
        vecs[0:64, 62] = inp["g_k_nope"][0]
        vecs[64:128, 62] = inp["g_k_nope"][0]
        vecs[:, 63] = 0.0 if half == 1 else NEG
        vecs[:, 64] = float(half)
        vecs[:, 65] = 1.0 - float(half)
        vecs[0:64, 66] = inp["g_k_nope"][0]
        m["vecs"] = vecs
        cs_o = np.zeros((NT, 128, 4, 64), f32)
        cs_x = np.zeros((NT, 128, 4, 64), f32)
        csq_o = np.zeros((NT, 128, 2, T), f32)
        for i in range(NT):
            for tl, dst in ((own_t[i], cs_o), (oth_t[i], cs_x)):
                cos, sin = _rope_tables(np.arange(tl * T, (tl + 1) * T))
                tab = np.concatenate([cos, cos, -sin, sin], axis=1).reshape(4, 128, 64).transpose(1, 0, 2)
                dst[i] = tab
            cos, sin = _rope_tables(np.arange(own_t[i] * T, (own_t[i] + 1) * T))
            csq_o[i, 64:96, 0, :] = np.concatenate([cos, cos], axis=1).T
            csq_o[i, 64:96, 1, :] = np.concatenate([-sin, sin], axis=1).T
        m["cs_own"], m["cs_oth"], m["csq_own"] = cs_o, cs_x, csq_o
        pc = np.ones((128, 4, 16), f32)
        if half == 0:
            for g, w in enumerate((2, 4, 8, 16)):
                pc[:, g, :] = w / np.minimum(np.arange(16) + 1, w).astype(f32)
        m["pcorr"] = pc
        if do_sample:
            sl = slice(16 * c, 16 * c + NS)
            m["x_smp"] = np.ascontiguousarray(inp["x_sample"][sl, 0, :])
            m["ptT"] = np.ascontiguousarray(inp["page_table"][sl].T.astype(np.int32))
            m["state_pool"] = np.ascontiguousarray(inp["state_pool"][0, sl])
            cos, sin = _rope_tables(np.array([PAST]))
            m["cs_smp"] = np.ascontiguousarray(np.broadcast_to(np.concatenate([cos, cos, -sin, sin], axis=1)[None], (128, 1, 64)))
            cq = np.zeros((128, 2, NS), f32)
            cq[64:96, 0, :] = np.concatenate([cos, cos], axis=1).T
            cq[64:96, 1, :] = np.concatenate([-sin, sin], axis=1).T
            m["csq_smp"] = cq
            m["ciota"] = np.ascontiguousarray(np.broadcast_to(np.arange(16, dtype=np.int32)[None], (128, 16)))
        in_maps.append(m)
    return in_maps


_NC_CACHE = {}


def kernel(**inputs):
    inp = {k: np.asarray(v) for k, v in inputs.items()}
    nphys = inp["cache_kv_latent"].shape[1]
    key = ("full", nphys)
    if key not in _NC_CACHE:
        _NC_CACHE[key] = build(NT=8, NS=16, NPHYS=nphys, do_sample=True)
    nc = _NC_CACHE[key]
    in_maps = make_in_maps(inp, do_sample=True)
    res = run_bass_kernel_spmd(nc, in_maps, core_ids=list(range(8)))
    return assemble(res.results, inp)


def assemble(results, inp, NT=8):
    f32 = np.float32
    yp = np.zeros((4, 16, T, D), f32)
    lat = np.zeros((1, 4, 16, T, KVR), f32)
    kr = np.zeros((1, 4, 16, T, DR), f32)
    pool_p = np.zeros((1, 4, 15, DP), f32)
    ys = np.zeros((128, 1, D), f32)
    lat_s = np.zeros((1, 128, 1, KVR), f32)
    kr_s = np.zeros((1, 128, 1, DR), f32)
    pool_s = np.zeros((1, 128, 15, DP), f32)
    for c, r in enumerate(results):
        b, half = c // 2, c % 2
        for i in range(NT):
            t = 2 * i + half
            yp[b, t] = r["y_own"][i * T:(i + 1) * T]
            lat[0, b, t] = r["lat_own"][i * T:(i + 1) * T]
            kr[0, b, t] = r["kr_own"][i * T:(i + 1) * T]
        if half == 1:
            pool_p[0, b] = r["pool_own"][1:16]
        sl = slice(16 * c, 16 * c + 16)
        ys[sl, 0] = r["y_smp"]
        lat_s[0, sl, 0] = r["lat_smp"]
        kr_s[0, sl, 0] = r["kr_smp"]
        pool_s[0, sl] = r["pool_smp"]
    return (yp.reshape(4, 16 * T, D), ys, lat.reshape(1, 4, 16 * T, KVR), kr.reshape(1, 4, 16 * T, DR), pool_p,
            lat_s, kr_s, pool_s)
```

```python
import contextlib
import numpy as np
import concourse.bass as bass
import concourse.mybir as mybir
from concourse.bass_utils import run_bass_kernel_spmd

F32 = mybir.dt.float32
BF16 = mybir.dt.bfloat16
I32 = mybir.dt.int32
ALU = mybir.AluOpType
AF = mybir.ActivationFunctionType
AX = mybir.AxisListType

D = 1024
T = 512
H = 8
QR = 384
KVR = 256
DR = 32
DP = 512
DFF = 2816
NCH_FF = DFF // 128
PAST = 16384
PAGE = 128
NPAGES = 128
EPS = 1e-6
SM = 96.0 ** -0.5
NEG = -30000.0


class Buf:
    __slots__ = ("name", "w", "rs", "dsem", "dcnt")

    def __init__(self, name):
        self.name = name
        self.w = None
        self.rs = []
        self.dsem = None
        self.dcnt = 0


class Sched:
    def __init__(self, nc, same_engine_sync=True):
        self.nc = nc
        self.engs = {"pe": nc.tensor, "act": nc.scalar, "dve": nc.vector, "pool": nc.gpsimd, "sp": nc.sync}
        self.sems = {}
        self.cnt = {}
        self.known = {k: {} for k in self.engs}
        self.same = same_engine_sync
        self._ctx = []
        for k in ("pe", "act", "dve", "pool"):
            cm = nc.semaphore("sem_" + k)
            self.sems[k] = cm.__enter__()
            self._ctx.append(cm)
            self.cnt[k] = 0
        self.dsems = {}
        self.nsem = 4
        self.nops = 0

    def _dsem(self, buf):
        if buf.dsem is None:
            cm = self.nc.semaphore("dsem_%d" % self.nsem)
            buf.dsem = cm.__enter__()
            self._ctx.append(cm)
            self.nsem += 1
            self.dsems[buf.name] = buf
        return buf.dsem

    def _wait(self, ek, key, sem, val):
        kn = self.known[ek]
        if kn.get(key, 0) >= val:
            return
        kn[key] = val
        self.engs[ek].wait_ge(sem, val)

    def _deps(self, ek, reads, writes):
        deps = {}
        own_raw = 0

        def add(tok, raw):
            nonlocal own_raw
            if tok is None:
                return
            key, sem, val = tok
            if key == ek:
                if raw and val > own_raw:
                    own_raw = val
                return
            if key not in deps or deps[key][1] < val:
                deps[key] = (sem, val)

        for b in reads:
            add(b.w, True)
        for b in writes:
            add(b.w, False)
            for r in b.rs:
                add(r, False)
        for key, (sem, val) in deps.items():
            self._wait(ek, key, sem, val)
        if own_raw and self.same and ek != "pe" and ek in self.sems:
            self._wait(ek, ek, self.sems[ek], own_raw)

    def op(self, ek, fn, reads=(), writes=(), signal=True):
        self._deps(ek, reads, writes)
        ins = fn(self.engs[ek])
        self.nops += 1
        if signal:
            ins.then_inc(self.sems[ek], 1)
            self.cnt[ek] += 1
            val = self.cnt[ek]
        else:
            val = self.cnt[ek] + 1
        tok = (ek, self.sems[ek], val)
        for b in reads:
            b.rs.append(tok)
            if len(b.rs) > 64:
                b.rs = _compact(b.rs)
        for b in writes:
            b.w = tok
            b.rs = []
        return ins

    def dma(self, qk, fn, reads=(), writes=(), tag=None):
        self._deps(qk, reads, writes)
        tb = tag if tag is not None else (writes[0] if writes else reads[0])
        sem = self._dsem(tb)
        ins = fn(self.engs[qk])
        self.nops += 1
        ins.then_inc(sem, 16)
        tb.dcnt += 16
        tok = ("d:" + tb.name, sem, tb.dcnt)
        for b in reads:
            b.rs.append(tok)
            if len(b.rs) > 64:
                b.rs = _compact(b.rs)
        for b in writes:
            b.w = tok
            b.rs = []
        return ins

    def inherit(self, dst, src):
        toks = []
        for b in src:
            if b.w is not None:
                toks.append(b.w)
            toks.extend(b.rs)
        toks = _compact(toks)
        for d in dst:
            d.rs = _compact(d.rs + toks)

    def barrier(self):
        for ek in ("pe", "act", "dve", "pool", "sp"):
            for k in ("pe", "act", "dve", "pool"):
                if k != ek and self.cnt[k] > 0:
                    self._wait(ek, k, self.sems[k], self.cnt[k])
            for name, b in self.dsems.items():
                if b.dcnt > 0:
                    self._wait(ek, "d:" + name, b.dsem, b.dcnt)

    def close(self):
        for cm in reversed(self._ctx):
            cm.__exit__(None, None, None)


def _compact(rs):
    best = {}
    for key, sem, val in rs:
        if key not in best or best[key][2] < val:
            best[key] = (key, sem, val)
    return list(best.values())


def build(NT=8, NS=16, NPHYS=20480, do_sample=True, same_sync=True, smp_stop=99, sub=9):
    nc = bass.Bass("TRN2", target_bir_lowering=False)
    S = Sched(nc, same_engine_sync=same_sync)
    es = contextlib.ExitStack()

    def din(name, shape, dtp=F32):
        return nc.dram_tensor(name, list(shape), dtp, kind="ExternalInput").ap()

    def dout(name, shape, dtp=F32):
        return nc.dram_tensor(name, list(shape), dtp, kind="ExternalOutput").ap()

    sb_sizes = {}

    def sb(name, shape, dtp, stack=None):
        sb_sizes[name] = int(np.prod(shape[1:])) * (4 if dtp in (F32, I32) else 2)
        try:
            return (stack or es).enter_context(nc.sbuf_tensor("s_" + name, list(shape), dtp))
        except AssertionError:
            tot = 0
            for k, v in sb_sizes.items():
                tot += v
                print("SBUF %-10s %7d  cum %7d" % (k, v, tot))
            raise

    x_own = din("x_own", [NT * T, D])
    x_oth = din("x_oth", [NT * T, D])
    cT_d = din("cT", [128, 8, 17])
    w_ada = din("w_ada", [D, 6 * D])
    b_gt_d = din("b_gt", [16, 2, D])
    vecs_d = din("vecs", [128, 80])
    rowv_d = din("rowv", [128, 288])
    cbf_d = din("cbf", [128, 5, 128])
    identf_d = din("identf", [128, 128])
    w_in_kv_d = din("w_in_kv", [D, 288])
    w_in_q_d = din("w_in_q", [D, QR])
    w_in_u_d = din("w_in_u", [D, DP])
    w_in_g_d = din("w_in_g", [D, 2 * D])
    w_uq_d = din("w_uq2", [QR, H, 192])
    w_uk_d = din("w_uk", [KVR, 512])
    w_uv_d = din("w_uv", [KVR, 512])
    w_ao_d = din("w_ao", [512, D])
    w_pool_d = din("w_pool", [4, 128, 256])
    w_out_d = din("w_out", [D, D])
    w_gu_d = din("w_gu", [D, 2 * DFF])
    w_down_d = din("w_down", [DFF, D])
    cs_own_d = din("cs_own", [NT, 128, 4, 64])
    cs_oth_d = din("cs_oth", [NT, 128, 4, 64])
    csq_own_d = din("csq_own", [NT, 128, 2, T])
    pcorr_d = din("pcorr", [128, 4, 16])

    y_own = dout("y_own", [NT * T, D])
    lat_own = dout("lat_own", [NT * T, KVR])
    kr_own = dout("kr_own", [NT * T, DR])
    pool_own = dout("pool_own", [16, DP])

    kt_s = nc.dram_tensor("kt_s", [2 * NT, 96, H, T], BF16, kind="Internal").ap()
    v_s = nc.dram_tensor("v_s", [2 * NT, 128, 4, H, 128], BF16, kind="Internal").ap()
    kt_buf = [Buf("kts%d" % i) for i in range(2 * NT)]
    v_buf = [Buf("vs%d" % i) for i in range(2 * NT)]

    if do_sample:
        x_smp = din("x_smp", [NS, D])
        cache_lat = din("cache_lat", [NPHYS * 16, 2048])
        cache_kr = din("cache_kr", [NPHYS * 2, 2048])
        ptT_d = din("ptT", [128, NS], I32)
        state_d = din("state_pool", [NS, 15, DP])
        w_ukT_d = din("w_ukT", [64, H, KVR])
        cs_smp_d = din("cs_smp", [128, 1, 64])
        csq_smp_d = din("csq_smp", [128, 2, NS])
        ciota_d = din("ciota", [128, 16], I32)
        y_smp = dout("y_smp", [NS, D])
        lat_smp = dout("lat_smp", [NS, KVR])
        kr_smp = dout("kr_smp", [NS, DR])
        pool_smp = dout("pool_smp", [NS, 15, DP])

    cbf = sb("cbf", [128, 5, 128], BF16)
    identb = cbf[:, 0, :]
    blk96 = cbf[:, 1, :]
    blk64 = cbf[:, 2, :]
    tri = cbf[:, 3, :]
    onesb = cbf[:, 4, :]
    identf = sb("identf", [128, 128], F32)
    vecs = sb("vecs", [128, 80], F32)
    rowv = sb("rowv", [128, 288], F32)
    modF = sb("modF", [128, 32, 17], F32)
    gt_bc = sb("gt_bc", [128, 2, D], F32)
    scT = sb("scT", [128, 8, 17], BF16)
    w_kv = sb("w_kv", [128, 8, 288], BF16)
    w_u = sb("w_u", [128, 8, DP], BF16)
    w_uq = sb("w_uq", [128, 3, H, 192], BF16)
    w_uk = sb("w_uk", [128, 2, 512], BF16)
    w_uv = sb("w_uv", [128, 2, 512], BF16)
    w_pool = sb("w_pool", [128, 4, 256], BF16)
    w_ao_sb = sb("w_ao_sb", [128, 4, D], BF16)
    b_const = Buf("const")

    V_B = 0
    V_G1 = 32
    V_G2 = 40
    V_SP = 48
    V_GQL = 56
    V_GQ96 = 59
    V_GQ96P = 60
    V_INV96 = 61
    V_GKN2 = 62
    V_FLAGNEG = 63
    V_FLAG = 64
    V_NFLAG = 65
    V_GQK = 66

    NWB = 3
    WELEMS = 4096
    wbuf = [sb("wbuf%d" % i, [128, WELEMS], BF16) for i in range(NWB)]
    wbufB = [Buf("wbuf%d" % i) for i in range(NWB)]

    NXT = 4
    xn = [sb("xn%d" % i, [128, D], BF16) for i in range(2)]
    xnB = [Buf("xn%d" % i) for i in range(2)]
    junk = sb("junk", [128, D], BF16)
    junkB = Buf("junk")
    st = sb("st", [128, 16], F32)
    stB = Buf("st")
    ckv_f = [sb("ckv_f%d" % i, [128, 288], F32) for i in range(2)]
    ckv_fB = [Buf("ckv_f%d" % i) for i in range(2)]
    ckv_b = [sb("ckv_b%d" % i, [128, 288], BF16) for i in range(2)]
    ckv_bB = [Buf("ckv_b%d" % i) for i in range(2)]
    rtmp = sb("rtmp", [128, 64], F32)
    rtmpB = Buf("rtmp")
    epsc = sb("epsc", [128, 1], F32)
    hTB, actTB, ckvnTB, kpeTB = Buf("hT"), Buf("actT"), Buf("ckvnT"), Buf("kpeT")
    sqbB = [Buf("sqb%d" % i) for i in range(3)]
    tfB = [Buf("tf%d" % i) for i in range(4)]
    lnvB = [tfB[0], tfB[1]]
    qlnTB, QTB, uTB, dTB, oTB, mTB = Buf("qlnT"), Buf("QT"), Buf("uT"), Buf("dT"), Buf("oT"), Buf("mT")
    KTstB = QTB
    rt1B, rt2B, pt1B, pt2B = tfB[2], tfB[3], tfB[2], tfB[3]
    VstB = mTB
    sigB = [tfB[0], tfB[1]]
    gtmpB = [tfB[2], tfB[3]]

    esP = contextlib.ExitStack()

    def alloc_tile_bufs(TT, stack):
        nonlocal hT, actT, ckvnT, kpeT, sqb, tf, lnv, qlnT, QT, rt1, rt2, uT, pt1, pt2, dT, oT, mT, sig, gtmp
        sfx = "_%d" % TT
        hT = sb("hT" + sfx, [128, 8, TT], BF16, stack)
        actT = sb("actT" + sfx, [128, NCH_FF, TT], BF16, stack)
        ckvnT = sb("ckvnT" + sfx, [128, 2, TT], BF16, stack)
        kpeT = sb("kpeT" + sfx, [32, TT], BF16, stack)
        sqb = [sb("sqb%d" % i + sfx, [128, T], BF16, stack) for i in range(3)]
        tf = [sb("tf%d" % i + sfx, [128, 16 + T], F32, stack) for i in range(4)]
        lnv = [tf[0], tf[1]]
        qlnT = sb("qlnT" + sfx, [128, 3, TT], BF16, stack)
        QT = sb("QT" + sfx, [96, H, TT], BF16, stack)
        rt1, rt2, pt1, pt2 = tf[2], tf[3], tf[2], tf[3]
        uT = sb("uT" + sfx, [128, 4, 16 + TT], F32, stack)
        dT = sb("dT" + sfx, [128, 4, TT], BF16, stack)
        oT = sb("oT" + sfx, [128, 4, TT], BF16, stack)
        mT = sb("mT" + sfx, [128, 8, TT], BF16, stack)
        sig = [tf[0], tf[1]]
        gtmp = [tf[2], tf[3]]

    hT = actT = ckvnT = kpeT = sqb = tf = lnv = qlnT = QT = rt1 = rt2 = uT = pt1 = pt2 = dT = oT = mT = sig = gtmp = None
    alloc_tile_bufs(T, esP)
    hTo = actT[:, 0:8, :]
    KTst = QT
    Vst = mT[:].rearrange("p a b -> p (a b)").rearrange("p (k h d) -> p k h d", k=4, h=H)
    KTb = [actT[0:96, 8 + 4 * i:12 + 4 * i, :] for i in range(2)]
    KTbB = [Buf("KTb%d" % i) for i in range(2)]
    NPT = 8
    PT = [actT[:, 16 + i, :] for i in range(6)] + [actT[:, 6, :], actT[:, 7, :]]
    PTB = [Buf("PT%d" % i) for i in range(NPT)]
    rden = actT[:, 4:6, :].rearrange("p a b -> p (a b)").bitcast(F32)
    rdenB = Buf("rden")
    rden2 = [rden, actT[:, 2:4, :].rearrange("p a b -> p (a b)").bitcast(F32)]
    rden2B = [rdenB, Buf("rden_b")]
    attn_group = KTbB + PTB + rden2B

    NPS = 8
    ps = [es.enter_context(nc.psum_tensor("ps%d" % i, [128, 512], F32)) for i in range(NPS)]
    psB = [Buf("ps%d" % i) for i in range(NPS)]
    ps_free = list(range(NPS))

    def ps_alloc():
        return ps_free.pop(0)

    def ps_release(i):
        ps_free.append(i)

    def mm(pi, out_ap, pairs, reads, last_signal=True, start=True):
        n = len(pairs)
        for i, (l, r) in enumerate(pairs):
            S.op("pe", lambda e: e.matmul(out_ap, lhsT=l, rhs=r, start=(start and i == 0), stop=(i == n - 1)),
                 reads=reads, writes=[psB[pi]], signal=(last_signal and i == n - 1))

    pieces = []
    wstate = {"issued": 0, "next": 0}

    def add_piece(parts):
        pieces.append(parts)

    def _issue(n):
        b = n % NWB
        for dst_fn, src in pieces[n]:
            S.dma("pool", lambda e: e.dma_start(out=dst_fn(wbuf[b]), in_=src), writes=[wbufB[b]])

    def wget():
        n = wstate["next"]
        wstate["next"] += 1
        lim = min(len(pieces), n + NWB)
        while wstate["issued"] < lim:
            _issue(wstate["issued"])
            wstate["issued"] += 1
        return wbuf[n % NWB], wbufB[n % NWB]

    def kview(k, n, off=0):
        return lambda wb: wb[:, off:off + k * n].rearrange("p (k n) -> p k n", k=k)

    def kmajor(src, c0, c1):
        return src[:, c0:c1].rearrange("(k p) n -> p k n", p=128)

    tiles = ["own%d" % i for i in range(NT)] + (["smp"] if do_sample else [])
    for j in range(12):
        add_piece([(kview(8, 512), kmajor(w_ada, j * 512, (j + 1) * 512))])
    for tname in tiles:
        if tname == "smp":
            for j in (4, 5, 10, 11):
                add_piece([(kview(8, 512), kmajor(w_ada, j * 512, (j + 1) * 512))])
        add_piece([(kview(8, QR), kmajor(w_in_q_d, 0, QR))])
        for j in range(4):
            add_piece([(kview(8, 512), kmajor(w_in_g_d, j * 512, (j + 1) * 512))])
        for j in range(2):
            add_piece([(kview(8, 512), kmajor(w_out_d, j * 512, (j + 1) * 512))])
        for j in range(11):
            add_piece([(kview(8, 256), kmajor(w_gu_d, j * 256, (j + 1) * 256)),
                       (kview(8, 256, 2048), kmajor(w_gu_d, DFF + j * 256, DFF + (j + 1) * 256))])
        for j in range(4):
            for kh in range(2):
                add_piece([(kview(11, 256), w_down_d[kh * 1408:(kh + 1) * 1408, j * 256:(j + 1) * 256].rearrange("(k p) n -> p k n", p=128))])

    def cload(q, dst, src):
        S.dma(q, lambda e: e.dma_start(out=dst, in_=src), writes=[b_const], tag=b_const)

    cload("pool", cbf[:], cbf_d)
    cload("sp", identf[:], identf_d)
    cload("sp", vecs[:], vecs_d)
    cload("sp", rowv[:], rowv_d)
    cload("pool", w_kv[:], kmajor(w_in_kv_d, 0, 288))
    cload("pool", w_u[:], kmajor(w_in_u_d, 0, DP))
    cload("pool", w_uq[:], w_uq_d.rearrange("(c p) h n -> p c h n", p=128))
    cload("pool", w_uk[:], kmajor(w_uk_d, 0, 512))
    cload("pool", w_uv[:], kmajor(w_uv_d, 0, 512))
    cload("pool", w_pool[:], w_pool_d.rearrange("g p n -> p g n"))
    cload("pool", w_ao_sb[:], w_ao_d.rearrange("(pr q) e -> q pr e", q=128))
    S.op("dve", lambda e: e.memset(uT[:], 0.0), writes=[uTB])

    def vcol(c, rows=128, r0=0):
        return vecs[r0:r0 + rows, c:c + 1]

    with contextlib.ExitStack() as es2:
        cT = es2.enter_context(nc.sbuf_tensor("s_cT", [128, 8, 17], F32))
        b_gt = actT[0:16, 0:8, :].rearrange("p a b -> p (a b)").bitcast(F32).rearrange("p (w d) -> p w d", w=2)
        gtP = actT[0:1, 8:16, :].rearrange("p a b -> p (a b)").bitcast(F32).rearrange("p (w d) -> p w d", w=2)
        onesr = es2.enter_context(nc.sbuf_tensor("s_onesr", [1, 128], F32))
        bset = Buf("setup")
        S.dma("sp", lambda e: e.dma_start(out=cT[:], in_=cT_d), writes=[bset])
        S.dma("sp", lambda e: e.dma_start(out=b_gt, in_=b_gt_d), writes=[bset])
        S.op("dve", lambda e: e.memset(onesr[:], 1.0), writes=[bset])
        S.op("act", lambda e: e.activation(out=scT[:], in_=cT[:], func=AF.Silu), reads=[bset], writes=[bset, b_const])
        FMAP = {0: 0, 1: 8, 3: 16, 4: 24}
        for j in range(12):
            wb, wbB = wget()
            wv = kview(8, 512)(wb)
            grp, hf = j // 2, j % 2
            if grp in FMAP:
                pi = ps_alloc()
                for c in range(4):
                    mm(pi, ps[pi][:, c * 32:c * 32 + 17], [(wv[:, k, c * 128:(c + 1) * 128], scT[:, k, :]) for k in range(8)],
                       [wbB, bset], last_signal=(c == 3))
                for c in range(4):
                    ch = FMAP[grp] + hf * 4 + c
                    S.op("dve", lambda e: e.tensor_scalar(out=modF[:, ch, :], in0=ps[pi][:, c * 32:c * 32 + 17],
                                                          scalar1=vcol(V_B + ch), scalar2=None, op0=ALU.add),
                         reads=[psB[pi], b_const], writes=[b_const])
                ps_release(pi)
            else:
                which = 0 if grp == 2 else 1
                pi = ps_alloc()
                pj = ps_alloc()
                mm(pi, ps[pi][0:1, :], [(scT[:, k, 0:1], wv[:, k, :]) for k in range(8)], [wbB, bset])
                cs_ = slice(hf * 512, (hf + 1) * 512)
                S.op("dve", lambda e: e.tensor_tensor(out=gtP[0:1, which, cs_], in0=ps[pi][0:1, :], in1=b_gt[0:1, which, cs_], op=ALU.add),
                     reads=[psB[pi], bset], writes=[bset])

                mm(pi, ps[pi][:, :], [(onesr[0:1, :], gtP[0:1, which, cs_])], [bset])
                S.op("act", lambda e: e.copy(out=gt_bc[:, which, cs_], in_=ps[pi][:, :]), reads=[psB[pi]], writes=[b_const])
                ps_release(pi)
                ps_release(pj)
        for k in range(8):
            S.op("dve", lambda e: e.tensor_scalar(out=modF[:, 8 + k, :], in0=modF[:, 8 + k, :], scalar1=1.0, scalar2=vcol(V_G1 + k),
                                                  op0=ALU.add, op1=ALU.mult), reads=[b_const], writes=[b_const])
            S.op("dve", lambda e: e.tensor_scalar(out=modF[:, 24 + k, :], in0=modF[:, 24 + k, :], scalar1=1.0, scalar2=vcol(V_G2 + k),
                                                  op0=ALU.add, op1=ALU.mult), reads=[b_const], writes=[b_const])
        S.barrier()

    def rstd_small(ss_ap, out_ap, inv_n):
        S.op("act", lambda e: e.activation(out=out_ap, in_=ss_ap, func=AF.Ln, bias=vcol_eps(out_ap), scale=inv_n),
             reads=[stB, b_const], writes=[stB])
        S.op("act", lambda e: e.activation(out=out_ap, in_=out_ap, func=AF.Exp, scale=-0.5), reads=[stB], writes=[stB])

    S.op("dve", lambda e: e.memset(epsc[:], EPS), writes=[b_const])

    def vcol_eps(ap):
        n = ap.shape[0]
        return epsc[0:n, :]

    junkB2 = [Buf("junk_a"), Buf("junk_b")]
    rtmp2 = [rtmp, sb("rtmp_b", [128, 64], F32, esP)]
    rtmp2B = [rtmpB, Buf("rtmp_b")]

    def front2(blocks, P, is_smp, hdst, hdstB, csB):
        nb = len(blocks)
        stv = st[0:P, :].rearrange("p (j c) -> p j c", j=2)
        for j, (xt, xtB, c0, cs_ap, o_lat, o_kr) in enumerate(blocks):
            S.op("act", lambda e: e.activation(out=xn[j][0:P, :], in_=xt[0:P, :], func=AF.Square, accum_out=st[0:P, 8 * j:8 * j + 1]),
                 reads=[xtB], writes=[xnB[j], stB])
        S.op("act", lambda e: e.activation(out=stv[:, 0:nb, 1], in_=stv[:, 0:nb, 0], func=AF.Ln, bias=epsc[0:P, :], scale=1.0 / D),
             reads=[stB, b_const], writes=[stB])
        S.op("act", lambda e: e.activation(out=stv[:, 0:nb, 1], in_=stv[:, 0:nb, 1], func=AF.Exp, scale=-0.5), reads=[stB], writes=[stB])
        for j, (xt, xtB, c0, cs_ap, o_lat, o_kr) in enumerate(blocks):
            S.op("act", lambda e: e.activation(out=xn[j][0:P, :], in_=xt[0:P, :], func=AF.Copy, scale=st[0:P, 8 * j + 1:8 * j + 2]),
                 reads=[xtB, stB], writes=[xnB[j]])
        for j, (xt, xtB, c0, cs_ap, o_lat, o_kr) in enumerate(blocks):
            pi = ps_alloc()
            pv = ps[pi][:].bitcast(BF16)
            for k in range(8):
                S.op("pe", lambda e: e.transpose(out=pv[:, k * 128:k * 128 + P], in_=xn[j][0:P, k * 128:(k + 1) * 128], identity=identb[0:P, 0:P]),
                     reads=[xnB[j], b_const], writes=[psB[pi]], signal=(k == 7))
            pv3 = pv.rearrange("p (k n) -> p k n", k=8)[:, :, 0:P]
            if not is_smp:
                a_ap = modF[:, 8:16, 0:1].to_broadcast([128, 8, P])
                s_ap = modF[:, 0:8, 0:1].to_broadcast([128, 8, P])
            else:
                a_ap = modF[:, 8:16, 1:1 + P]
                s_ap = modF[:, 0:8, 1:1 + P]
            hv = hdst[:, :, c0:c0 + P]
            S.op("dve", lambda e: e.tensor_tensor(out=hv, in0=pv3, in1=a_ap, op=ALU.mult), reads=[psB[pi], b_const], writes=[hdstB])
            S.op("dve", lambda e: e.tensor_tensor(out=hv, in0=hv, in1=s_ap, op=ALU.add), reads=[hdstB, b_const], writes=[hdstB])
            ps_release(pi)
        pcs = []
        for j, (xt, xtB, c0, cs_ap, o_lat, o_kr) in enumerate(blocks):
            pi = ps_alloc()
            pcs.append(pi)
            mm(pi, ps[pi][0:P, 0:288], [(hdst[:, k, c0:c0 + P], w_kv[:, k, :]) for k in range(8)], [hdstB, b_const])
            S.op("act", lambda e: e.activation(out=junk[0:P, j * 288:j * 288 + 256], in_=ps[pi][0:P, 0:256], func=AF.Square, accum_out=st[0:P, 8 * j + 2:8 * j + 3]),
                 reads=[psB[pi]], writes=[junkB2[j], stB])
            S.op("act", lambda e: e.activation(out=junk[0:P, j * 288 + 256:j * 288 + 288], in_=ps[pi][0:P, 256:288], func=AF.Square, accum_out=st[0:P, 8 * j + 3:8 * j + 4]),
                 reads=[psB[pi]], writes=[junkB2[j], stB])
        S.op("act", lambda e: e.activation(out=stv[:, 0:nb, 4], in_=stv[:, 0:nb, 2], func=AF.Ln, bias=epsc[0:P, :], scale=1.0 / KVR),
             reads=[stB, b_const], writes=[stB])
        S.op("act", lambda e: e.activation(out=stv[:, 0:nb, 5], in_=stv[:, 0:nb, 3], func=AF.Ln, bias=epsc[0:P, :], scale=1.0 / DR),
             reads=[stB, b_const], writes=[stB])
        S.op("act", lambda e: e.activation(out=stv[:, 0:nb, 4:6], in_=stv[:, 0:nb, 4:6], func=AF.Exp, scale=-0.5), reads=[stB], writes=[stB])
        for j, (xt, xtB, c0, cs_ap, o_lat, o_kr) in enumerate(blocks):
            pi = pcs[j]
            cf, cfB, cb, cbB = ckv_f[j], ckv_fB[j], ckv_b[j], ckv_bB[j]
            rt, rtB = rtmp2[j], rtmp2B[j]
            smp_ci["i"] = j
            S.op("dve", lambda e: e.scalar_tensor_tensor(out=cf[0:P, 0:256], in0=ps[pi][0:P, 0:256], scalar=st[0:P, 8 * j + 4:8 * j + 5], in1=rowv[0:P, 0:256],
                                                         op0=ALU.mult, op1=ALU.mult), reads=[psB[pi], stB, b_const], writes=[cfB])
            S.op("dve", lambda e: e.scalar_tensor_tensor(out=rt[0:P, 0:32], in0=ps[pi][0:P, 256:288], scalar=st[0:P, 8 * j + 5:8 * j + 6], in1=rowv[0:P, 256:288],
                                                         op0=ALU.mult, op1=ALU.mult), reads=[psB[pi], stB, b_const], writes=[rtB])
            ps_release(pi)
            S.op("dve", lambda e: e.tensor_tensor(out=rt[0:P, 32:64], in0=rt[0:P, 0:32], in1=cs_ap[0:P, 0:32], op=ALU.mult),
                 reads=[rtB, csB], writes=[rtB])
            S.op("dve", lambda e: e.tensor_tensor(out=cf[0:P, 256:272], in0=rt[0:P, 16:32], in1=cs_ap[0:P, 32:48], op=ALU.mult),
                 reads=[rtB, csB], writes=[cfB])
            S.op("dve", lambda e: e.tensor_tensor(out=cf[0:P, 272:288], in0=rt[0:P, 0:16], in1=cs_ap[0:P, 48:64], op=ALU.mult),
                 reads=[rtB, csB], writes=[cfB])
            S.op("dve", lambda e: e.tensor_tensor(out=cf[0:P, 256:288], in0=cf[0:P, 256:288], in1=rt[0:P, 32:64], op=ALU.add),
                 reads=[rtB, cfB], writes=[cfB])
            S.op("dve", lambda e: e.tensor_copy(out=cb[0:P, :], in_=cf[0:P, :]), reads=[cfB], writes=[cbB])
            if o_lat is not None:
                S.dma("sp", lambda e: e.dma_start(out=o_lat, in_=cf[0:P, 0:256]), reads=[cfB], tag=cfB)
                S.dma("sp", lambda e: e.dma_start(out=o_kr, in_=cf[0:P, 256:288]), reads=[cfB], tag=cfB)
        for j, (xt, xtB, c0, cs_ap, o_lat, o_kr) in enumerate(blocks):
            cb, cbB = ckv_b[j], ckv_bB[j]
            pi = ps_alloc()
            pv = ps[pi][:].bitcast(BF16)
            for rc in range(2):
                S.op("pe", lambda e: e.transpose(out=pv[:, rc * 128:rc * 128 + P], in_=cb[0:P, rc * 128:(rc + 1) * 128], identity=identb[0:P, 0:P]),
                     reads=[cbB, b_const], writes=[psB[pi]], signal=False)
            S.op("pe", lambda e: e.transpose(out=pv[0:32, 256:256 + P], in_=cb[0:P, 256:288], identity=identb[0:P, 0:P]),
                 reads=[cbB, b_const], writes=[psB[pi]])
            S.op("act", lambda e: e.copy(out=ckvnT[:, :, c0:c0 + P], in_=pv[:, 0:256].rearrange("p (k n) -> p k n", k=2)[:, :, 0:P]),
                 reads=[psB[pi]], writes=[ckvnTB])
            S.op("act", lambda e: e.copy(out=kpeT[0:32, c0:c0 + P], in_=pv[0:32, 256:256 + P]), reads=[psB[pi]], writes=[kpeTB])
            ps_release(pi)

    class _N:
        n = 0
    front = _N
    front.n = 0

    def rstd_big(pi, rows, scale, li):
        S.op("act", lambda e: e.activation(out=lnv[li][0:rows, :], in_=ps[pi][0:rows, :], func=AF.Ln, bias=epsc[0:rows, :], scale=scale),
             reads=[psB[pi], b_const], writes=[lnvB[li]])
        S.op("act", lambda e: e.activation(out=lnv[li][0:rows, :], in_=lnv[li][0:rows, :], func=AF.Exp, scale=-0.5),
             reads=[lnvB[li]], writes=[lnvB[li]])

    def kv_build(slot, n):
        for pr in range(4):
            pk = ps_alloc()
            mm(pk, ps[pk][:, 0:n], [(w_uk[:, rc, pr * 128:(pr + 1) * 128], ckvnT[:, rc, 0:n]) for rc in range(2)], [ckvnTB, b_const])
            si = pr % 3
            S.op("act", lambda e: e.activation(out=sqb[si][:, 0:n], in_=ps[pk][:, 0:n], func=AF.Square), reads=[psB[pk]], writes=[sqbB[si]])
            pss = ps_alloc()
            mm(pss, ps[pss][:, 0:n], [(blk64, sqb[si][:, 0:n])], [sqbB[si], b_const])
            li = pr % 2
            rstd_big_n(pss, 128, 1.0 / 64, li, n)
            ps_release(pss)
            for hh in range(2):
                r0 = hh * 64
                S.op("dve", lambda e: e.scalar_tensor_tensor(out=KTst[0:64, 2 * pr + hh, 0:n], in0=ps[pk][r0:r0 + 64, 0:n], scalar=vcol(V_GKN2, 64, r0),
                                                             in1=lnv[li][r0:r0 + 64, 0:n], op0=ALU.mult, op1=ALU.mult),
                     reads=[psB[pk], lnvB[li], b_const], writes=[KTstB])
            ps_release(pk)
        S.op("dve", lambda e: e.tensor_copy(out=KTst[64:96, :, 0:n], in_=kpeT[0:32, 0:n].unsqueeze(1).to_broadcast([32, H, n])),
             reads=[kpeTB], writes=[KTstB])
        S.dma("sp", lambda e: e.dma_start(out=kt_s[slot], in_=KTst[:]), reads=[KTstB], writes=[kt_buf[slot]], tag=kt_buf[slot])
        S.op("pool", lambda e: e.memset(Vst[:, :, :, 64:128], 1.0), writes=[VstB])
        for ks in range(n // 128):
            pvv = ps_alloc()
            mm(pvv, ps[pvv][:, :], [(ckvnT[:, rc, ks * 128:(ks + 1) * 128], w_uv[:, rc, :]) for rc in range(2)], [ckvnTB, b_const])
            S.op("act", lambda e: e.copy(out=Vst[:, ks, :, 0:64], in_=ps[pvv][:, :].rearrange("p (h d) -> p h d", h=H)),
                 reads=[psB[pvv]], writes=[VstB])
            ps_release(pvv)
        S.dma("sp", lambda e: e.dma_start(out=v_s[slot], in_=Vst), reads=[VstB], writes=[v_buf[slot]], tag=v_buf[slot])

    def q_build(n, csq_ap, csqBuf, hsrc, hsrcB):
        wb, wbB = wget()
        wq = kview(8, QR)(wb)
        pq = [ps_alloc() for _ in range(3)]
        for c in range(3):
            mm(pq[c], ps[pq[c]][:, 0:n], [(wq[:, k, c * 128:(c + 1) * 128], hsrc[:, k, 0:n]) for k in range(8)], [wbB, hsrcB])
            S.op("act", lambda e: e.activation(out=sqb[c][:, 0:n], in_=ps[pq[c]][:, 0:n], func=AF.Square), reads=[psB[pq[c]]], writes=[sqbB[c]])
        pss = ps_alloc()
        mm(pss, ps[pss][:, 0:n], [(onesb, sqb[c][:, 0:n]) for c in range(3)], sqbB + [b_const])
        rstd_big_n(pss, 128, 1.0 / QR, 0, n)
        ps_release(pss)
        for c in range(3):
            S.op("dve", lambda e: e.scalar_tensor_tensor(out=qlnT[:, c, 0:n], in0=ps[pq[c]][:, 0:n], scalar=vcol(V_GQL + c), in1=lnv[0][:, 0:n],
                                                         op0=ALU.mult, op1=ALU.mult), reads=[psB[pq[c]], lnvB[0], b_const], writes=[qlnTB])
            ps_release(pq[c])
        for h in range(H):
            pa = ps_alloc()
            pb = ps_alloc()
            mm(pa, ps[pa][0:96, 0:n], [(w_uq[:, c, h, 0:96], qlnT[:, c, 0:n]) for c in range(3)], [qlnTB, b_const])
            mm(pb, ps[pb][0:96, 0:n], [(w_uq[:, c, h, 96:192], qlnT[:, c, 0:n]) for c in range(3)], [qlnTB, b_const])
            si = h % 3
            S.op("act", lambda e: e.activation(out=sqb[si][0:96, 0:n], in_=ps[pa][0:96, 0:n], func=AF.Square), reads=[psB[pa]], writes=[sqbB[si]])
            pss = ps_alloc()
            mm(pss, ps[pss][0:96, 0:n], [(blk96[0:96, 0:96], sqb[si][0:96, 0:n])], [sqbB[si], b_const])
            li = h % 2
            rstd_big_n(pss, 96, vcol(V_INV96, 96), li, n)
            ps_release(pss)
            S.op("dve", lambda e: e.scalar_tensor_tensor(out=QT[0:64, h, 0:n], in0=ps[pa][0:64, 0:n], scalar=vcol(V_GQ96, 64), in1=lnv[li][0:64, 0:n],
                                                         op0=ALU.mult, op1=ALU.mult), reads=[psB[pa], lnvB[li], b_const], writes=[QTB])
            S.op("dve", lambda e: e.scalar_tensor_tensor(out=rt1[64:96, 0:n], in0=ps[pa][64:96, 0:n], scalar=vcol(V_GQ96, 32, 64), in1=csq_ap[64:96, 0, 0:n],
                                                         op0=ALU.mult, op1=ALU.mult), reads=[psB[pa], csqBuf, b_const], writes=[rt1B])
            S.op("dve", lambda e: e.scalar_tensor_tensor(out=rt2[64:96, 0:n], in0=ps[pb][64:96, 0:n], scalar=vcol(V_GQ96P, 32, 64), in1=csq_ap[64:96, 1, 0:n],
                                                         op0=ALU.mult, op1=ALU.mult), reads=[psB[pb], csqBuf, b_const], writes=[rt2B])
            S.op("dve", lambda e: e.tensor_tensor(out=rt1[64:96, 0:n], in0=rt1[64:96, 0:n], in1=rt2[64:96, 0:n], op=ALU.add),
                 reads=[rt1B, rt2B], writes=[rt1B])
            S.op("dve", lambda e: e.tensor_tensor(out=QT[64:96, h, 0:n], in0=rt1[64:96, 0:n], in1=lnv[li][64:96, 0:n], op=ALU.mult),
                 reads=[rt1B, lnvB[li]], writes=[QTB])
            ps_release(pa)
            ps_release(pb)

    def rstd_big_n(pi, rows, scale, li, n):
        S.op("act", lambda e: e.activation(out=lnv[li][0:rows, 0:n], in_=ps[pi][0:rows, 0:n], func=AF.Ln, bias=epsc[0:rows, :], scale=scale),
             reads=[psB[pi], b_const], writes=[lnvB[li]])
        S.op("act", lambda e: e.activation(out=lnv[li][0:rows, 0:n], in_=lnv[li][0:rows, 0:n], func=AF.Exp, scale=-0.5),
             reads=[lnvB[li]], writes=[lnvB[li]])

    def attention(i):
        nslots = 2 * i + 2
        S.inherit(attn_group, [actTB])
        for hg in range(2):
            acc = [ps_alloc() for _ in range(4)]
            bufof = {}

            def load(s):
                bi = attention.n % 2
                attention.n += 1
                S.dma("sp", lambda e: e.dma_start(out=KTb[bi], in_=kt_s[s][:, hg * 4:(hg + 1) * 4, :]), reads=[kt_buf[s]], writes=[KTbB[bi]])
                S.dma("sp", lambda e: e.dma_start(out=Vb[bi][:], in_=v_s[s][:, :, hg * 4:(hg + 1) * 4, :]), reads=[v_buf[s]], writes=[VbB[bi]])
                bufof[s] = bi

            steps = [(s, ks) for s in range(nslots) for ks in range(4)]
            pts = {}

            def qk_exp(n, hh):
                s, ks = steps[n]
                bi = bufof[s]
                own = (s == 2 * i)
                oth = (s == 2 * i + 1)
                q0 = ks * 128 if own else 0
                h = hg * 4 + hh
                pi = ps_alloc()
                mm(pi, ps[pi][:, q0:T], [(KTb[bi][0:96, hh, ks * 128:(ks + 1) * 128], QT[0:96, h, q0:T])], [KTbB[bi], QTB])
                pj = attention.p % NPT
                attention.p += 1
                if oth:
                    S.op("act", lambda e: e.activation(out=PT[pj][:, q0:T], in_=ps[pi][:, q0:T], func=AF.Exp, scale=SM, bias=vcol(V_FLAGNEG)),
                         reads=[psB[pi], b_const], writes=[PTB[pj]])
                else:
                    S.op("act", lambda e: e.activation(out=PT[pj][:, q0:T], in_=ps[pi][:, q0:T], func=AF.Exp, scale=SM),
                         reads=[psB[pi]], writes=[PTB[pj]])
                ps_release(pi)
                if own:
                    S.op("dve", lambda e: e.tensor_tensor(out=PT[pj][:, q0:q0 + 128], in0=PT[pj][:, q0:q0 + 128], in1=tri, op=ALU.mult),
                         reads=[PTB[pj], b_const], writes=[PTB[pj]])
                pts[(n, hh)] = (pj, q0)

            load(0)
            for hh in range(4):
                qk_exp(0, hh)
            for n, (s, ks) in enumerate(steps):
                if ks == 0 and s + 1 < nslots:
                    load(s + 1)
                bi = bufof[s]
                last = (n == len(steps) - 1)
                for hh in range(4):
                    if n + 1 < len(steps):
                        qk_exp(n + 1, hh)
                    pj, q0 = pts.pop((n, hh))
                    S.op("pe", lambda e: e.matmul(ps[acc[hh]][:, q0:T], lhsT=Vb[bi][:, ks, hh, :], rhs=PT[pj][:, q0:T], start=(n == 0), stop=last),
                         reads=[VbB[bi], PTB[pj]], writes=[psB[acc[hh]]], signal=True)
            for hh in range(4):
                h = hg * 4 + hh
                a = acc[hh]
                rd, rdB = rden2[hh % 2], rden2B[hh % 2]
                S.op("act", lambda e: e.activation(out=rd[64:128, :], in_=ps[a][64:128, :], func=AF.Ln), reads=[psB[a]], writes=[rdB])
                S.op("act", lambda e: e.activation(out=rd[64:128, :], in_=rd[64:128, :], func=AF.Exp, scale=-1.0), reads=[rdB], writes=[rdB])
                r0 = (h % 2) * 64
                S.op("dve", lambda e: e.tensor_tensor(out=oT[r0:r0 + 64, h // 2, :], in0=ps[a][0:64, :], in1=rd[64:128, :], op=ALU.mult),
                     reads=[psB[a], rdB], writes=[oTB])
                ps_release(a)

    attention.n = 0
    attention.p = 0

    def pool_branch(n, first_tile):
        for g in range(4):
            w = 2 << g
            cur, curB = uT[:, g, :], uTB
            lo = 16 - (w - 1)
            width = 1
            bufs = [(pt1, pt1B), (pt2, pt2B)]
            bi = 0
            while width < w:
                lo2 = lo + width
                dst, dstB = bufs[bi]
                S.op("pool", lambda e: e.tensor_tensor(out=dst[:, lo2:16 + n], in0=cur[:, lo2:16 + n], in1=cur[:, lo2 - width:16 + n - width], op=ALU.add),
                     reads=[curB], writes=[dstB])
                cur, curB = dst[:, :], dstB
                lo = lo2
                width *= 2
                bi ^= 1
            if first_tile:
                S.op("pool", lambda e: e.tensor_tensor(out=cur[:, 16:32], in0=cur[:, 16:32], in1=pcorr[:, g, :], op=ALU.mult),
                     reads=[curB, b_const], writes=[curB])
            S.op("dve", lambda e: e.scalar_tensor_tensor(out=dT[:, g, 0:n], in0=cur[:, 16:16 + n], scalar=1.0 / w, in1=uT[:, g, 16:16 + n],
                                                         op0=ALU.mult, op1=ALU.subtract), reads=[curB, uTB], writes=[dTB])

    def merge_out_ffn(n, nblk, P, xts, xtBs, gt_ap_fn, a2_fn, hsrc, hsrcB, y_rows_fn):
        wao, wb_aoB = w_ao_sb, b_const
        for g in range(4):
            pi = ps_alloc()
            mm(pi, ps[pi][:, 0:n], [(w_u[:, k, g * 128:(g + 1) * 128], hsrc[:, k, 0:n]) for k in range(8)], [hsrcB, b_const])
            S.op("act", lambda e: e.copy(out=uT[:, g, 16:16 + n], in_=ps[pi][:, 0:n]), reads=[psB[pi]], writes=[uTB])
            ps_release(pi)
        merge_out_ffn.pool_hook()
        for e8 in range(8):
            if e8 % 4 == 0:
                wga, wgaB = wget()
            wv = kview(8, 512)(wga)
            c = (e8 % 4) * 128
            pa = ps_alloc()
            mm(pa, ps[pa][:, 0:n], [(wao[:, pr, e8 * 128:(e8 + 1) * 128], oT[:, pr, 0:n]) for pr in range(4)], [wb_aoB, oTB])
            pg = ps_alloc()
            mm(pg, ps[pg][:, 0:n], [(wv[:, k, c:c + 128], hsrc[:, k, 0:n]) for k in range(8)], [wgaB, hsrcB])
            si = e8 % 2
            S.op("act", lambda e: e.activation(out=sig[si][:, 0:n], in_=ps[pg][:, 0:n], func=AF.Sigmoid), reads=[psB[pg]], writes=[sigB[si]])
            ps_release(pg)
            S.op("dve", lambda e: e.tensor_tensor(out=mT[:, e8, 0:n], in0=ps[pa][:, 0:n], in1=sig[si][:, 0:n], op=ALU.mult),
                 reads=[psB[pa], sigB[si]], writes=[mTB])
            ps_release(pa)
        for e8 in range(8):
            if e8 % 4 == 0:
                wgb, wgbB = wget()
            wv = kview(8, 512)(wgb)
            c = (e8 % 4) * 128
            g, hf = e8 // 2, e8 % 2
            pb = ps_alloc()
            mm(pb, ps[pb][:, 0:n], [(w_pool[:, g, hf * 128:(hf + 1) * 128], dT[:, g, 0:n])], [dTB, b_const])
            pg = ps_alloc()
            mm(pg, ps[pg][:, 0:n], [(wv[:, k, c:c + 128], hsrc[:, k, 0:n]) for k in range(8)], [wgbB, hsrcB])
            si = e8 % 2
            S.op("act", lambda e: e.activation(out=sig[si][:, 0:n], in_=ps[pg][:, 0:n], func=AF.Sigmoid), reads=[psB[pg]], writes=[sigB[si]])
            ps_release(pg)
            S.op("dve", lambda e: e.scalar_tensor_tensor(out=gtmp[si][:, 0:n], in0=ps[pb][:, 0:n], scalar=vcol(V_SP + e8), in1=sig[si][:, 0:n],
                                                         op0=ALU.mult, op1=ALU.mult), reads=[psB[pb], sigB[si], b_const], writes=[gtmpB[si]])
            ps_release(pb)
            S.op("dve", lambda e: e.tensor_tensor(out=mT[:, e8, 0:n], in0=mT[:, e8, 0:n], in1=gtmp[si][:, 0:n], op=ALU.add),
                 reads=[gtmpB[si]], writes=[mTB])
        for hf in range(2):
            wo, woB = wget()
            wv = kview(8, 512)(wo)
            for tb in range(nblk):
                pi = ps_alloc()
                mm(pi, ps[pi][0:P, :], [(mT[:, k, tb * P:(tb + 1) * P], wv[:, k, :]) for k in range(8)], [woB, mTB])
                gi = (hf * nblk + tb) % 2
                cs_ = slice(hf * 512, (hf + 1) * 512)
                S.op("dve", lambda e: e.tensor_tensor(out=gtmp[gi][0:P, 0:512], in0=ps[pi][0:P, :], in1=gt_ap_fn(0, P)[:, cs_], op=ALU.mult),
                     reads=[psB[pi], b_const], writes=[gtmpB[gi]])
                ps_release(pi)
                S.op("dve", lambda e: e.tensor_tensor(out=xts[tb][0:P, cs_], in0=xts[tb][0:P, cs_], in1=gtmp[gi][0:P, 0:512], op=ALU.add),
                     reads=[gtmpB[gi], xtBs[tb]], writes=[xtBs[tb]])
        for tb in range(nblk):
            xt, xtB = xts[tb], xtBs[tb]
            S.op("act", lambda e: e.activation(out=junk[0:P, :], in_=xt[0:P, :], func=AF.Square, accum_out=st[0:P, 6:7]),
                 reads=[xtB], writes=[junkB, stB])
            rstd_small(st[0:P, 6:7], st[0:P, 7:8], 1.0 / D)
            xi = front.n % 2
            front.n += 1
            S.op("act", lambda e: e.activation(out=xn[xi][0:P, :], in_=xt[0:P, :], func=AF.Copy, scale=st[0:P, 7:8]),
                 reads=[xtB, stB], writes=[xnB[xi]])
            pi = ps_alloc()
            pv = ps[pi][:].bitcast(BF16)
            for k in range(8):
                S.op("pe", lambda e: e.transpose(out=pv[:, k * 128:k * 128 + P], in_=xn[xi][0:P, k * 128:(k + 1) * 128], identity=identb[0:P, 0:P]),
                     reads=[xnB[xi], b_const], writes=[psB[pi]], signal=(k == 7))
            pv3 = pv.rearrange("p (k n) -> p k n", k=8)[:, :, 0:P]
            a_ap, s_ap = a2_fn(P)
            hv = hsrc[:, :, tb * P:(tb + 1) * P]
            S.op("dve", lambda e: e.tensor_tensor(out=hv, in0=pv3, in1=a_ap, op=ALU.mult), reads=[psB[pi], b_const], writes=[hsrcB])
            S.op("dve", lambda e: e.tensor_tensor(out=hv, in0=hv, in1=s_ap, op=ALU.add), reads=[hsrcB, b_const], writes=[hsrcB])
            ps_release(pi)
        S.inherit([actTB], attn_group)
        for j in range(11):
            wg, wgB = wget()
            wgg = kview(8, 256)(wg)
            wup = kview(8, 256, 2048)(wg)
            for jj in range(2):
                ch = 2 * j + jj
                pg = ps_alloc()
                pu = ps_alloc()
                mm(pg, ps[pg][:, 0:n], [(wgg[:, k, jj * 128:(jj + 1) * 128], hsrc[:, k, 0:n]) for k in range(8)], [wgB, hsrcB])
                mm(pu, ps[pu][:, 0:n], [(wup[:, k, jj * 128:(jj + 1) * 128], hsrc[:, k, 0:n]) for k in range(8)], [wgB, hsrcB])
                si = ch % 2
                S.op("act", lambda e: e.activation(out=sig[si][:, 0:n], in_=ps[pg][:, 0:n], func=AF.Silu), reads=[psB[pg]], writes=[sigB[si]])
                ps_release(pg)
                S.op("dve", lambda e: e.tensor_tensor(out=actT[:, ch, 0:n], in0=ps[pu][:, 0:n], in1=sig[si][:, 0:n], op=ALU.mult),
                     reads=[psB[pu], sigB[si]], writes=[actTB])
                ps_release(pu)
        for qd in range(4):
            cs_ = slice(qd * 256, (qd + 1) * 256)
            pds = [ps_alloc() for _ in range(nblk)]
            for kh in range(2):
                wd, wdB = wget()
                wv = kview(11, 256)(wd)
                for tb in range(nblk):
                    for k in range(11):
                        S.op("pe", lambda e: e.matmul(ps[pds[tb]][0:P, 0:256], lhsT=actT[:, kh * 11 + k, tb * P:(tb + 1) * P], rhs=wv[:, k, :],
                                                      start=(kh == 0 and k == 0), stop=(kh == 1 and k == 10)),
                             reads=[wdB, actTB], writes=[psB[pds[tb]]], signal=(k == 10))
            for tb in range(nblk):
                pi = pds[tb]
                gi = (qd * nblk + tb) % 2
                S.op("dve", lambda e: e.tensor_tensor(out=gtmp[gi][0:P, 0:256], in0=ps[pi][0:P, 0:256], in1=gt_ap_fn(1, P)[:, cs_], op=ALU.mult),
                     reads=[psB[pi], b_const], writes=[gtmpB[gi]])
                ps_release(pi)
                S.op("dve", lambda e: e.tensor_tensor(out=xts[tb][0:P, cs_], in0=xts[tb][0:P, cs_], in1=gtmp[gi][0:P, 0:256], op=ALU.add),
                     reads=[gtmpB[gi], xtBs[tb]], writes=[xtBs[tb]])
        for tb in range(nblk):
            S.dma("sp", lambda e: e.dma_start(out=y_rows_fn(tb), in_=xts[tb][0:P, :]), reads=[xtBs[tb]], tag=xtBs[tb])


    def sample_phase():
        esS = contextlib.ExitStack()
        alloc_tile_bufs(NS, esS)
        G = 16
        NB = PAGE // G
        gtS = sb("gtS", [16, 2, D], F32, esS)
        x_s = sb("x_s", [16, D], F32, esS)
        x_sB = Buf("x_s")
        cs_s = sb("cs_s", [128, 1, 64], F32, esS)
        csq_s = sb("csq_s", [128, 2, NS], F32, esS)
        ptT = sb("ptT", [128, NS], I32, esS)
        ciota = sb("ciota", [128, 16], I32, esS)
        idx_l = sb("idx_l", [128, NS, 16], I32, esS)
        idx_k = sb("idx_k", [128, NS, 2], I32, esS)
        w_ukT = sb("w_ukT", [64, H, KVR], BF16, esS)
        qg = sb("qg", [64, H, NS], BF16, esS)
        qabs = sb("qabs", [128, 2, NS, H], BF16, esS)
        qpe = sb("qpe", [32, NS, H], BF16, esS)
        glat = [sb("glat%d" % i, [128, G, KVR], BF16, esS) for i in range(3)]
        glatB = [Buf("glat%d" % i) for i in range(3)]
        gkr = [sb("gkr%d" % i, [128, PAGE, DR], BF16, esS) for i in range(2)]
        gkrB = [Buf("gkr%d" % i) for i in range(2)]
        latT = [sb("latT%d" % i, [128, 2, 128], BF16, esS) for i in range(2)]
        latTB = [Buf("latT%d" % i) for i in range(2)]
        kT = [sb("kT%d" % i, [32, 128], BF16, esS) for i in range(2)]
        kTB = [Buf("kT%d" % i) for i in range(2)]
        ssb = [sb("ssb%d" % i, [128, G, H], F32, esS) for i in range(2)]
        ssbB = [Buf("ssb%d" % i) for i in range(2)]
        scb = [sb("scb%d" % i, [128, G, H], F32, esS) for i in range(2)]
        scbB = [Buf("scb%d" % i) for i in range(2)]
        pb = [sb("pb%d" % i, [128, G, H], BF16, esS) for i in range(2)]
        pbB = [Buf("pb%d" % i) for i in range(2)]
        nw = sb("nw", [16, 4, H], F32, esS)
        nwB = Buf("nw")
        pnew = sb("pnew", [16, H], BF16, esS)
        pnewB = Buf("pnew")
        dn = sb("dn", [8, 4, H], F32, esS)
        dnB = Buf("dn")
        oln = sb("oln", [8, KVR], BF16, esS)
        olnB = Buf("oln")
        OLT = sb("OLT", [128, 2, NS, H], BF16, esS)
        OLTB = Buf("OLT")
        prevT = sb("prevT", [128, 4, NS, 15], F32, esS)
        prevTB = Buf("prevT")
        stt = [sb("stt%d" % i, [120, DP], F32, esS) for i in range(2)]
        sttB = [Buf("stt%d" % i) for i in range(2)]
        wsum = sb("wsum", [128, 4, NS], F32, esS)
        wsumB = Buf("wsum")
        bS = Buf("smp_const")

        def sload(q, dst, src):
            S.dma(q, lambda e: e.dma_start(out=dst, in_=src), writes=[bS], tag=bS)

        S.dma("sp", lambda e: e.dma_start(out=x_s[:], in_=x_smp), writes=[x_sB])
        sload("sp", cs_s[:], cs_smp_d)
        sload("sp", csq_s[:], csq_smp_d)
        sload("sp", ptT[:], ptT_d)
        sload("sp", ciota[:], ciota_d)
        sload("pool", w_ukT[:], w_ukT_d)
        S.dma("sp", lambda e: e.dma_start(out=pool_smp[:, 0:14, :], in_=state_d[:, 1:15, :]), writes=[bS], tag=bS)
        for blk in range(2):
            S.dma("sp", lambda e: e.dma_start(out=stt[blk][:], in_=state_d[blk * 8:(blk + 1) * 8].rearrange("s r d -> (s r) d")), writes=[sttB[blk]])
        S.op("pool", lambda e: e.tensor_scalar(out=idx_l[:], in0=ptT[:].unsqueeze(2).to_broadcast([128, NS, 16]), scalar1=16, scalar2=None, op0=ALU.mult),
             reads=[bS], writes=[bS])
        S.op("pool", lambda e: e.tensor_tensor(out=idx_l[:], in0=idx_l[:], in1=ciota[:].unsqueeze(1).to_broadcast([128, NS, 16]), op=ALU.add),
             reads=[bS], writes=[bS])
        S.op("pool", lambda e: e.tensor_scalar(out=idx_k[:], in0=ptT[:].unsqueeze(2).to_broadcast([128, NS, 2]), scalar1=2, scalar2=None, op0=ALU.mult),
             reads=[bS], writes=[bS])
        S.op("pool", lambda e: e.tensor_tensor(out=idx_k[:], in0=idx_k[:], in1=ciota[:, 0:2].unsqueeze(1).to_broadcast([128, NS, 2]), op=ALU.add),
             reads=[bS], writes=[bS])
        b_gt = sb("b_gt_s", [16, 2, D], F32, esS)
        b_gtB = Buf("b_gt_s")
        S.dma("sp", lambda e: e.dma_start(out=b_gt[:], in_=b_gt_d), writes=[b_gtB])
        for which in range(2):
            for hf in range(2):
                wb, wbB = wget()
                wv = kview(8, 512)(wb)
                pj = ps_alloc()
                mm(pj, ps[pj][0:16, :], [(scT[:, k, 1:17], wv[:, k, :]) for k in range(8)], [wbB, b_const])
                cs_ = slice(hf * 512, (hf + 1) * 512)
                S.op("dve", lambda e: e.tensor_tensor(out=gtS[:, which, cs_], in0=ps[pj][0:16, :], in1=b_gt[:, which, cs_], op=ALU.add),
                     reads=[psB[pj], b_gtB], writes=[b_const])
                ps_release(pj)
        for blk in range(2):
            for g in range(4):
                pi = ps_alloc()
                S.op("pe", lambda e: e.transpose(out=ps[pi][:, 0:120], in_=stt[blk][:, g * 128:(g + 1) * 128], identity=identf[0:120, 0:120]),
                     reads=[sttB[blk], b_const], writes=[psB[pi]])
                S.op("act", lambda e: e.copy(out=prevT[:, g, blk * 8:(blk + 1) * 8, :], in_=ps[pi][:, 0:120].rearrange("p (s r) -> p s r", s=8)),
                     reads=[psB[pi]], writes=[prevTB])
                ps_release(pi)
        if smp_stop <= 1:
            S.barrier(); esS.close(); return
        front2([(x_s, x_sB, 0, cs_s[:, 0, :], lat_smp, kr_smp)], NS, True, hT, hTB, bS)
        q_build(NS, csq_s, bS, hT, hTB)
        S.op("dve", lambda e: e.tensor_scalar(out=qg[:], in0=QT[0:64, :, 0:NS], scalar1=vcol(V_GQK, 64), scalar2=None, op0=ALU.mult),
             reads=[QTB, b_const], writes=[bS])
        pi = ps_alloc()
        for rc in range(2):
            for h in range(H):
                c0 = (rc * H + h) * NS
                mm(pi, ps[pi][:, c0:c0 + NS], [(w_ukT[0:64, h, rc * 128:(rc + 1) * 128], qg[0:64, h, :])], [bS], last_signal=(rc == 1 and h == H - 1))
        S.op("act", lambda e: e.copy(out=qabs[:].rearrange("p c s h -> p c h s"), in_=ps[pi][:, 0:2 * H * NS].rearrange("p (c h s) -> p c h s", c=2, h=H)),
             reads=[psB[pi]], writes=[bS])
        ps_release(pi)
        S.op("pool", lambda e: e.tensor_copy(out=qpe[:].rearrange("p s h -> p h s"), in_=QT[64:96, :, 0:NS]), reads=[QTB], writes=[bS])
        pk = ps_alloc()
        mm(pk, ps[pk][0:NS, :], [(ckvnT[:, rc, 0:NS], w_uk[:, rc, :]) for rc in range(2)], [ckvnTB, b_const])
        S.op("act", lambda e: e.activation(out=sqb[0][0:NS, :], in_=ps[pk][0:NS, :], func=AF.Square), reads=[psB[pk]], writes=[sqbB[0]])
        ps_release(pk)
        S.op("dve", lambda e: e.tensor_reduce(out=nw[:, 0, :], in_=sqb[0][0:NS, :].rearrange("p (h d) -> p h d", h=H), axis=AX.X, op=ALU.add),
             reads=[sqbB[0]], writes=[nwB])
        S.op("act", lambda e: e.activation(out=nw[:, 1, :], in_=nw[:, 0, :], func=AF.Ln, bias=epsc[0:NS, :], scale=1.0 / 64), reads=[nwB, b_const], writes=[nwB])
        S.op("act", lambda e: e.activation(out=nw[:, 1, :], in_=nw[:, 1, :], func=AF.Exp, scale=-0.5), reads=[nwB], writes=[nwB])

        if smp_stop <= 2:
            S.barrier(); esS.close(); return
        cache_l2 = cache_lat
        cache_k2 = cache_kr
        st_ = {"gl": 0, "t": 0, "b": 0}

        def gather_lat(s, b):
            gi = st_["gl"] % 3
            st_["gl"] += 1
            for cc in range(2):
                c = 2 * b + cc
                S.dma("pool", lambda e: e.indirect_dma_start(out=glat[gi][:, cc * 8:(cc + 1) * 8, :].rearrange("p r d -> p (r d)"), out_offset=None,
                                                             in_=cache_l2, in_offset=bass.IndirectOffsetOnAxis(ap=idx_l[:, s, c:c + 1], axis=0)),
                      reads=[bS], writes=[glatB[gi]])
            return gi

        def gather_kr(s):
            gi = s % 2
            for c in range(2):
                S.dma("pool", lambda e: e.indirect_dma_start(out=gkr[gi][:, c * 64:(c + 1) * 64, :].rearrange("p r d -> p (r d)"), out_offset=None,
                                                             in_=cache_k2, in_offset=bass.IndirectOffsetOnAxis(ap=idx_k[:, s, c:c + 1], axis=0)),
                      reads=[bS], writes=[gkrB[gi]])
            return gi

        seq = [(s, b) for s in range(NS) for b in range(NB)]
        if smp_stop < 90:
            seq = seq[:max(1, smp_stop - 3)]
        ones1k = sb("ones1k", [128, 1024], BF16, esS)
        S.op("pool", lambda e: e.memset(ones1k[:], 1.0), writes=[bS])
        latT4 = [sb("latT4_%d" % i, [128, 4, 2, 128], BF16, esS) for i in range(3)]
        latT4B = [Buf("latT4_%d" % i) for i in range(3)]
        kT4 = [sb("kT4_%d" % i, [32, 4, 128], BF16, esS) for i in range(3)]
        kT4B = [Buf("kT4_%d" % i) for i in range(3)]
        OD = ps_alloc()
        OD_bf = ps[OD][:].bitcast(BF16)
        gl_of, kr_of, pdr_of = {}, {}, {}
        gl_of[0] = gather_lat(*seq[0])
        kr_of[0] = gather_kr(0)

        def batch_res(n_):
            if n_ >= len(seq) or n_ in gl_of:
                return
            s_, b_ = seq[n_]
            gl_of[n_] = gather_lat(s_, b_)
            if b_ == 0 and s_ not in kr_of:
                kr_of[s_] = gather_kr(s_)

        quads = [(n_, q) for n_ in range(len(seq)) for q in range(4)]

        def emit_T(qi):
            n_, q = quads[qi]
            batch_res(n_)
            if n_ not in pdr_of:
                pdr_of[n_] = ps_alloc()
            s_, b_ = seq[n_]
            gl, kr_i, pdr = gl_of[n_], kr_of[s_], pdr_of[n_]
            A = ps_alloc()
            Av = ps[A][:].bitcast(BF16)
            pdv = ps[pdr][:].bitcast(BF16)
            li = qi % 3
            for t in range(4):
                g = q * 4 + t
                for rc in range(2):
                    S.op("pe", lambda e: e.transpose(out=Av[:, (t * 2 + rc) * 128:(t * 2 + rc + 1) * 128], in_=glat[gl][:, g, rc * 128:(rc + 1) * 128], identity=identb),
                         reads=[glatB[gl], b_const], writes=[psB[A]], signal=(t == 3 and rc == 1))
            for t in range(4):
                row = b_ * G + q * 4 + t
                S.op("pe", lambda e: e.transpose(out=pdv[0:32, 512 + t * 128:512 + (t + 1) * 128], in_=gkr[kr_i][:, row, :], identity=identb),
                     reads=[gkrB[kr_i], b_const], writes=[psB[pdr]], signal=(t == 3))
            S.op("dve", lambda e: e.tensor_tensor(out=latT4[li][:].rearrange("p t c n -> p (t c n)"), in0=Av[:, :], in1=ones1k[:, :], op=ALU.mult),
                 reads=[psB[A], bS], writes=[latT4B[li]])
            ps_release(A)
            S.op("act", lambda e: e.copy(out=kT4[li][:].rearrange("p t n -> p (t n)"), in_=pdv[0:32, 512:1024]), reads=[psB[pdr]], writes=[kT4B[li]])

        pending = []

        def batch_pe(s, b, bi, gl):
            for g in range(G):
                S.op("pe", lambda e: e.matmul(ps[OD][0:8, 0:KVR], lhsT=pb[bi][:, g, :], rhs=glat[gl][:, g, :], start=(b == 0 and g == 0), stop=False),
                     reads=[pbB[bi], glatB[gl]], writes=[psB[OD]], signal=(g == G - 1))
            S.op("pe", lambda e: e.matmul(ps[OD][0:8, 256:256 + G * H], lhsT=onesb[:, 0:8], rhs=pb[bi][:].rearrange("p g h -> p (g h)"), start=(b == 0), stop=(b == NB - 1)),
                 reads=[pbB[bi], b_const], writes=[psB[OD]])
            if b == NB - 1:
                pdn = ps_alloc()
                mm(pdn, ps[pdn][0:NS, 0:8], [(ckvnT[:, rc, 0:NS], qabs[:, rc, s, :]) for rc in range(2)], [ckvnTB, bS], last_signal=False)
                mm(pdn, ps[pdn][0:NS, 8:16], [(kpeT[0:32, 0:NS], qpe[:, s, :])], [kpeTB, bS])
                S.op("dve", lambda e: e.tensor_tensor(out=nw[:, 2, :], in0=ps[pdn][0:NS, 0:8], in1=nw[:, 1, :], op=ALU.mult), reads=[psB[pdn], nwB], writes=[nwB])
                S.op("dve", lambda e: e.tensor_tensor(out=nw[:, 2, :], in0=ps[pdn][0:NS, 8:16], in1=nw[:, 2, :], op=ALU.add), reads=[psB[pdn], nwB], writes=[nwB])
                S.op("act", lambda e: e.activation(out=nw[:, 3, :], in_=nw[:, 2, :], func=AF.Exp, scale=SM), reads=[nwB], writes=[nwB])
                S.op("dve", lambda e: e.tensor_scalar(out=pnew[:], in0=nw[:, 3, :], scalar1=identf[0:NS, s:s + 1], scalar2=None, op0=ALU.mult),
                     reads=[nwB, b_const], writes=[pnewB])
                S.op("pe", lambda e: e.matmul(ps[OD][0:8, 0:KVR], lhsT=pnew[:, :], rhs=ckv_b[smp_ci["i"]][0:NS, 0:KVR], start=False, stop=True),
                     reads=[pnewB, ckv_bB[smp_ci["i"]]], writes=[psB[OD]])
                S.op("dve", lambda e: e.tensor_reduce(out=dn[:, 0, :], in_=ps[OD][0:8, 256:256 + G * H].rearrange("p (g h) -> p h g", g=G), axis=AX.X, op=ALU.add),
                     reads=[psB[OD]], writes=[dnB])
                mm(pdn, ps[pdn][0:8, 16:16 + H], [(onesb[0:NS, 0:8], pnew[:, :])], [pnewB, b_const])
                S.op("dve", lambda e: e.tensor_tensor(out=dn[:, 0, :], in0=ps[pdn][0:8, 16:16 + H], in1=dn[:, 0, :], op=ALU.add), reads=[psB[pdn], dnB], writes=[dnB])
                ps_release(pdn)
                S.op("dve", lambda e: e.tensor_tensor(out=dn[:, 1, :], in0=dn[:, 0, :], in1=identf[0:8, 0:8], op=ALU.mult), reads=[dnB, b_const], writes=[dnB])
                S.op("dve", lambda e: e.tensor_reduce(out=dn[:, 2, 0:1], in_=dn[:, 1, :], axis=AX.X, op=ALU.add), reads=[dnB], writes=[dnB])
                S.op("dve", lambda e: e.reciprocal(out=dn[:, 2, 1:2], in_=dn[:, 2, 0:1]), reads=[dnB], writes=[dnB])
                S.op("dve", lambda e: e.tensor_scalar(out=oln[:], in0=ps[OD][0:8, 0:KVR], scalar1=dn[:, 2, 1:2], scalar2=None, op0=ALU.mult),
                     reads=[psB[OD], dnB], writes=[olnB])
                for rc in range(2):
                    c0 = 768 + (rc * NS + s) * H
                    S.op("pe", lambda e: e.transpose(out=OD_bf[:, c0:c0 + H], in_=oln[0:8, rc * 128:(rc + 1) * 128], identity=identb[0:8, 0:8]),
                         reads=[olnB, b_const], writes=[psB[OD]], signal=(rc == 1))

        emit_T(0)
        if len(quads) > 1:
            emit_T(1)
        for qi, (n_, q) in enumerate(quads):
            s, b = seq[n_]
            if q == 1:
                batch_res(n_ + 1)
                batch_res(n_ + 2)
            if qi + 2 < len(quads):
                emit_T(qi + 2)
            gl, pdr = gl_of[n_], pdr_of[n_]
            bi = n_ % 2
            li = qi % 3
            for t in range(4):
                g = q * 4 + t
                pk = ps_alloc()
                mm(pk, ps[pk][:, :], [(latT4[li][:, t, rc, :], w_uk[:, rc, :]) for rc in range(2)], [latT4B[li], b_const])
                si = st_["t"] % 3
                st_["t"] += 1
                S.op("act", lambda e: e.activation(out=sqb[si][:, :], in_=ps[pk][:, :], func=AF.Square), reads=[psB[pk]], writes=[sqbB[si]])
                ps_release(pk)
                red = "dve"
                S.op(red, lambda e: e.tensor_reduce(out=ssb[bi][:, g, :], in_=sqb[si][:, :].rearrange("p (h d) -> p h d", h=H), axis=AX.X, op=ALU.add),
                     reads=[sqbB[si]], writes=[ssbB[bi]])
                mm(pdr, ps[pdr][:, g * 16:g * 16 + 8], [(latT4[li][:, t, rc, :], qabs[:, rc, s, :]) for rc in range(2)], [latT4B[li], bS], last_signal=False)
                mm(pdr, ps[pdr][:, g * 16 + 8:g * 16 + 16], [(kT4[li][:, t, :], qpe[:, s, :])], [kT4B[li], bS], last_signal=True)
            if q == 0 and pending:
                batch_pe(*pending.pop(0))
            if q < 3:
                continue
            S.op("act", lambda e: e.activation(out=ssb[bi][:], in_=ssb[bi][:], func=AF.Ln, bias=epsc[:], scale=1.0 / 64), reads=[ssbB[bi], b_const], writes=[ssbB[bi]])
            S.op("act", lambda e: e.activation(out=ssb[bi][:], in_=ssb[bi][:], func=AF.Exp, scale=-0.5), reads=[ssbB[bi]], writes=[ssbB[bi]])
            drv = ps[pdr][:, 0:G * 16].rearrange("p (g x) -> p g x", g=G)
            S.op("dve", lambda e: e.tensor_tensor(out=scb[bi][:], in0=drv[:, :, 0:8], in1=ssb[bi][:], op=ALU.mult), reads=[psB[pdr], ssbB[bi]], writes=[scbB[bi]])
            S.op("dve", lambda e: e.tensor_tensor(out=scb[bi][:], in0=drv[:, :, 8:16], in1=scb[bi][:], op=ALU.add), reads=[psB[pdr], scbB[bi]], writes=[scbB[bi]])
            ps_release(pdr)
            S.op("act", lambda e: e.activation(out=pb[bi][:], in_=scb[bi][:], func=AF.Exp, scale=SM), reads=[scbB[bi]], writes=[pbB[bi]])
            pending.append((s, b, bi, gl))
        while pending:
            batch_pe(*pending.pop(0))
        pT_all = OD
        pT_v = OD_bf[:, 768:1024]
        if smp_stop < 95:
            S.barrier(); esS.close(); return
        S.op("act", lambda e: e.copy(out=OLT[:].rearrange("p c s h -> p (c s h)"), in_=pT_v[:, 0:2 * NS * H]), reads=[psB[pT_all]], writes=[OLTB])
        ps_release(pT_all)
        for h in range(H):
            pi = ps_alloc()
            mm(pi, ps[pi][0:64, 0:NS], [(w_uv[:, rc, h * 64:(h + 1) * 64], OLT[:, rc, :, h]) for rc in range(2)], [OLTB, b_const])
            r0 = (h % 2) * 64
            S.op("act", lambda e: e.copy(out=oT[r0:r0 + 64, h // 2, 0:NS], in_=ps[pi][0:64, 0:NS]), reads=[psB[pi]], writes=[oTB])
            ps_release(pi)

        def pool_hook_s():
            for g in range(4):
                w = 2 << g
                S.op("dve", lambda e: e.tensor_reduce(out=wsum[:, g, :], in_=prevT[:, g, :, 15 - (w - 1):15], axis=AX.X, op=ALU.add),
                     reads=[prevTB], writes=[wsumB])
                S.op("dve", lambda e: e.tensor_tensor(out=wsum[:, g, :], in0=wsum[:, g, :], in1=uT[:, g, 16:16 + NS], op=ALU.add),
                     reads=[wsumB, uTB], writes=[wsumB])
                S.op("dve", lambda e: e.scalar_tensor_tensor(out=dT[:, g, 0:NS], in0=wsum[:, g, :], scalar=1.0 / w, in1=uT[:, g, 16:16 + NS],
                                                             op0=ALU.mult, op1=ALU.subtract), reads=[wsumB, uTB], writes=[dTB])
            pi = ps_alloc()
            for g in range(4):
                S.op("pe", lambda e: e.transpose(out=ps[pi][0:NS, g * 128:(g + 1) * 128], in_=uT[:, g, 16:16 + NS], identity=identf[:]),
                     reads=[uTB, b_const], writes=[psB[pi]], signal=(g == 3))
            S.op("act", lambda e: e.copy(out=gtmp[0][0:NS, 0:512], in_=ps[pi][0:NS, :]), reads=[psB[pi]], writes=[gtmpB[0]])
            ps_release(pi)
            S.dma("sp", lambda e: e.dma_start(out=pool_smp[:, 14, :], in_=gtmp[0][0:NS, 0:512]), reads=[gtmpB[0]], tag=gtmpB[0])

        merge_out_ffn.pool_hook = pool_hook_s
        merge_out_ffn(NS, 1, NS, [x_s], [x_sB],
                      lambda which, P: gtS[0:P, which, :],
                      lambda P: (modF[:, 24:32, 1:1 + P], modF[:, 16:24, 1:1 + P]),
                      hT, hTB,
                      lambda tb: y_smp)
        S.barrier()
        esS.close()

    smp_ci = {"i": 0}

    x_tok = [sb("x_tok%d" % i, [128, D], F32, esP) for i in range(NXT)]
    x_tokB = [Buf("x_tok%d" % i) for i in range(NXT)]
    xo_tok = [sb("xo_tok%d" % i, [128, D], F32, esP) for i in range(2)]
    xo_tokB = [Buf("xo_tok%d" % i) for i in range(2)]
    cs_tok = [sb("cs_tok%d" % i, [128, 4, 64], F32, esP) for i in range(2)]
    cs_tokB = [Buf("cs_tok%d" % i) for i in range(2)]
    csq = sb("csq", [128, 2, T], F32, esP)
    csqB = Buf("csq")
    utail = [sb("utail%d" % i, [128, 4, 16], F32, esP) for i in range(2)]
    utailB = [Buf("utail%d" % i) for i in range(2)]
    pcorr = sb("pcorr", [128, 4, 16], F32, esP)
    Vb = [sb("Vb%d" % i, [128, 4, 4, 128], BF16, esP) for i in range(2)]
    VbB = [Buf("Vb%d" % i) for i in range(2)]
    cload("sp", pcorr[:], pcorr_d)
    S.op("dve", lambda e: e.memset(utail[0][:], 0.0), writes=[utailB[0]])
    S.op("dve", lambda e: e.memset(utail[1][:], 0.0), writes=[utailB[1]])
    for i in range(2):
        S.op("pool", lambda e: e.memset(Vb[i][:], 1.0), writes=[VbB[i]])
    xq = {"n": 0}

    def x_block():
        i = xq["n"] % NXT
        xq["n"] += 1
        return x_tok[i], x_tokB[i]

    for i in range(NT):
        ci = i % 2
        S.dma("sp", lambda e: e.dma_start(out=cs_tok[0][:], in_=cs_own_d[i]), writes=[cs_tokB[0]])
        S.dma("sp", lambda e: e.dma_start(out=cs_tok[1][:], in_=cs_oth_d[i]), writes=[cs_tokB[1]])
        S.dma("sp", lambda e: e.dma_start(out=csq[:], in_=csq_own_d[i]), writes=[csqB])
        for half in range(2):
            blks = []
            for jj in range(2):
                sbk = 2 * half + jj
                r0 = i * T + sbk * 128
                S.dma("sp", lambda e: e.dma_start(out=xo_tok[jj][:], in_=x_oth[r0:r0 + 128, :]), writes=[xo_tokB[jj]])
                blks.append((xo_tok[jj], xo_tokB[jj], sbk * 128, cs_tok[1][:, sbk, :], None, None))
            front2(blks, 128, False, hTo, actTB, cs_tokB[1])
        kv_build(2 * i + 1, T)
        ut = utail[i % 2]
        utB_ = utailB[i % 2]
        for g in range(4):
            pi = ps_alloc()
            mm(pi, ps[pi][:, 0:16], [(w_u[:, k, g * 128:(g + 1) * 128], hTo[:, k, T - 16:T]) for k in range(8)], [actTB, b_const])
            S.op("act", lambda e: e.copy(out=ut[:, g, :], in_=ps[pi][:, 0:16]), reads=[psB[pi]], writes=[utB_])
            ps_release(pi)
        xts, xtBs = [], []
        for half in range(2):
            blks = []
            for jj in range(2):
                sbk = 2 * half + jj
                xt, xtB = x_block()
                xts.append(xt)
                xtBs.append(xtB)
                r0 = i * T + sbk * 128
                S.dma("sp", lambda e: e.dma_start(out=xt[:], in_=x_own[r0:r0 + 128, :]), writes=[xtB])
                blks.append((xt, xtB, sbk * 128, cs_tok[0][:, sbk, :], lat_own[r0:r0 + 128, :], kr_own[r0:r0 + 128, :]))
            front2(blks, 128, False, hT, hTB, cs_tokB[0])
        kv_build(2 * i, T)
        q_build(T, csq, csqB, hT, hTB)
        attention(i)

        def pool_hook(i=i):
            prev = utail[(i + 1) % 2]
            prevB = utailB[(i + 1) % 2]
            cur = utail[i % 2]
            curB = utailB[i % 2]
            S.op("pool", lambda e: e.tensor_scalar(out=pt1[:, 0:64].rearrange("p (g n) -> p g n", g=4), in0=prev[:], scalar1=vcol(V_NFLAG), scalar2=None, op0=ALU.mult),
                 reads=[prevB, b_const], writes=[pt1B])
            S.op("dve", lambda e: e.scalar_tensor_tensor(out=uT[:, :, 1:16], in0=cur[:, :, 1:16], scalar=vcol(V_FLAG),
                                                          in1=pt1[:, 0:64].rearrange("p (g n) -> p g n", g=4)[:, :, 1:16], op0=ALU.mult, op1=ALU.add),
                 reads=[curB, pt1B, b_const], writes=[uTB])
            pool_branch(T, first_tile=(i == 0))
            if i == NT - 1:
                pi = ps_alloc()
                for g in range(4):
                    S.op("pe", lambda e: e.transpose(out=ps[pi][0:16, g * 128:(g + 1) * 128], in_=uT[:, g, T:T + 16], identity=identf[:]),
                         reads=[uTB, b_const], writes=[psB[pi]], signal=(g == 3))
                S.op("act", lambda e: e.copy(out=gtmp[0][0:16, 0:512], in_=ps[pi][0:16, :]), reads=[psB[pi]], writes=[gtmpB[0]])
                ps_release(pi)
                S.dma("sp", lambda e: e.dma_start(out=pool_own, in_=gtmp[0][0:16, 0:512]), reads=[gtmpB[0]], tag=gtmpB[0])

        merge_out_ffn.pool_hook = pool_hook
        merge_out_ffn(T, 4, 128, xts, xtBs,
                      lambda which, P: gt_bc[0:P, which, :],
                      lambda P: (modF[:, 24:32, 0:1].to_broadcast([128, 8, P]), modF[:, 16:24, 0:1].to_broadcast([128, 8, P])),
                      hT, hTB,
                      lambda tb, i=i: y_own[i * T + tb * 128:i * T + (tb + 1) * 128, :])

    S.barrier()
    esP.close()
    if do_sample:
        sample_phase()
        S.barrier()
    es.close()
    S.close()
    return nc


def _rope_tables(pos):
    inv = 1.0 / (10000.0 ** (np.arange(0, DR, 2, dtype=np.float32) / DR))
    ang = pos.astype(np.float32)[:, None] * inv[None, :].astype(np.float32)
    ang = ang.astype(np.float32)
    return np.cos(ang).astype(np.float32), np.sin(ang).astype(np.float32)


def _consts():
    cb = np.zeros((128, 5, 128), np.float32)
    cb[:, 0, :] = np.eye(128)
    cb[0:64, 1, 0:64] = 1.0
    cb[64:96, 1, 64:96] = 1.0
    cb[0:64, 2, 0:64] = 1.0
    cb[64:128, 2, 64:128] = 1.0
    cb[:, 3, :] = (np.arange(128)[:, None] <= np.arange(128)[None, :]).astype(np.float32)
    cb[:, 4, :] = 1.0
    return cb


def _fm(v, ncol):
    return np.ascontiguousarray(v.reshape(ncol, 128).T)


PERM = np.concatenate([np.arange(16, 32), np.arange(0, 16)])


def make_in_maps(inp, NT=8, NS=16, do_sample=True, cores=range(8)):
    f32 = np.float32
    w_in = inp["w_in"][0]
    w_uq = inp["w_uq"][0]
    b_ada = inp["b_ada"][0]
    shared = {}
    shared["w_ada"] = np.ascontiguousarray(inp["w_ada"][0])
    shared["b_gt"] = np.ascontiguousarray(np.broadcast_to(np.stack([b_ada[2048:3072], b_ada[5120:6144]])[None], (16, 2, D)))
    shared["rowv"] = np.ascontiguousarray(np.broadcast_to(np.concatenate([inp["g_kv_lat"][0], inp["g_k_rope"][0]])[None], (128, 288)))
    shared["cbf"] = _consts()
    shared["identf"] = np.eye(128, dtype=f32)
    shared["w_in_q"] = np.ascontiguousarray(w_in[:, 0:384])
    shared["w_in_kv"] = np.ascontiguousarray(w_in[:, 384:672])
    shared["w_in_u"] = np.ascontiguousarray(w_in[:, 672:1184])
    shared["w_in_g"] = np.ascontiguousarray(w_in[:, 1184:3232])
    wq2 = np.zeros((QR, H, 192), f32)
    wq2[:, :, 0:96] = w_uq
    wq2[:, :, 96 + 64:192] = w_uq[:, :, 64 + PERM]
    shared["w_uq2"] = wq2
    shared["w_uk"] = np.ascontiguousarray(inp["w_uk"][0].reshape(KVR, 512))
    shared["w_uv"] = np.ascontiguousarray(inp["w_uv"][0].reshape(KVR, 512))
    shared["w_ao"] = np.ascontiguousarray(inp["w_attn_o"][0].reshape(512, D))
    shared["w_pool"] = np.ascontiguousarray(inp["w_pool"][0])
    shared["w_out"] = np.ascontiguousarray(inp["w_out"][0])
    shared["w_gu"] = np.ascontiguousarray(inp["w_gu"][0])
    shared["w_down"] = np.ascontiguousarray(inp["w_down"][0])
    if do_sample:
        shared["w_ukT"] = np.ascontiguousarray(inp["w_uk"][0].transpose(2, 1, 0))
        shared["cache_lat"] = inp["cache_kv_latent"][0].reshape(-1, 2048)
        shared["cache_kr"] = inp["cache_k_rope"][0].reshape(-1, 2048)
    gq = inp["g_q_rope"][0]
    in_maps = []
    for c in cores:
        b, half = c // 2, c % 2
        m = dict(shared)
        xs = inp["x_prompt"][b].reshape(16, T, D)
        own_t = [2 * i + half for i in range(NT)]
        oth_t = [2 * i + 1 - half for i in range(NT)]
        m["x_own"] = np.ascontiguousarray(xs[own_t].reshape(NT * T, D))
        m["x_oth"] = np.ascontiguousarray(xs[oth_t].reshape(NT * T, D))
        cT = np.zeros((128, 8, 17), f32)
        cT[:, :, 0] = _fm(inp["c_prompt"][b], 8)
        for s in range(NS):
            cT[:, :, 1 + s] = _fm(inp["c_sample"][16 * c + s], 8)
        m["cT"] = cT
        vecs = np.zeros((128, 80), f32)
        for gi, base in enumerate([0, 1024, 3072, 4096]):
            vecs[:, gi * 8:(gi + 1) * 8] = _fm(b_ada[base:base + 1024], 8)
        vecs[:, 32:40] = _fm(inp["g_norm1"][0], 8)
        vecs[:, 40:48] = _fm(inp["g_norm2"][0], 8)
        vecs[:, 48:56] = _fm(inp["s_pool"][0], 8)
        vecs[:, 56:59] = _fm(inp["g_q_lat"][0], 3)
        vecs[0:64, 59] = inp["g_q_nope"][0]
        vecs[64:96, 59] = gq
        vecs[64:96, 60] = gq[PERM]
        vecs[0:64, 61] = 1.0 / 64
        vecs[64:96, 61] = 1.0 / 32
        vecs[0:64, 62] = inp["g_k_nope"][0]
        vecs[64:128, 62] = inp["g_k_nope"][0]
        vecs[:, 63] = 0.0 if half == 1 else NEG
        vecs[:, 64] = float(half)
        vecs[:, 65] = 1.0 - float(half)
        vecs[0:64, 66] = inp["g_k_nope"][0]
        m["vecs"] = vecs
        cs_o = np.zeros((NT, 128, 4, 64), f32)
        cs_x = np.zeros((NT, 128, 4, 64), f32)
        csq_o = np.zeros((NT, 128, 2, T), f32)
        for i in range(NT):
            for tl, dst in ((own_t[i], cs_o), (oth_t[i], cs_x)):
                cos, sin = _rope_tables(np.arange(tl * T, (tl + 1) * T))
                tab = np.concatenate([cos, cos, -sin, sin], axis=1).reshape(4, 128, 64).transpose(1, 0, 2)
                dst[i] = tab
            cos, sin = _rope_tables(np.arange(own_t[i] * T, (own_t[i] + 1) * T))
            csq_o[i, 64:96, 0, :] = np.concatenate([cos, cos], axis=1).T
            csq_o[i, 64:96, 1, :] = np.concatenate([-sin, sin], axis=1).T
        m["cs_own"], m["cs_oth"], m["csq_own"] = cs_o, cs_x, csq_o
        pc = np.ones((128, 4, 16), f32)
        if half == 0:
            for g, w in enumerate((2, 4, 8, 16)):
                pc[:, g, :] = w / np.minimum(np.arange(16) + 1, w).astype(f32)
        m["pcorr"] = pc
        if do_sample:
            sl = slice(16 * c, 16 * c + NS)
            m["x_smp"] = np.ascontiguousarray(inp["x_sample"][sl, 0, :])
            m["ptT"] = np.ascontiguousarray(inp["page_table"][sl].T.astype(np.int32))
            m["state_pool"] = np.ascontiguousarray(inp["state_pool"][0, sl])
            cos, sin = _rope_tables(np.array([PAST]))
            m["cs_smp"] = np.ascontiguousarray(np.broadcast_to(np.concatenate([cos, cos, -sin, sin], axis=1)[None], (128, 1, 64)))
            cq = np.zeros((128, 2, NS), f32)
            cq[64:96, 0, :] = np.concatenate([cos, cos], axis=1).T
            cq[64:96, 1, :] = np.concatenate([-sin, sin], axis=1).T
            m["csq_smp"] = cq
            m["ciota"] = np.ascontiguousarray(np.broadcast_to(np.arange(16, dtype=np.int32)[None], (128, 16)))
        in_maps.append(m)
    return in_maps


_NC_CACHE = {}


def kernel(**inputs):
    inp = {k: np.asarray(v) for k, v in inputs.items()}
    nphys = inp["cache_kv_latent"].shape[1]
    key = ("full", nphys)
    if key not in _NC_CACHE:
        _NC_CACHE[key] = build(NT=8, NS=16, NPHYS=nphys, do_sample=True)
    nc = _NC_CACHE[key]
    in_maps = make_in_maps(inp, do_sample=True)
    res = run_bass_kernel_spmd(nc, in_maps, core_ids=list(range(8)))
    return assemble(res.results, inp)


def assemble(results, inp, NT=8):
    f32 = np.float32
    yp = np.zeros((4, 16, T, D), f32)
    lat = np.zeros((1, 4, 16, T, KVR), f32)
    kr = np.zeros((1, 4, 16, T, DR), f32)
    pool_p = np.zeros((1, 4, 15, DP), f32)
    ys = np.zeros((128, 1, D), f32)
    lat_s = np.zeros((1, 128, 1, KVR), f32)
    kr_s = np.zeros((1, 128, 1, DR), f32)
    pool_s = np.zeros((1, 128, 15, DP), f32)
    for c, r in enumerate(results):
        b, half = c // 2, c % 2
        for i in range(NT):
            t = 2 * i + half
            yp[b, t] = r["y_own"][i * T:(i + 1) * T]
            lat[0, b, t] = r["lat_own"][i * T:(i + 1) * T]
            kr[0, b, t] = r["kr_own"][i * T:(i + 1) * T]
        if half == 1:
            pool_p[0, b] = r["pool_own"][1:16]
        sl = slice(16 * c, 16 * c + 16)
        ys[sl, 0] = r["y_smp"]
        lat_s[0, sl, 0] = r["lat_smp"]
        kr_s[0, sl, 0] = r["kr_smp"]
        pool_s[0, sl] = r["pool_smp"]
    return (yp.reshape(4, 16 * T, D), ys, lat.reshape(1, 4, 16 * T, KVR), kr.reshape(1, 4, 16 * T, DR), pool_p,
            lat_s, kr_s, pool_s)
```
